# Optimizing a Trainium2 kernel written in Bass

```python
import jax, jax.numpy as jnp
from jax import lax
import numpy as np

D_MODEL = 1024
BATCH = 8
SEQ = 8192
DEPTH = 2

PLE_DIM = 256
D_FF = 2816
D_POOL = D_MODEL
N_POOL_GROUPS = 4
POOL_GROUP = D_POOL // N_POOL_GROUPS
POOL_WINDOWS = (2, 4, 8, 16)
D_CONV = D_MODEL
CONV_K = 31
N_IN = D_POOL + 2 * D_CONV + 2 * D_MODEL
RMS_EPS = 1e-6
LN_EPS = 1e-5

kernel_name = "hybrid_pool_conformer_macaron_ple"


def rmsnorm(x, g):
    x32 = x.astype(jnp.float32)
    y = x32 * lax.rsqrt(jnp.mean(x32 * x32, axis=-1, keepdims=True) + RMS_EPS)
    return (y * g.astype(jnp.float32)).astype(x.dtype)


def layernorm(x, g, b):
    x32 = x.astype(jnp.float32)
    mu = jnp.mean(x32, axis=-1, keepdims=True)
    var = jnp.mean(jnp.square(x32 - mu), axis=-1, keepdims=True)
    y = (x32 - mu) * lax.rsqrt(var + LN_EPS)
    return (y * g.astype(jnp.float32) + b.astype(jnp.float32)).astype(x.dtype)


def swiglu(x, w_gate, w_up, w_down):
    return (jax.nn.silu(x @ w_gate) * (x @ w_up)) @ w_down


def causal_multiscale_pool(z):
    S = z.shape[1]
    cs = jnp.cumsum(z.astype(jnp.float32), axis=1)
    pos = jnp.arange(S, dtype=jnp.int32)
    outs = []
    for g, w in enumerate(POOL_WINDOWS):
        sl = slice(g * POOL_GROUP, (g + 1) * POOL_GROUP)
        cs_g = cs[..., sl]
        lower = jnp.pad(cs_g, ((0, 0), (w, 0), (0, 0)))[:, :S]
        count = jnp.minimum(pos + 1, w).astype(jnp.float32)[None, :, None]
        mean = (cs_g - lower) / count
        outs.append(mean - z[..., sl].astype(jnp.float32))
    return jnp.concatenate(outs, axis=-1).astype(z.dtype)


def causal_depthwise_conv(x, w, b):
    K, C = w.shape
    y = lax.conv_general_dilated(
        x, w[:, None, :].astype(x.dtype), window_strides=(1,), padding=[(K - 1, 0)],
        dimension_numbers=("NWC", "WIO", "NWC"), feature_group_count=C)
    return y + b


def setup_inputs(seed: int = 0) -> dict:
    key = jax.random.key(seed)
    ks = iter(jax.random.split(key, 32))

    def nrm(shape, fan_in):
        return jax.random.normal(next(ks), shape, jnp.float32) * (fan_in ** -0.5)

    def gain(shape):
        return 1.0 + 0.02 * jax.random.normal(next(ks), shape, jnp.float32)

    def bias(shape):
        return 0.02 * jax.random.normal(next(ks), shape, jnp.float32)

    L = DEPTH
    return {
        "x": jax.random.normal(next(ks), (BATCH, SEQ, D_MODEL), jnp.float32),
        "p": jax.random.normal(next(ks), (DEPTH, BATCH, SEQ, PLE_DIM), jnp.float32),
        "ffn1_norm": gain((L, D_MODEL)),
        "ffn1_w_gate": nrm((L, D_MODEL, D_FF), D_MODEL),
        "ffn1_w_up": nrm((L, D_MODEL, D_FF), D_MODEL),
        "ffn1_w_down": nrm((L, D_FF, D_MODEL), D_FF),
        "mix_norm": gain((L, D_MODEL)),
        "w_in": nrm((L, D_MODEL, N_IN), D_MODEL),
        "pool_w": nrm((L, N_POOL_GROUPS, POOL_GROUP, POOL_GROUP), POOL_GROUP),
        "pool_scale": gain((L, D_POOL)),
        "conv_dw_w": nrm((L, CONV_K, D_CONV), CONV_K),
        "conv_dw_b": bias((L, D_CONV)),
        "conv_ln_g": gain((L, D_CONV)),
        "conv_ln_b": bias((L, D_CONV)),
        "conv_w_out": nrm((L, D_CONV, D_MODEL), D_CONV),
        "w_out": nrm((L, D_MODEL, D_MODEL), D_MODEL),
        "ffn2_norm": gain((L, D_MODEL)),
        "ffn2_w_gate": nrm((L, D_MODEL, D_FF), D_MODEL),
        "ffn2_w_up": nrm((L, D_MODEL, D_FF), D_MODEL),
        "ffn2_w_down": nrm((L, D_FF, D_MODEL), D_FF),
        "ple_norm": gain((L, D_MODEL)),
        "ple_w_gate": nrm((L, D_MODEL, D_MODEL), D_MODEL),
        "ple_w_proj": nrm((L, PLE_DIM, D_MODEL), PLE_DIM),
        "final_norm": gain((D_MODEL,)),
    }


def reference(x, p, ffn1_norm, ffn1_w_gate, ffn1_w_up, ffn1_w_down, mix_norm, w_in,
              pool_w, pool_scale, conv_dw_w, conv_dw_b, conv_ln_g, conv_ln_b, conv_w_out,
              w_out, ffn2_norm, ffn2_w_gate, ffn2_w_up, ffn2_w_down, ple_norm, ple_w_gate,
              ple_w_proj, final_norm):
    B, S, _ = x.shape
    h = x
    split_pts = [D_POOL, D_POOL + D_CONV, D_POOL + 2 * D_CONV, D_POOL + 2 * D_CONV + D_MODEL]
    for i in range(DEPTH):
        h = h + 0.5 * swiglu(rmsnorm(h, ffn1_norm[i]), ffn1_w_gate[i], ffn1_w_up[i], ffn1_w_down[i])

        u = rmsnorm(h, mix_norm[i])
        z = u @ w_in[i]
        z_pool, z_glu_a, z_glu_g, g_pool, g_conv = jnp.split(z, split_pts, axis=-1)

        pooled = causal_multiscale_pool(z_pool).reshape(B, S, N_POOL_GROUPS, POOL_GROUP)
        a = jnp.einsum("bsgc,gcd->bsgd", pooled, pool_w[i]).reshape(B, S, D_POOL)
        a = a * pool_scale[i]

        c = z_glu_a * jax.nn.sigmoid(z_glu_g)
        c = causal_depthwise_conv(c, conv_dw_w[i], conv_dw_b[i])
        c = jax.nn.silu(layernorm(c, conv_ln_g[i], conv_ln_b[i]))
        c = c @ conv_w_out[i]

        m = jax.nn.sigmoid(g_pool) * a + jax.nn.sigmoid(g_conv) * c
        h = h + m @ w_out[i]

        h = h + 0.5 * swiglu(rmsnorm(h, ffn2_norm[i]), ffn2_w_gate[i], ffn2_w_up[i], ffn2_w_down[i])

        gate = jax.nn.sigmoid(rmsnorm(h, ple_norm[i]) @ ple_w_gate[i])
        h = h + gate * (p[i] @ ple_w_proj[i])
    return rmsnorm(h, final_norm)
```

```python
import numpy as np
import concourse.bass as bass
import concourse.mybir as mybir
from concourse.bass_utils import run_bass_kernel_spmd

F32 = mybir.dt.float32
BF16 = mybir.dt.bfloat16
AF = mybir.ActivationFunctionType
ALU = mybir.AluOpType

D = 1024
DFF = 2816
NCH = 8
NJ = 22
T = 512
L = 2
PLE = 256
KC = 31
HZ = 16
HC = 30
SEQ = 8192
NCORES = 8
NSRC = 56
NU = 64
UW = 4096
RING = 5
EARLY = True
RMS_EPS = 1e-6
LN_EPS = 1e-5
NVEC = L * 64 + 8

V_FFN1, V_MIX, V_PSCALE, V_CB, V_LNG, V_LNB, V_FFN2, V_PLE = range(8)


def vcol(l, v, c):
    return (l * 8 + v) * 8 + c


def unit_table():
    units = []
    src = 0
    for jp in range(11):
        units.append(("gu1", jp, 4096, src)); src += 1
    for m in range(8):
        units.append(("dn1", m, NJ * 128, src)); src += 1
    for q in range(4):
        units.append(("glu", q, 4096, src)); src += 1
    for q in range(2):
        units.append(("zp", q, 4096, src)); src += 1
    for c in range(8):
        units.append(("cv", c, KC * 128, None))
    for q in range(2):
        units.append(("gp", q, 4096, src)); src += 1
    units.append(("pl", 0, 2048, src)); src += 1
    units.append(("gc", 0, 4096, src)); src += 1
    units.append(("co", 0, 4096, src)); src += 1
    units.append(("gc", 1, 4096, src)); src += 1
    units.append(("co", 1, 4096, src)); src += 1
    for q in range(2):
        units.append(("wo", q, 4096, src)); src += 1
    for jp in range(11):
        units.append(("gu2", jp, 4096, src)); src += 1
    for m in range(8):
        units.append(("dn2", m, NJ * 128, src)); src += 1
    units.append(("pp", 0, 2048, src)); src += 1
    for q in range(2):
        units.append(("pg", q, 4096, src)); src += 1
    assert len(units) == NU and src == NSRC
    return units


UNITS = unit_table()


def _colblock(W, n):
    K = W.shape[0]
    blk = W[:, n * 128:(n + 1) * 128].reshape(K // 128, 128, 128)
    return np.ascontiguousarray(blk.transpose(1, 0, 2)).reshape(128, (K // 128) * 128)


def host_arrange_weights(inp):
    wsrc = np.zeros((L * NSRC, 128, UW), np.float32)
    for l in range(L):
        win = inp["w_in"][l]
        for (kind, arg, ncols, src) in UNITS:
            if src is None:
                continue
            dst = wsrc[l * NSRC + src]
            if kind in ("gu1", "gu2"):
                wg = inp["ffn1_w_gate" if kind == "gu1" else "ffn2_w_gate"][l]
                wu = inp["ffn1_w_up" if kind == "gu1" else "ffn2_w_up"][l]
                for jj in range(2):
                    j = 2 * arg + jj
                    dst[:, (jj * 2 + 0) * 1024:(jj * 2 + 1) * 1024] = _colblock(wg, j)
                    dst[:, (jj * 2 + 1) * 1024:(jj * 2 + 2) * 1024] = _colblock(wu, j)
            elif kind in ("dn1", "dn2"):
                wd = inp["ffn1_w_down" if kind == "dn1" else "ffn2_w_down"][l]
                dst[:, :NJ * 128] = _colblock(wd, arg)
            elif kind == "glu":
                chunks = []
                for mm in (2 * arg, 2 * arg + 1):
                    chunks += [8 + mm, 16 + mm]
                for i, ch in enumerate(chunks):
                    dst[:, i * 1024:(i + 1) * 1024] = _colblock(win, ch)
            elif kind in ("zp", "gp", "gc"):
                base = {"zp": 0, "gp": 24, "gc": 32}[kind]
                for i in range(4):
                    dst[:, i * 1024:(i + 1) * 1024] = _colblock(win, base + 4 * arg + i)
            elif kind == "pl":
                pw = inp["pool_w"][l]
                for g in range(4):
                    for oc in range(2):
                        for kc in range(2):
                            pos = ((g * 2 + oc) * 2 + kc) * 128
                            dst[:, pos:pos + 128] = pw[g, kc * 128:(kc + 1) * 128, oc * 128:(oc + 1) * 128]
            elif kind in ("co", "wo", "pg"):
                W = inp[{"co": "conv_w_out", "wo": "w_out", "pg": "ple_w_gate"}[kind]][l]
                for i in range(4):
                    dst[:, i * 1024:(i + 1) * 1024] = _colblock(W, 4 * arg + i)
            elif kind == "pp":
                W = inp["ple_w_proj"][l]
                for n in range(8):
                    dst[:, n * 256:(n + 1) * 256] = _colblock(W, n)
            else:
                raise AssertionError(kind)
    return wsrc


def host_arrange_vecs(inp):
    vecs = np.zeros((128, NVEC), np.float32)
    names = ["ffn1_norm", "mix_norm", "pool_scale", "conv_dw_b", "conv_ln_g", "conv_ln_b",
             "ffn2_norm", "ple_norm"]
    for l in range(L):
        for v, nm in enumerate(names):
            vecs[:, vcol(l, v, 0):vcol(l, v, 0) + 8] = inp[nm][l].reshape(8, 128).T
    vecs[:, L * 64:L * 64 + 8] = inp["final_norm"].reshape(8, 128).T
    wtap = np.zeros((128, L * 8 * KC), np.float32)
    for l in range(L):
        w = inp["conv_dw_w"][l].reshape(KC, 8, 128)
        wtap[:, l * 8 * KC:(l + 1) * 8 * KC] = w.transpose(2, 1, 0).reshape(128, 8 * KC)
    return vecs, wtap


def host_consts():
    ident = np.eye(128, dtype=np.float32)
    invc = np.zeros((128, 4 * 16), np.float32)
    for g, w in enumerate((2, 4, 8, 16)):
        for t in range(16):
            invc[:, g * 16 + t] = np.float32(1.0) / np.float32(min(t + 1, w))
    return ident, invc


class _Op:
    __slots__ = ("eng", "fn", "deps", "is_target", "token", "dma")

    def __init__(self, eng, fn, dma):
        self.eng = eng
        self.fn = fn
        self.deps = set()
        self.is_target = False
        self.token = None
        self.dma = dma


class Sched:
    ENGS = ("pe", "act", "dve", "pool", "sp")

    def __init__(self):
        self.ops = []
        self.eng_ops = {e: [] for e in self.ENGS}
        self.last_w = {}
        self.readers = {}
        self.dma_count = {}
        self.pe_log = []
        self.cur_tag = ""

    def add(self, eng, fn, reads=(), writes=(), dma_sem=None, tag=None, npe=1):
        idx = len(self.ops)
        if eng == "pe":
            self.pe_log.append((tag or self.cur_tag, npe))
        dma = None
        if dma_sem is not None:
            n = self.dma_count.get(id(dma_sem), 0) + 1
            self.dma_count[id(dma_sem)] = n
            dma = (dma_sem, 16 * n)
        op = _Op(eng, fn, dma)
        deps = op.deps
        for r in reads:
            w = self.last_w.get(r)
            if w is not None:
                deps.add(w)
            if isinstance(r, tuple) and r[0] == "bank":
                rd = self.readers.get(r)
                if rd:
                    deps.update(v for k_, v in rd.items() if k_ != eng)
        for r in writes:
            w = self.last_w.get(r)
            if w is not None:
                deps.add(w)
            rd = self.readers.get(r)
            if rd:
                deps.update(rd.values())
        keep = set()
        for d in deps:
            dop = self.ops[d]
            if dop.dma is None and dop.eng == eng and eng in ("pe", "sp"):
                continue
            keep.add(d)
            dop.is_target = True
        op.deps = keep
        key = eng if dma is None else ("dma", idx)
        for r in reads:
            self.readers.setdefault(r, {})[key] = idx
        for r in writes:
            self.last_w[r] = idx
            self.readers[r] = {}
        self.ops.append(op)
        self.eng_ops[eng].append(idx)
        return idx

    def assign_tokens(self, eng_sems):
        for e in self.ENGS:
            cnt = 0
            for i in self.eng_ops[e]:
                op = self.ops[i]
                if op.dma is not None:
                    op.token = op.dma
                elif op.is_target:
                    cnt += 1
                    op.token = (eng_sems[e], cnt)

    def emit_engine(self, eng, e, eng_sems):
        waited = {}
        for i in self.eng_ops[eng]:
            op = self.ops[i]
            need = {}
            for d in op.deps:
                sem, val = self.ops[d].token
                k = id(sem)
                if waited.get(k, 0) >= val:
                    continue
                if k not in need or need[k][1] < val:
                    need[k] = (sem, val)
            for k, (sem, val) in need.items():
                e.wait_ge(sem, val)
                waited[k] = val
            inst = op.fn(e)
            if op.dma is None and op.is_target:
                assert inst is not None
                inst.then_inc(eng_sems[eng], 1)


def build_program(NT, stop_after=None):
    from contextlib import ExitStack
    nc = bass.Bass("TRN2", target_bir_lowering=False)
    S_LOC = NT * T
    x_d = nc.dram_tensor("x", [S_LOC, D], F32, kind="ExternalInput").ap()
    p_d = nc.dram_tensor("p", [L, S_LOC, PLE], F32, kind="ExternalInput").ap()
    wsrc_d = nc.dram_tensor("wsrc", [L * NSRC, 128, UW], F32, kind="ExternalInput").ap()
    vecs_d = nc.dram_tensor("vecs", [128, NVEC], F32, kind="ExternalInput").ap()
    wtap_d = nc.dram_tensor("wtap", [128, L * 8 * KC], F32, kind="ExternalInput").ap()
    ident_d = nc.dram_tensor("ident", [128, 128], F32, kind="ExternalInput").ap()
    invc_d = nc.dram_tensor("invc", [128, 64], F32, kind="ExternalInput").ap()
    wbf_d = nc.dram_tensor("wbf", [L * NU, 128, UW], BF16, kind="Internal").ap()
    out_d = nc.dram_tensor("out", [S_LOC, D], F32, kind="ExternalOutput").ap()

    S = Sched()
    with ExitStack() as es:
        def sb(name, shape, dt):
            return es.enter_context(nc.sbuf_tensor(name + "_sb", shape, dt))

        def sem(name):
            return es.enter_context(nc.semaphore(name))

        h = sb("h", [128, NCH, T], F32)
        xn = sb("xn", [128, NCH, T], BF16)
        hid = sb("hid", [128, NJ * T], BF16)
        sq = sb("sq", [128, 4, T], BF16)
        st4 = sb("st4", [128, 8], F32)
        r4 = sb("r4", [128, 8], F32)
        mu4 = sb("mu4", [128, 8], F32)
        dg = sb("dg", [128, 2, T], F32)
        onesf = sb("onesf", [128, 128], F32)
        rstd = sb("rstd", [128, T], F32)
        mu = sb("mu", [128, T], F32)
        mhalf = sb("mhalf", [128, 8], F32)
        tmpA = sb("tmpA", [128, 4, T + HZ], F32)
        tmpB = sb("tmpB", [128, 4, T], F32)
        zp = sb("zp", [128, NCH, T + HZ], F32)
        pooled = sb("pooled", [128, NCH, T], BF16)
        cbf = sb("cbf", [128, NCH, T + HC], BF16)
        lnout = sb("lnout", [128, NCH, T], BF16)
        Y = sb("Y", [128, 4, D], F32)
        xin = sb("xin", [128, 4, D], F32)
        pin = sb("pin", [128, 2, 4, PLE], F32)
        pT = sb("pT", [128, 2, T], BF16)
        ring = sb("ring", [128, RING, UW], BF16)
        halo_z = sb("halo_z", [128, L * NCH, HZ], F32)
        halo_c = sb("halo_c", [128, L * NCH, HC], BF16)
        vecs = sb("vecs", [128, NVEC], F32)
        wtap = sb("wtap", [128, L * 8 * KC], F32)
        ident = sb("ident", [128, 128], F32)
        invc = sb("invc", [128, 64], F32)
        ones = sb("ones", [128, 128], BF16)
        ps = es.enter_context(nc.psum_tensor("ps", [128, 8, T], F32))

        hid32 = hid[:].bitcast(F32)
        Ybf = Y.bitcast(BF16)

        def cbf2(c, a, b):
            off = (c % 2) * 1024
            return Ybf[:, c // 2, off + a: off + b]

        eng_sems = {e: sem("m_" + e) for e in Sched.ENGS}
        slot_sem = [sem(f"slot{i}") for i in range(RING)]
        xin_sem = sem("xin")
        pin_sem = [sem("pin0"), sem("pin1")]
        out_sem = sem("outst")

        def R_hid(j0, j1):
            return [("hid", j) for j in range(j0, j1)]

        def co32(c):
            return hid32[:, c * T:(c + 1) * T]

        def R_co(c):
            return [("hid", 2 * c), ("hid", 2 * c + 1)]

        def ma(c):
            return Y[:, c // 2, (c % 2) * T:(c % 2 + 1) * T]

        def R_ma(c):
            return [("Y", c // 2)]

        bank_rr = [0]

        def next_bank():
            b = bank_rr[0]
            bank_rr[0] = (b + 1) % 6
            return b

        const_list = [(vecs[:], vecs_d, "vecs"), (wtap[:], wtap_d, "wtap"),
                      (ident[:], ident_d, "ident"), (invc[:], invc_d, "invc")]
        for dst, src, res in const_list:
            csem_ = sem("c_" + res)

            def fnc(e, dst=dst, src=src, csem_=csem_):
                return e.dma_start(out=dst, in_=src).then_inc(csem_, 16)
            S.add("sp", fnc, writes=[res], dma_sem=csem_)
        S.add("dve", lambda e: e.memset(ones[:], 1.0), writes=["ones"])
        S.add("dve", lambda e: e.memset(onesf[:], 1.0), writes=["onesf"])
        S.add("dve", lambda e: e.memset(mhalf[:], -0.5), writes=["mhalf"])
        S.add("dve", lambda e: e.memset(halo_z[:], 0.0), writes=[("halo_z", i) for i in range(L * NCH)])
        S.add("dve", lambda e: e.memset(halo_c[:], 0.0), writes=[("halo_c", i) for i in range(L * NCH)])
        CONSTS = ["ones", "vecs", "wtap", "ident", "invc"]

        cast_sems = [sem(f"cast{i_}") for i_ in range(8)]
        cast_list = [(l_, pos) for l_ in range(L) for pos in range(NU) if UNITS[pos][3] is not None]
        cst = {"next": 0}

        def emit_cast(n):
            for _ in range(n):
                k = cst["next"]
                if k >= len(cast_list):
                    return
                cst["next"] = k + 1
                l_, pos = cast_list[k]
                srci = UNITS[pos][3]
                csem = cast_sems[k % 8]
                src_ap = wsrc_d[l_ * NSRC + srci:l_ * NSRC + srci + 1]
                dst_ap = wbf_d[l_ * NU + pos:l_ * NU + pos + 1]

                def fn(e, src_ap=src_ap, dst_ap=dst_ap, csem=csem):
                    return e.dma_start(out=dst_ap, in_=src_ap, max_dma_last_dim=4096).then_inc(csem, 16)
                S.add("pool", fn, writes=[("wbf", l_, pos), ("castsem", k % 8)], dma_sem=csem)

        cast_idx = {lp: k_ for k_, lp in enumerate(cast_list)}
        emit_cast(8)

        zpf = zp[:].rearrange("p a b -> p (a b)").bitcast(BF16)
        Yf = Ybf[:].rearrange("p a b -> p (a b)")
        cbff = cbf[:].rearrange("p a b -> p (a b)")
        poolf = pooled[:].rearrange("p a b -> p (a b)")
        stg = [
            (zpf, 0, [("zp", m_) for m_ in range(4)]),
            (zpf, 4224, [("zp", m_) for m_ in range(4, 8)]),
            (Yf, 0, [("Y", 0), ("Y", 1)]),
            (Yf, 4096, [("Y", 2), ("Y", 3)]),
            (cbff, 0, [("cbf", m_) for m_ in range(8)]),
            (poolf, 0, [("pooled", m_) for m_ in range(8)]),
        ]
        dg_sem = [sem(f"dg{i_}") for i_ in range(6)]
        bcnt = {"dve": 0, "act": 0}
        nbuild = 0
        for l in range(L):
            for c in range(8):
                eng = "act" if nbuild % 8 in (1, 4, 6) else "dve"
                nbuild += 1
                sidx = (0 if eng == "dve" else 3) + bcnt[eng] % 3
                bcnt[eng] += 1
                buf, off, res = stg[sidx]
                u = 25 + c

                def fn(e, l=l, c=c, buf=buf, off=off, eng=eng):
                    inst = None
                    for k in range(KC):
                        col = (l * 8 + c) * KC + k
                        o = buf[:, off + k * 128: off + (k + 1) * 128]
                        if eng == "dve":
                            inst = e.tensor_scalar(out=o, in0=ident[:], scalar1=wtap[:, col:col + 1],
                                                   scalar2=None, op0=ALU.mult)
                        else:
                            inst = e.activation(out=o, in_=ident[:], func=AF.Copy,
                                                scale=wtap[:, col:col + 1])
                    return inst
                S.add(eng, fn, reads=["ident", "wtap"], writes=res)

                def fn2(e, l=l, u=u, buf=buf, off=off, sidx=sidx):
                    return e.dma_start(out=wbf_d[l * NU + u, :, 0:KC * 128],
                                       in_=buf[:, off:off + KC * 128]).then_inc(dg_sem[sidx], 16)
                S.add("act" if eng == "act" else "pool", fn2, reads=res, writes=[("wbf", l, u)],
                      dma_sem=dg_sem[sidx])

        stream = []
        for i in range(NT):
            for l in range(L):
                for u in range(NU):
                    stream.append((i, l, u))
        ws = {"next_load": 0, "pos": 0}

        def issue_load():
            s = ws["next_load"]
            if s >= len(stream):
                return
            ws["next_load"] = s + 1
            _, l, u = stream[s]
            slot = s % RING
            ncols = UNITS[u][2]

            def fn(e, l=l, u=u, slot=slot, ncols=ncols):
                return e.dma_start(out=ring[:, slot, 0:ncols],
                                   in_=wbf_d[l * NU + u, :, 0:ncols]).then_inc(slot_sem[slot], 16)
            rds = [("wbf", l, u)]
            if stream[s][0] == 0 and (l, u) in cast_idx:
                rds.append(("castsem", cast_idx[(l, u)] % 8))
            S.add("sp", fn, reads=rds, writes=[("slot", slot)], dma_sem=slot_sem[slot])

        class UnitCursor:
            def __init__(self):
                self.released = set()
                self.cur = {}

            def acquire(self, i, l, kind, arg):
                if i == 0:
                    emit_cast(1)
                s = ws["pos"]
                ti, tl, tu = stream[s]
                assert (ti, tl) == (i, l) and UNITS[tu][0] == kind and UNITS[tu][1] == arg, \
                    (stream[s], UNITS[tu], i, l, kind, arg)
                ws["pos"] = s + 1
                self.cur[(kind, arg)] = s
                return s % RING

            def release(self, kind, arg):
                s = self.cur.pop((kind, arg))
                self.released.add(s)
                while (ws["next_load"] - RING) in self.released:
                    self.released.discard(ws["next_load"] - RING)
                    if ws["next_load"] >= len(stream):
                        break
                    issue_load()

        UC = UnitCursor()
        for _ in range(RING):
            issue_load()

        def mm_group(bank, pairs, reads, extra_writes=()):
            n = len(pairs)

            def fn(e, pairs=pairs, bank=bank, n=n):
                inst = None
                for k, (lt, rh) in enumerate(pairs):
                    inst = e.matmul(ps[:, bank, :], lhsT=lt, rhs=rh, start=(k == 0), stop=(k == n - 1))
                return inst
            S.add("pe", fn, reads=reads, writes=[("bank", bank)] + list(extra_writes), npe=n)

        def stat_mms(k, col0, first, last):
            def fn(e, k=k, col0=col0, first=first, last=last):
                inst = None
                for s_ in range(4):
                    inst = e.matmul(ps[:, 6, col0 + s_:col0 + s_ + 1], lhsT=sq[:, k, s_ * 128:(s_ + 1) * 128],
                                    rhs=ones[:, 0:1], start=(first and s_ == 0), stop=(last and s_ == 3),
                                    skip_group_check=True)
                return inst
            S.add("pe", fn, reads=[("sq", k), "ones"], writes=[("bank", 6)], tag="stat", npe=4)

        def bcast(src4, col, dsel, bank):
            def fn(e, src4=src4, col=col, dsel=dsel):
                inst = None
                for s_ in range(4):
                    inst = e.tensor_scalar(out=dg[:, dsel, s_ * 128:(s_ + 1) * 128], in0=ident[:],
                                           scalar1=src4[:, col + s_:col + s_ + 1], scalar2=None, op0=ALU.mult)
                return inst
            S.add("dve", fn, reads=["ident", ("small", id(src4))], writes=[("dg", dsel)])
            S.add("pe", lambda e, dsel=dsel, bank=bank: e.matmul(ps[:, bank, :], lhsT=onesf[:], rhs=dg[:, dsel, :],
                                                                 start=True, stop=True),
                  reads=[("dg", dsel), "onesf"], writes=[("bank", bank)], tag="bcast")

        class NormAcc:
            def __init__(self):
                self.n = 0
                self.k = 0
                self.gbase = None
                self.want_xg = True

            def start(self, gbase, want_xg=True, buf="A"):
                self.n = 0
                self.gbase = gbase
                self.want_xg = want_xg
                self.buf = buf

            def chunk_act(self, c):
                k = self.k
                self.k = (k + 1) % 4
                S.add("act", lambda e, c=c, k=k: e.activation(out=sq[:, k, :], in_=h[:, c, :], func=AF.Square),
                      reads=[("h", c)], writes=[("sq", k)])
                if self.want_xg and EARLY:
                    gb = self.gbase
                    dst = lnout if self.buf == "A" else xn
                    rn = "lnout" if self.buf == "A" else "xn"
                    S.add("act", lambda e, c=c, gb=gb, dst=dst: e.activation(out=dst[:, c, :], in_=h[:, c, :],
                                                                             func=AF.Copy,
                                                                             scale=vecs[:, gb + c:gb + c + 1]),
                          reads=[("h", c), "vecs"], writes=[(rn, c)])
                return k

            def chunk_pe(self, k):
                stat_mms(k, 0, self.n == 0, self.n == NCH - 1)
                self.n += 1

            def finish_rstd(self):
                S.add("act", lambda e: e.activation(out=st4[:, 0:4], in_=ps[:, 6, 0:4], func=AF.Identity,
                                                    scale=1.0 / D, bias=epsr[:, 0:1]),
                      reads=[("bank", 6), "eps"], writes=[("small", id(st4))])
                S.add("pool", lambda e: e.tensor_tensor(out=r4[:, 0:4], in0=st4[:, 0:4], in1=mhalf[:, 0:4], op=ALU.pow),
                      reads=[("small", id(st4)), "mhalf"], writes=[("small", id(r4))])
                bcast(r4, 0, 0, 7)
                if EARLY:
                    S.add("act", lambda e: e.activation(out=rstd[:], in_=ps[:, 7, :], func=AF.Copy),
                          reads=[("bank", 7)], writes=["rstd"])

            def xn_ops(self, gbase, c0=0, c1=NCH):
                for c in range(c0, c1):
                    if EARLY:
                        S.add("dve", lambda e, c=c: e.scalar_tensor_tensor(
                            out=xn[:, c, :], in0=h[:, c, :], scalar=vecs[:, gbase + c:gbase + c + 1],
                            in1=rstd[:], op0=ALU.mult, op1=ALU.mult),
                            reads=[("h", c), "rstd", "vecs"], writes=[("xn", c)])
                    else:
                        S.add("dve", lambda e, c=c: e.scalar_tensor_tensor(
                            out=xn[:, c, :], in0=h[:, c, :], scalar=vecs[:, gbase + c:gbase + c + 1],
                            in1=ps[:, 7, :], op0=ALU.mult, op1=ALU.mult),
                            reads=[("h", c), ("bank", 7), "vecs"], writes=[("xn", c)])

        def run_boundary(n_items, mm_fn, evac_fn, xn_gbase=None):
            pend = {}
            pend[0] = mm_fn(0, True)
            pend[1] = mm_fn(1, True)
            NA.finish_rstd()
            pend[2] = mm_fn(2, True)
            evac_fn(0, pend[0], True)
            xq = 0
            for q in range(3, n_items):
                pend[q] = mm_fn(q, True)
                evac_fn(q - 2, pend[q - 2], True)
                if xn_gbase is not None and xq < NCH:
                    NA.xn_ops(xn_gbase, xq, min(NCH, xq + 2))
                    xq += 2
            evac_fn(n_items - 2, pend[n_items - 2], True)
            evac_fn(n_items - 1, pend[n_items - 1], True)
            if xn_gbase is not None and xq < NCH:
                NA.xn_ops(xn_gbase, xq, NCH)

        eps_t = sb("eps_t", [128, 2], F32)
        epsr = eps_t
        S.add("dve", lambda e: e.memset(eps_t[:, 0:1], RMS_EPS), writes=["eps"])
        S.add("dve", lambda e: e.memset(eps_t[:, 1:2], LN_EPS), reads=["eps"], writes=["eps"])

        NA = NormAcc()
        tA = [0]
        tB = [0]

        def nextA():
            k = tA[0]
            tA[0] = (k + 1) % 4
            return k

        def nextB():
            k = tB[0]
            tB[0] = (k + 1) % 4
            return k

        class Lag:
            def __init__(self, lag):
                self.lag = lag
                self.q = []

            def push(self, c):
                self.q.append(NA.chunk_act(c))
                if len(self.q) > self.lag:
                    NA.chunk_pe(self.q.pop(0))

            def flush(self):
                while self.q:
                    NA.chunk_pe(self.q.pop(0))

        def ffn(i, l, which, gbase, next_gbase):
            S.cur_tag = f"t{i}l{l}ffn{which}"
            gu = "gu1" if which == 1 else "gu2"
            dn = "dn1" if which == 1 else "dn2"
            st = {"slot": None}

            def mm_j(j, early):
                jp, jj = divmod(j, 2)
                if jj == 0:
                    st["slot"] = UC.acquire(i, l, gu, jp)
                slot = st["slot"]
                bg = next_bank()
                bu = next_bank()
                src = xn
                rn = "xn"
                for (bank, sel) in ((bg, 0), (bu, 1)):
                    base = (jj * 2 + sel) * 1024
                    pairs = [(ring[:, slot, base + k * 128: base + (k + 1) * 128], src[:, k, :])
                             for k in range(NCH)]
                    mm_group(bank, pairs, reads=[("slot", slot)] + [(rn, k) for k in range(NCH)])
                if jj == 1:
                    UC.release(gu, jp)
                return (bg, bu)

            def evac_j(j, banks, early):
                bg, bu = banks
                ka = nextA()
                if early:
                    k1 = nextB()
                    S.add("dve", lambda e, bg=bg, k1=k1: e.tensor_tensor(out=tmpB[:, k1, :], in0=ps[:, bg, :],
                                                                         in1=rstd[:], op=ALU.mult),
                          reads=[("bank", bg), "rstd"], writes=[("tmpB", k1)])
                    S.add("act", lambda e, k1=k1, ka=ka: e.activation(out=tmpA[:, ka, 0:T], in_=tmpB[:, k1, :],
                                                                     func=AF.Silu),
                          reads=[("tmpB", k1)], writes=[("tmpA", ka)])
                    k2 = nextB()
                    S.add("dve", lambda e, bu=bu, k2=k2: e.tensor_tensor(out=tmpB[:, k2, :], in0=ps[:, bu, :],
                                                                         in1=rstd[:], op=ALU.mult),
                          reads=[("bank", bu), "rstd"], writes=[("tmpB", k2)])
                    S.add("dve", lambda e, ka=ka, k2=k2, j=j: e.tensor_tensor(
                        out=hid[:, j * T:(j + 1) * T], in0=tmpA[:, ka, 0:T], in1=tmpB[:, k2, :], op=ALU.mult),
                        reads=[("tmpA", ka), ("tmpB", k2)], writes=[("hid", j)])
                else:
                    S.add("act", lambda e, bg=bg, ka=ka: e.activation(out=tmpA[:, ka, 0:T], in_=ps[:, bg, :],
                                                                     func=AF.Silu),
                          reads=[("bank", bg)], writes=[("tmpA", ka)])
                    S.add("dve", lambda e, bu=bu, ka=ka, j=j: e.tensor_tensor(
                        out=hid[:, j * T:(j + 1) * T], in0=tmpA[:, ka, 0:T], in1=ps[:, bu, :], op=ALU.mult),
                        reads=[("tmpA", ka), ("bank", bu)], writes=[("hid", j)])

            run_boundary(NJ, mm_j, evac_j)

            NA.start(next_gbase, True, "A")
            lag = Lag(1)
            for m in range(NCH):
                slot = UC.acquire(i, l, dn, m)
                b = next_bank()
                pairs = [(ring[:, slot, j * 128:(j + 1) * 128], hid[:, j * T:(j + 1) * T]) for j in range(NJ)]
                mm_group(b, pairs, reads=[("slot", slot)] + R_hid(0, NJ))
                UC.release(dn, m)
                S.add("dve", lambda e, b=b, m=m: e.scalar_tensor_tensor(
                    out=h[:, m, :], in0=ps[:, b, :], scalar=0.5, in1=h[:, m, :], op0=ALU.mult, op1=ALU.add),
                    reads=[("bank", b), ("h", m)], writes=[("h", m)])
                lag.push(m)
            lag.flush()

        def mixer(i, l):
            first_tile = (i == 0)
            S.cur_tag = f"t{i}l{l}mix"
            for m in range(NCH):
                S.add("pool", lambda e, m=m: e.tensor_copy(out=zp[:, m, 0:HZ], in_=halo_z[:, l * NCH + m, :]),
                      reads=[("halo_z", l * NCH + m)], writes=[("zp", m)])
                S.add("pool", lambda e, m=m: e.tensor_copy(out=cbf[:, m, 0:HC], in_=halo_c[:, l * NCH + m, :]),
                      reads=[("halo_c", l * NCH + m)], writes=[("cbf", m)])

            def zpool_chunk(m, slot):
                b = next_bank()
                base = (m % 4) * 1024
                pairs = [(ring[:, slot, base + k * 128: base + (k + 1) * 128], xn[:, k, :]) for k in range(NCH)]
                mm_group(b, pairs, reads=[("slot", slot)] + [("xn", k) for k in range(NCH)])
                S.add("act", lambda e, b=b, m=m: e.activation(out=zp[:, m, HZ:HZ + T], in_=ps[:, b, :], func=AF.Copy),
                      reads=[("bank", b)], writes=[("zp", m)])
                S.add("pool", lambda e, m=m: e.tensor_copy(out=halo_z[:, l * NCH + m, :], in_=zp[:, m, T:T + HZ]),
                      reads=[("zp", m)], writes=[("halo_z", l * NCH + m)])

            def pool_sums(m):
                g = m // 2
                w = 2 << g
                src = zp[:, m, :]
                src_res = ("zp", m)
                lo = 0
                sh = 1
                ksrc = None
                for lev in range(g + 1):
                    kd = nextA()
                    nlo = lo + sh
                    if ksrc is None:
                        a0 = zp[:, m, nlo:T + HZ]
                        a1 = zp[:, m, nlo - sh:T + HZ - sh]
                    else:
                        a0 = tmpA[:, ksrc, nlo:T + HZ]
                        a1 = tmpA[:, ksrc, nlo - sh:T + HZ - sh]
                    S.add("pool" if g == 3 else "dve", lambda e, kd=kd, nlo=nlo, a0=a0, a1=a1: e.tensor_tensor(
                        out=tmpA[:, kd, nlo:T + HZ], in0=a0, in1=a1, op=ALU.add),
                        reads=[src_res], writes=[("tmpA", kd)])
                    src_res = ("tmpA", kd)
                    ksrc = kd
                    lo = nlo
                    sh *= 2
                S.add("dve", lambda e, ksrc=ksrc, m=m, w=w: e.scalar_tensor_tensor(
                    out=pooled[:, m, :], in0=tmpA[:, ksrc, HZ:HZ + T], scalar=1.0 / w, in1=zp[:, m, HZ:HZ + T],
                    op0=ALU.mult, op1=ALU.subtract),
                    reads=[("tmpA", ksrc), ("zp", m)], writes=[("pooled", m)])
                if first_tile:
                    kb = nextB()
                    S.add("dve", lambda e, ksrc=ksrc, kb=kb, g=g: e.tensor_tensor(
                        out=tmpB[:, kb, 0:HZ], in0=tmpA[:, ksrc, HZ:2 * HZ], in1=invc[:, g * 16:(g + 1) * 16],
                        op=ALU.mult),
                        reads=[("tmpA", ksrc), "invc"], writes=[("tmpB", kb)])
                    S.add("dve", lambda e, kb=kb, m=m: e.tensor_tensor(
                        out=pooled[:, m, 0:HZ], in0=tmpB[:, kb, 0:HZ], in1=zp[:, m, HZ:2 * HZ], op=ALU.subtract),
                        reads=[("tmpB", kb), ("zp", m)], writes=[("pooled", m)])

            gst = {"slot": None}

            def glu_mm(m, early):
                if m % 2 == 0:
                    gst["slot"] = UC.acquire(i, l, "glu", m // 2)
                slot = gst["slot"]
                ba = next_bank()
                bgk = next_bank()
                src = lnout if early else xn
                rn = "lnout" if early else "xn"
                for (bank, sel) in ((ba, 0), (bgk, 1)):
                    base = ((m % 2) * 2 + sel) * 1024
                    pairs = [(ring[:, slot, base + k * 128: base + (k + 1) * 128], src[:, k, :]) for k in range(NCH)]
                    mm_group(bank, pairs, reads=[("slot", slot)] + [(rn, k) for k in range(NCH)])
                if m % 2 == 1:
                    UC.release("glu", m // 2)
                return (ba, bgk)

            def glu_evac(m, banks, early):
                ba, bgk = banks
                kb = nextB()
                if early:
                    k0 = nextB()
                    S.add("dve", lambda e, bgk=bgk, k0=k0: e.tensor_tensor(out=tmpB[:, k0, :], in0=ps[:, bgk, :],
                                                                           in1=rstd[:], op=ALU.mult),
                          reads=[("bank", bgk), "rstd"], writes=[("tmpB", k0)])
                    S.add("act", lambda e, k0=k0, kb=kb: e.activation(out=tmpB[:, kb, :], in_=tmpB[:, k0, :],
                                                                     func=AF.Tanh, scale=0.5),
                          reads=[("tmpB", k0)], writes=[("tmpB", kb)])
                    k3 = nextB()
                    S.add("dve", lambda e, ba=ba, kb=kb, k3=k3: e.scalar_tensor_tensor(
                        out=tmpB[:, k3, :], in0=tmpB[:, kb, :], scalar=1.0, in1=ps[:, ba, :],
                        op0=ALU.add, op1=ALU.mult),
                        reads=[("tmpB", kb), ("bank", ba)], writes=[("tmpB", k3)])
                    S.add("dve", lambda e, k3=k3, m=m: e.tensor_tensor(out=cbf[:, m, HC:HC + T], in0=tmpB[:, k3, :],
                                                                       in1=rstd[:], op=ALU.mult),
                          reads=[("tmpB", k3), "rstd"], writes=[("cbf", m)])
                else:
                    S.add("act", lambda e, bgk=bgk, kb=kb: e.activation(out=tmpB[:, kb, :], in_=ps[:, bgk, :],
                                                                        func=AF.Tanh, scale=0.5),
                          reads=[("bank", bgk)], writes=[("tmpB", kb)])
                    S.add("dve", lambda e, ba=ba, kb=kb, m=m: e.scalar_tensor_tensor(
                        out=cbf[:, m, HC:HC + T], in0=tmpB[:, kb, :], scalar=1.0, in1=ps[:, ba, :],
                        op0=ALU.add, op1=ALU.mult),
                        reads=[("tmpB", kb), ("bank", ba)], writes=[("cbf", m)])
                S.add("pool", lambda e, m=m: e.tensor_copy(out=halo_c[:, l * NCH + m, :], in_=cbf[:, m, T:T + HC]),
                      reads=[("cbf", m)], writes=[("halo_c", l * NCH + m)])
                S.add("pool", lambda e, m=m: e.tensor_copy(out=cbf2(m, 0, T + HC - 1), in_=cbf[:, m, 1:T + HC]),
                      reads=[("cbf", m)], writes=[("Y", m // 2)])

            run_boundary(NCH, glu_mm, glu_evac, xn_gbase=vcol(l, V_MIX, 0))

            for m in range(NCH):
                if m % 4 == 0:
                    zslot = UC.acquire(i, l, "zp", m // 4)
                zpool_chunk(m, zslot)
                if m % 4 == 3:
                    UC.release("zp", m // 4)
                pool_sums(m)

            lnb = (6, 7)
            cbias = vcol(l, V_CB, 0)
            pend = []

            def ln_stat_act(c):
                k1 = NA.k
                NA.k = (k1 + 1) % 4
                k2 = NA.k
                NA.k = (k2 + 1) % 4
                S.add("act", lambda e, c=c, k1=k1: e.activation(out=sq[:, k1, :], in_=co32(c), func=AF.Copy),
                      reads=R_co(c), writes=[("sq", k1)])
                S.add("act", lambda e, c=c, k2=k2: e.activation(out=sq[:, k2, :], in_=co32(c), func=AF.Square),
                      reads=R_co(c), writes=[("sq", k2)])
                return (c, k1, k2)

            def ln_stat_pe(ck):
                c, k1, k2 = ck
                stat_mms(k1, 0, c == 0, False)
                stat_mms(k2, 4, False, c == NCH - 1)

            for c in range(NCH):
                slot = UC.acquire(i, l, "cv", c)
                b = next_bank()
                pairs = [(ring[:, slot, k * 128:(k + 1) * 128],
                          cbf[:, c, k:k + T] if k % 2 == 0 else cbf2(c, k - 1, k - 1 + T)) for k in range(KC)]
                mm_group(b, pairs, reads=[("slot", slot), ("cbf", c), ("Y", c // 2)])
                UC.release("cv", c)
                if pend:
                    ln_stat_pe(pend.pop(0))
                S.add("act", lambda e, b=b, c=c: e.activation(
                    out=co32(c), in_=ps[:, b, :], func=AF.Identity, scale=0.5,
                    bias=vecs[:, cbias + c:cbias + c + 1]),
                    reads=[("bank", b), "vecs"], writes=R_co(c))
                pend.append(ln_stat_act(c))
            while pend:
                ln_stat_pe(pend.pop(0))

            S.add("dve", lambda e: e.tensor_scalar(out=mu4[:, 0:4], in0=ps[:, 6, 0:4], scalar1=1.0 / D, scalar2=None,
                                                   op0=ALU.mult),
                  reads=[("bank", 6)], writes=[("small", id(mu4))])
            S.add("dve", lambda e: e.tensor_tensor(out=mu4[:, 4:8], in0=mu4[:, 0:4], in1=mu4[:, 0:4], op=ALU.mult),
                  reads=[("small", id(mu4))], writes=[("small", id(mu4))])
            S.add("dve", lambda e: e.scalar_tensor_tensor(
                out=st4[:, 0:4], in0=ps[:, 6, 4:8], scalar=1.0 / D, in1=mu4[:, 4:8], op0=ALU.mult, op1=ALU.subtract),
                reads=[("bank", 6), ("small", id(mu4))], writes=[("small", id(st4))])
            S.add("pool", lambda e: e.tensor_scalar(out=st4[:, 4:8], in0=st4[:, 0:4], scalar1=1.0, scalar2=LN_EPS,
                                                    op0=ALU.mult, op1=ALU.add),
                  reads=[("small", id(st4))], writes=[("small", id(st4))])
            S.add("pool", lambda e: e.tensor_tensor(out=r4[:, 0:4], in0=st4[:, 4:8], in1=mhalf[:, 0:4], op=ALU.pow),
                  reads=[("small", id(st4)), "mhalf"], writes=[("small", id(r4))])
            S.add("dve", lambda e: e.scalar_tensor_tensor(
                out=r4[:, 4:8], in0=mu4[:, 0:4], scalar=-1.0, in1=r4[:, 0:4], op0=ALU.mult, op1=ALU.mult),
                reads=[("small", id(mu4)), ("small", id(r4))], writes=[("small", id(r4))])
            bcast(r4, 0, 0, 7)
            S.add("act", lambda e: e.activation(out=rstd[:], in_=ps[:, 7, :], func=AF.Copy),
                  reads=[("bank", 7)], writes=["rstd"])
            bx = next_bank()
            bcast(r4, 4, 1, bx)
            S.add("act", lambda e, bx=bx: e.activation(out=mu[:], in_=ps[:, bx, :], func=AF.Copy),
                  reads=[("bank", bx)], writes=["mu"])

            lg = vcol(l, V_LNG, 0)
            lb = vcol(l, V_LNB, 0)

            def ln_apply_chunk(c):
                k1 = nextB()
                S.add("dve", lambda e, c=c, k1=k1: e.tensor_tensor(out=tmpB[:, k1, :], in0=co32(c), in1=rstd[:],
                                                                   op=ALU.mult),
                      reads=R_co(c) + ["rstd"], writes=[("tmpB", k1)])
                k2 = nextB()
                S.add("dve", lambda e, k1=k1, k2=k2: e.tensor_tensor(out=tmpB[:, k2, :], in0=tmpB[:, k1, :],
                                                                     in1=mu[:], op=ALU.add),
                      reads=[("tmpB", k1), "mu"], writes=[("tmpB", k2)])
                S.add("act", lambda e, c=c, k2=k2: e.activation(
                    out=lnout[:, c, :], in_=tmpB[:, k2, :], func=AF.Silu,
                    scale=vecs[:, lg + c:lg + c + 1], bias=vecs[:, lb + c:lb + c + 1]),
                    reads=[("tmpB", k2), "vecs"], writes=[("lnout", c)])

            psc = vcol(l, V_PSCALE, 0)
            for m in range(NCH):
                if m % 4 == 0:
                    gslot = UC.acquire(i, l, "gp", m // 4)
                b = next_bank()
                base = (m % 4) * 1024
                pairs = [(ring[:, gslot, base + k * 128: base + (k + 1) * 128], xn[:, k, :]) for k in range(NCH)]
                mm_group(b, pairs, reads=[("slot", gslot)] + [("xn", k) for k in range(NCH)])
                if m % 4 == 3:
                    UC.release("gp", m // 4)
                S.add("act", lambda e, b=b, m=m: e.activation(out=ma(m), in_=ps[:, b, :], func=AF.Tanh, scale=0.5),
                      reads=[("bank", b)], writes=R_ma(m))
                ln_apply_chunk(m)
            pslot = UC.acquire(i, l, "pl", 0)
            for m in range(NCH):
                g = m // 2
                oc = m % 2
                b = next_bank()
                pairs = [(ring[:, pslot, ((g * 2 + oc) * 2 + kc) * 128:((g * 2 + oc) * 2 + kc + 1) * 128],
                          pooled[:, 2 * g + kc, :]) for kc in range(2)]
                mm_group(b, pairs, reads=[("slot", pslot), ("pooled", 2 * g), ("pooled", 2 * g + 1)])
                kb = nextB()
                S.add("act", lambda e, b=b, kb=kb, m=m: e.activation(
                    out=tmpB[:, kb, :], in_=ps[:, b, :], func=AF.Copy, scale=vecs[:, psc + m:psc + m + 1]),
                    reads=[("bank", b), "vecs"], writes=[("tmpB", kb)])
                S.add("dve", lambda e, kb=kb, m=m: e.scalar_tensor_tensor(
                    out=ma(m), in0=ma(m), scalar=1.0, in1=tmpB[:, kb, :], op0=ALU.add, op1=ALU.mult),
                    reads=R_ma(m) + [("tmpB", kb)], writes=R_ma(m))
            UC.release("pl", 0)

            for m in range(NCH):
                if m % 4 == 0:
                    gslot = UC.acquire(i, l, "gc", m // 4)
                    cslot = UC.acquire(i, l, "co", m // 4)
                bg = next_bank()
                base = (m % 4) * 1024
                pairs = [(ring[:, gslot, base + k * 128: base + (k + 1) * 128], xn[:, k, :]) for k in range(NCH)]
                mm_group(bg, pairs, reads=[("slot", gslot)] + [("xn", k) for k in range(NCH)])
                bc = next_bank()
                pairs = [(ring[:, cslot, base + k * 128: base + (k + 1) * 128], lnout[:, k, :]) for k in range(NCH)]
                mm_group(bc, pairs, reads=[("slot", cslot)] + [("lnout", k) for k in range(NCH)])
                if m % 4 == 3:
                    UC.release("gc", m // 4)
                    UC.release("co", m // 4)
                k1 = nextB()
                S.add("act", lambda e, bg=bg, k1=k1: e.activation(out=tmpB[:, k1, :], in_=ps[:, bg, :],
                                                                  func=AF.Tanh, scale=0.5),
                      reads=[("bank", bg)], writes=[("tmpB", k1)])
                k2 = nextB()
                S.add("dve", lambda e, bc=bc, k1=k1, k2=k2: e.scalar_tensor_tensor(
                    out=tmpB[:, k2, :], in0=tmpB[:, k1, :], scalar=1.0, in1=ps[:, bc, :], op0=ALU.add, op1=ALU.mult),
                    reads=[("tmpB", k1), ("bank", bc)], writes=[("tmpB", k2)])
                S.add("dve", lambda e, k2=k2, m=m: e.tensor_tensor(out=pooled[:, m, :], in0=tmpB[:, k2, :],
                                                                   in1=ma(m), op=ALU.add),
                      reads=[("tmpB", k2)] + R_ma(m), writes=[("pooled", m)])

            NA.start(vcol(l, V_FFN2, 0), True, "B")
            lag = Lag(2)
            for m in range(NCH):
                if m % 4 == 0:
                    wslot = UC.acquire(i, l, "wo", m // 4)
                b = next_bank()
                base = (m % 4) * 1024
                pairs = [(ring[:, wslot, base + k * 128: base + (k + 1) * 128], pooled[:, k, :]) for k in range(NCH)]
                mm_group(b, pairs, reads=[("slot", wslot)] + [("pooled", k) for k in range(NCH)])
                if m % 4 == 3:
                    UC.release("wo", m // 4)
                S.add("dve", lambda e, b=b, m=m: e.scalar_tensor_tensor(
                    out=h[:, m, :], in0=ps[:, b, :], scalar=0.5, in1=h[:, m, :], op0=ALU.mult, op1=ALU.add),
                    reads=[("bank", b), ("h", m)], writes=[("h", m)])
                lag.push(m)
            lag.flush()

        def load_p(i, l):
            k = (i * L + l) % 2

            def fn(e, i=i, l=l, k=k):
                return e.dma_start(out=pin[:, k, :, :],
                                   in_=p_d[l, i * T:(i + 1) * T, :].rearrange("(s p) f -> p s f", p=128)
                                   ).then_inc(pin_sem[k], 16)
            S.add("act", fn, writes=[("pin", k)], dma_sem=pin_sem[k])

        def ple_ptrans(i, l):
            k = (i * L + l) % 2
            for kc in range(2):
                b = next_bank()

                def fn(e, b=b, kc=kc, k=k):
                    inst = None
                    for s in range(4):
                        inst = e.transpose(ps[:, b, s * 128:(s + 1) * 128], pin[:, k, s, kc * 128:(kc + 1) * 128],
                                           ident[:])
                    return inst
                S.add("pe", fn, reads=[("pin", k), "ident"], writes=[("bank", b)], npe=4, tag="pT")
                S.add("act", lambda e, b=b, kc=kc: e.activation(out=pT[:, kc, :], in_=ps[:, b, :], func=AF.Copy),
                      reads=[("bank", b)], writes=[("pT", kc)])

        def ple(i, l, next_gbase, want_xg):
            S.cur_tag = f"t{i}l{l}ple"
            ple_ptrans(i, l)
            pst = {"pslot": None, "gslot": None}
            NA.start(next_gbase, want_xg, "B")
            lag = Lag(2)

            def ple_mm(m, early):
                if m == 0:
                    pst["pslot"] = UC.acquire(i, l, "pp", 0)
                if m % 4 == 0:
                    pst["gslot"] = UC.acquire(i, l, "pg", m // 4)
                pslot, gslot = pst["pslot"], pst["gslot"]
                src = lnout if early else xn
                rn = "lnout" if early else "xn"
                bg = next_bank()
                base = (m % 4) * 1024
                pairs = [(ring[:, gslot, base + kk * 128: base + (kk + 1) * 128], src[:, kk, :]) for kk in range(NCH)]
                mm_group(bg, pairs, reads=[("slot", gslot)] + [(rn, kk) for kk in range(NCH)])
                bp = next_bank()
                pairs = [(ring[:, pslot, (m * 2 + kc) * 128:(m * 2 + kc + 1) * 128], pT[:, kc, :]) for kc in range(2)]
                mm_group(bp, pairs, reads=[("slot", pslot), ("pT", 0), ("pT", 1)])
                if m % 4 == 3:
                    UC.release("pg", m // 4)
                if m == NCH - 1:
                    UC.release("pp", 0)
                return (bg, bp)

            def ple_evac(m, banks, early):
                bg, bp = banks
                k1 = nextB()
                if early:
                    k0 = nextB()
                    S.add("dve", lambda e, bg=bg, k0=k0: e.tensor_tensor(out=tmpB[:, k0, :], in0=ps[:, bg, :],
                                                                         in1=rstd[:], op=ALU.mult),
                          reads=[("bank", bg), "rstd"], writes=[("tmpB", k0)])
                    S.add("act", lambda e, k0=k0, k1=k1: e.activation(out=tmpB[:, k1, :], in_=tmpB[:, k0, :],
                                                                     func=AF.Tanh, scale=0.5),
                          reads=[("tmpB", k0)], writes=[("tmpB", k1)])
                else:
                    S.add("act", lambda e, bg=bg, k1=k1: e.activation(out=tmpB[:, k1, :], in_=ps[:, bg, :],
                                                                      func=AF.Tanh, scale=0.5),
                          reads=[("bank", bg)], writes=[("tmpB", k1)])
                k2 = nextB()
                S.add("dve", lambda e, bp=bp, k1=k1, k2=k2: e.scalar_tensor_tensor(
                    out=tmpB[:, k2, :], in0=tmpB[:, k1, :], scalar=1.0, in1=ps[:, bp, :], op0=ALU.add, op1=ALU.mult),
                    reads=[("tmpB", k1), ("bank", bp)], writes=[("tmpB", k2)])
                S.add("dve", lambda e, k2=k2, m=m: e.scalar_tensor_tensor(
                    out=h[:, m, :], in0=tmpB[:, k2, :], scalar=0.5, in1=h[:, m, :], op0=ALU.mult, op1=ALU.add),
                    reads=[("tmpB", k2), ("h", m)], writes=[("h", m)])
                lag.push(m)

            run_boundary(NCH, ple_mm, ple_evac)
            lag.flush()

        def load_x(i):
            def fn(e, i=i):
                return e.dma_start(out=xin[:], in_=x_d[i * T:(i + 1) * T, :].rearrange("(s p) f -> p s f", p=128)
                                   ).then_inc(xin_sem, 16)
            S.add("act", fn, writes=["xin"], dma_sem=xin_sem)

        def x_to_h(i):
            NA.start(vcol(0, V_FFN1, 0), True, "B")
            lag = Lag(1)
            for c in range(NCH):
                b = next_bank()

                def fn(e, b=b, c=c):
                    inst = None
                    for s in range(4):
                        inst = e.transpose(ps[:, b, s * 128:(s + 1) * 128], xin[:, s, c * 128:(c + 1) * 128], ident[:])
                    return inst
                S.add("pe", fn, reads=["xin", "ident"], writes=[("bank", b)], tag="xT", npe=4)
                eng = "act" if c % 2 == 0 else "dve"
                if eng == "act":
                    S.add("act", lambda e, b=b, c=c: e.activation(out=h[:, c, :], in_=ps[:, b, :], func=AF.Copy),
                          reads=[("bank", b)], writes=[("h", c)])
                else:
                    S.add("dve", lambda e, b=b, c=c: e.tensor_copy(out=h[:, c, :], in_=ps[:, b, :]),
                          reads=[("bank", b)], writes=[("h", c)])
                lag.push(c)
            lag.flush()

        def store_out(i, raw):
            if not raw:
                NA.finish_rstd()
            gf = L * 64
            for c in range(NCH):
                if raw:
                    S.add("dve", lambda e, c=c: e.tensor_copy(out=co32(c), in_=h[:, c, :]),
                          reads=[("h", c)], writes=R_co(c))
                else:
                    S.add("dve", lambda e, c=c: e.scalar_tensor_tensor(
                        out=co32(c), in0=h[:, c, :], scalar=vecs[:, gf + c:gf + c + 1],
                        in1=(rstd[:] if EARLY else ps[:, 7, :]), op0=ALU.mult, op1=ALU.mult),
                        reads=[("h", c), ("rstd" if EARLY else ("bank", 7)), "vecs"], writes=R_co(c))
            for s in range(4):
                for half in range(2):
                    b = next_bank()

                    def fn(e, b=b, s=s, half=half):
                        inst = None
                        for cc in range(4):
                            c = half * 4 + cc
                            inst = e.transpose(ps[:, b, cc * 128:(cc + 1) * 128],
                                               hid32[:, c * T + s * 128: c * T + (s + 1) * 128], ident[:])
                        return inst
                    S.add("pe", fn, reads=[r for c in range(half * 4, half * 4 + 4) for r in R_co(c)] + ["ident"],
                          writes=[("bank", b)], tag="outT", npe=4)
                    if half == 0:
                        S.add("act", lambda e, b=b, s=s: e.activation(out=Y[:, s, 0:512], in_=ps[:, b, :], func=AF.Copy),
                              reads=[("bank", b)], writes=[("Y", s)])
                    else:
                        S.add("dve", lambda e, b=b, s=s: e.tensor_copy(out=Y[:, s, 512:1024], in_=ps[:, b, :]),
                              reads=[("bank", b)], writes=[("Y", s)])

            def fn(e, i=i):
                return e.dma_start(out=out_d[i * T:(i + 1) * T, :].rearrange("(s p) f -> p s f", p=128),
                                   in_=Y[:]).then_inc(out_sem, 16)
            S.add("act", fn, reads=[("Y", s) for s in range(4)], dma_sem=out_sem)

        stages_all = ["ffn1", "mix", "ffn2", "ple"]
        load_x(0)
        load_p(0, 0)
        for i in range(NT):
            x_to_h(i)
            if i + 1 < NT:
                load_x(i + 1)
            stopped = False
            for l in range(L):
                ffn(i, l, 1, vcol(l, V_FFN1, 0), vcol(l, V_MIX, 0))
                if stop_after == (l, "ffn1"):
                    stopped = True
                    break
                mixer(i, l)
                if stop_after == (l, "mix"):
                    stopped = True
                    break
                ffn(i, l, 2, vcol(l, V_FFN2, 0), vcol(l, V_PLE, 0))
                if stop_after == (l, "ffn2"):
                    stopped = True
                    break
                if l + 1 < L:
                    ple(i, l, vcol(l + 1, V_FFN1, 0), True)
                else:
                    ple(i, l, L * 64, False)
                if l + 1 < L:
                    load_p(i, l + 1)
                elif i + 1 < NT:
                    load_p(i + 1, 0)
                if stop_after == (l, "ple"):
                    stopped = True
                    break
            if stopped:
                while ws["pos"] < (i + 1) * L * NU:
                    _, sl, su = stream[ws["pos"]]
                    UC.acquire(i, sl, UNITS[su][0], UNITS[su][1])
                    UC.release(UNITS[su][0], UNITS[su][1])
                store_out(i, raw=True)
            else:
                store_out(i, raw=False)
        S.add("act", lambda e: None, writes=[("Y", s) for s in range(4)])

        S.assign_tokens(eng_sems)
        nc._pe_log = S.pe_log
        with nc.Block() as block:
            @block.tensor
            def _(e):
                S.emit_engine("pe", e, eng_sems)

            @block.scalar
            def _(e):
                S.emit_engine("act", e, eng_sems)

            @block.vector
            def _(e):
                S.emit_engine("dve", e, eng_sems)

            @block.gpsimd
            def _(e):
                S.emit_engine("pool", e, eng_sems)

            @block.sync
            def _(e):
                S.emit_engine("sp", e, eng_sems)
    return nc


def _run(inputs, NT, stop_after=None, trace=False):
    inp = {k: np.asarray(v) for k, v in inputs.items()}
    S_LOC = NT * T
    wsrc = host_arrange_weights(inp)
    vecs, wtap = host_arrange_vecs(inp)
    ident, invc = host_consts()
    nc = build_program(NT, stop_after=stop_after)
    in_maps = []
    for c in range(NCORES):
        in_maps.append({
            "x": np.ascontiguousarray(inp["x"][c, :S_LOC, :]),
            "p": np.ascontiguousarray(inp["p"][:, c, :S_LOC, :]),
            "wsrc": wsrc, "vecs": vecs, "wtap": wtap, "ident": ident, "invc": invc,
        })
    res = run_bass_kernel_spmd(nc, in_maps, core_ids=list(range(NCORES)), **({"trace": True} if trace else {}))
    out = np.stack([res.results[c]["out"] for c in range(NCORES)], axis=0)
    return out.astype(np.float32, copy=False), res


def kernel(**inputs):
    out, _ = _run(inputs, SEQ // T)
    return out
```

```python
import numpy as np
import concourse.bass as bass
import concourse.mybir as mybir
from concourse.bass_utils import run_bass_kernel_spmd

F32 = mybir.dt.float32
BF16 = mybir.dt.bfloat16
AF = mybir.ActivationFunctionType
ALU = mybir.AluOpType

D = 1024
DFF = 2816
NCH = 8
NJ = 22
T = 512
L = 2
PLE = 256
KC = 31
HZ = 16
HC = 30
SEQ = 8192
NCORES = 8
NSRC = 56
NU = 64
UW = 4096
RING = 5
EARLY = True
RMS_EPS = 1e-6
LN_EPS = 1e-5
NVEC = L * 64 + 8

V_FFN1, V_MIX, V_PSCALE, V_CB, V_LNG, V_LNB, V_FFN2, V_PLE = range(8)


def vcol(l, v, c):
    return (l * 8 + v) * 8 + c


def unit_table():
    units = []
    src = 0
    for jp in range(11):
        units.append(("gu1", jp, 4096, src)); src += 1
    for m in range(8):
        units.append(("dn1", m, NJ * 128, src)); src += 1
    for q in range(4):
        units.append(("glu", q, 4096, src)); src += 1
    for q in range(2):
        units.append(("zp", q, 4096, src)); src += 1
    for c in range(8):
        units.append(("cv", c, KC * 128, None))
    for q in range(2):
        units.append(("gp", q, 4096, src)); src += 1
    units.append(("pl", 0, 2048, src)); src += 1
    units.append(("gc", 0, 4096, src)); src += 1
    units.append(("co", 0, 4096, src)); src += 1
    units.append(("gc", 1, 4096, src)); src += 1
    units.append(("co", 1, 4096, src)); src += 1
    for q in range(2):
        units.append(("wo", q, 4096, src)); src += 1
    for jp in range(11):
        units.append(("gu2", jp, 4096, src)); src += 1
    for m in range(8):
        units.append(("dn2", m, NJ * 128, src)); src += 1
    units.append(("pp", 0, 2048, src)); src += 1
    for q in range(2):
        units.append(("pg", q, 4096, src)); src += 1
    assert len(units) == NU and src == NSRC
    return units


UNITS = unit_table()


def _colblock(W, n):
    K = W.shape[0]
    blk = W[:, n * 128:(n + 1) * 128].reshape(K // 128, 128, 128)
    return np.ascontiguousarray(blk.transpose(1, 0, 2)).reshape(128, (K // 128) * 128)


def host_arrange_weights(inp):
    wsrc = np.zeros((L * NSRC, 128, UW), np.float32)
    for l in range(L):
        win = inp["w_in"][l]
        for (kind, arg, ncols, src) in UNITS:
            if src is None:
                continue
            dst = wsrc[l * NSRC + src]
            if kind in ("gu1", "gu2"):
                wg = inp["ffn1_w_gate" if kind == "gu1" else "ffn2_w_gate"][l]
                wu = inp["ffn1_w_up" if kind == "gu1" else "ffn2_w_up"][l]
                for jj in range(2):
                    j = 2 * arg + jj
                    dst[:, (jj * 2 + 0) * 1024:(jj * 2 + 1) * 1024] = _colblock(wg, j)
                    dst[:, (jj * 2 + 1) * 1024:(jj * 2 + 2) * 1024] = _colblock(wu, j)
            elif kind in ("dn1", "dn2"):
                wd = inp["ffn1_w_down" if kind == "dn1" else "ffn2_w_down"][l]
                dst[:, :NJ * 128] = _colblock(wd, arg)
            elif kind == "glu":
                chunks = []
                for mm in (2 * arg, 2 * arg + 1):
                    chunks += [8 + mm, 16 + mm]
                for i, ch in enumerate(chunks):
                    dst[:, i * 1024:(i + 1) * 1024] = _colblock(win, ch)
            elif kind in ("zp", "gp", "gc"):
                base = {"zp": 0, "gp": 24, "gc": 32}[kind]
                for i in range(4):
                    dst[:, i * 1024:(i + 1) * 1024] = _colblock(win, base + 4 * arg + i)
            elif kind == "pl":
                pw = inp["pool_w"][l]
                for g in range(4):
                    for oc in range(2):
                        for kc in range(2):
                            pos = ((g * 2 + oc) * 2 + kc) * 128
                            dst[:, pos:pos + 128] = pw[g, kc * 128:(kc + 1) * 128, oc * 128:(oc + 1) * 128]
            elif kind in ("co", "wo", "pg"):
                W = inp[{"co": "conv_w_out", "wo": "w_out", "pg": "ple_w_gate"}[kind]][l]
                for i in range(4):
                    dst[:, i * 1024:(i + 1) * 1024] = _colblock(W, 4 * arg + i)
            elif kind == "pp":
                W = inp["ple_w_proj"][l]
                for n in range(8):
                    dst[:, n * 256:(n + 1) * 256] = _colblock(W, n)
            else:
                raise AssertionError(kind)
    return wsrc


def host_arrange_vecs(inp):
    vecs = np.zeros((128, NVEC), np.float32)
    names = ["ffn1_norm", "mix_norm", "pool_scale", "conv_dw_b", "conv_ln_g", "conv_ln_b",
             "ffn2_norm", "ple_norm"]
    for l in range(L):
        for v, nm in enumerate(names):
            vecs[:, vcol(l, v, 0):vcol(l, v, 0) + 8] = inp[nm][l].reshape(8, 128).T
    vecs[:, L * 64:L * 64 + 8] = inp["final_norm"].reshape(8, 128).T
    wtap = np.zeros((128, L * 8 * KC), np.float32)
    for l in range(L):
        w = inp["conv_dw_w"][l].reshape(KC, 8, 128)
        wtap[:, l * 8 * KC:(l + 1) * 8 * KC] = w.transpose(2, 1, 0).reshape(128, 8 * KC)
    return vecs, wtap


def host_consts():
    ident = np.eye(128, dtype=np.float32)
    invc = np.zeros((128, 4 * 16), np.float32)
    for g, w in enumerate((2, 4, 8, 16)):
        for t in range(16):
            invc[:, g * 16 + t] = np.float32(1.0) / np.float32(min(t + 1, w))
    return ident, invc


class _Op:
    __slots__ = ("eng", "fn", "deps", "is_target", "token", "dma")

    def __init__(self, eng, fn, dma):
        self.eng = eng
        self.fn = fn
        self.deps = set()
        self.is_target = False
        self.token = None
        self.dma = dma


class Sched:
    ENGS = ("pe", "act", "dve", "pool", "sp")

    def __init__(self):
        self.ops = []
        self.eng_ops = {e: [] for e in self.ENGS}
        self.last_w = {}
        self.readers = {}
        self.dma_count = {}
        self.pe_log = []
        self.cur_tag = ""

    def add(self, eng, fn, reads=(), writes=(), dma_sem=None, tag=None, npe=1):
        idx = len(self.ops)
        if eng == "pe":
            self.pe_log.append((tag or self.cur_tag, npe))
        dma = None
        if dma_sem is not None:
            n = self.dma_count.get(id(dma_sem), 0) + 1
            self.dma_count[id(dma_sem)] = n
            dma = (dma_sem, 16 * n)
        op = _Op(eng, fn, dma)
        deps = op.deps
        for r in reads:
            w = self.last_w.get(r)
            if w is not None:
                deps.add(w)
            if isinstance(r, tuple) and r[0] == "bank":
                rd = self.readers.get(r)
                if rd:
                    deps.update(v for k_, v in rd.items() if k_ != eng)
        for r in writes:
            w = self.last_w.get(r)
            if w is not None:
                deps.add(w)
            rd = self.readers.get(r)
            if rd:
                deps.update(rd.values())
        keep = set()
        for d in deps:
            dop = self.ops[d]
            if dop.dma is None and dop.eng == eng and eng in ("pe", "sp"):
                continue
            keep.add(d)
            dop.is_target = True
        op.deps = keep
        key = eng if dma is None else ("dma", idx)
        for r in reads:
            self.readers.setdefault(r, {})[key] = idx
        for r in writes:
            self.last_w[r] = idx
            self.readers[r] = {}
        self.ops.append(op)
        self.eng_ops[eng].append(idx)
        return idx

    def assign_tokens(self, eng_sems):
        for e in self.ENGS:
            cnt = 0
            for i in self.eng_ops[e]:
                op = self.ops[i]
                if op.dma is not None:
                    op.token = op.dma
                elif op.is_target:
                    cnt += 1
                    op.token = (eng_sems[e], cnt)

    def emit_engine(self, eng, e, eng_sems):
        waited = {}
        for i in self.eng_ops[eng]:
            op = self.ops[i]
            need = {}
            for d in op.deps:
                sem, val = self.ops[d].token
                k = id(sem)
                if waited.get(k, 0) >= val:
                    continue
                if k not in need or need[k][1] < val:
                    need[k] = (sem, val)
            for k, (sem, val) in need.items():
                e.wait_ge(sem, val)
                waited[k] = val
            inst = op.fn(e)
            if op.dma is None and op.is_target:
                assert inst is not None
                inst.then_inc(eng_sems[eng], 1)


def build_program(NT, stop_after=None):
    from contextlib import ExitStack
    nc = bass.Bass("TRN2", target_bir_lowering=False)
    S_LOC = NT * T
    x_d = nc.dram_tensor("x", [S_LOC, D], F32, kind="ExternalInput").ap()
    p_d = nc.dram_tensor("p", [L, S_LOC, PLE], F32, kind="ExternalInput").ap()
    wsrc_d = nc.dram_tensor("wsrc", [L * NSRC, 128, UW], F32, kind="ExternalInput").ap()
    vecs_d = nc.dram_tensor("vecs", [128, NVEC], F32, kind="ExternalInput").ap()
    wtap_d = nc.dram_tensor("wtap", [128, L * 8 * KC], F32, kind="ExternalInput").ap()
    ident_d = nc.dram_tensor("ident", [128, 128], F32, kind="ExternalInput").ap()
    invc_d = nc.dram_tensor("invc", [128, 64], F32, kind="ExternalInput").ap()
    wbf_d = nc.dram_tensor("wbf", [L * NU, 128, UW], BF16, kind="Internal").ap()
    out_d = nc.dram_tensor("out", [S_LOC, D], F32, kind="ExternalOutput").ap()

    S = Sched()
    with ExitStack() as es:
        def sb(name, shape, dt):
            return es.enter_context(nc.sbuf_tensor(name + "_sb", shape, dt))

        def sem(name):
            return es.enter_context(nc.semaphore(name))

        h = sb("h", [128, NCH, T], F32)
        xn = sb("xn", [128, NCH, T], BF16)
        hid = sb("hid", [128, NJ * T], BF16)
        sq = sb("sq", [128, 4, T], BF16)
        st4 = sb("st4", [128, 8], F32)
        r4 = sb("r4", [128, 8], F32)
        mu4 = sb("mu4", [128, 8], F32)
        dg = sb("dg", [128, 2, T], F32)
        onesf = sb("onesf", [128, 128], F32)
        rstd = sb("rstd", [128, T], F32)
        mu = sb("mu", [128, T], F32)
        mhalf = sb("mhalf", [128, 8], F32)
        tmpA = sb("tmpA", [128, 4, T + HZ], F32)
        tmpB = sb("tmpB", [128, 4, T], F32)
        zp = sb("zp", [128, NCH, T + HZ], F32)
        pooled = sb("pooled", [128, NCH, T], BF16)
        cbf = sb("cbf", [128, NCH, T + HC], BF16)
        lnout = sb("lnout", [128, NCH, T], BF16)
        Y = sb("Y", [128, 4, D], F32)
        xin = sb("xin", [128, 4, D], F32)
        pin = sb("pin", [128, 2, 4, PLE], F32)
        pT = sb("pT", [128, 2, T], BF16)
        ring = sb("ring", [128, RING, UW], BF16)
        halo_z = sb("halo_z", [128, L * NCH, HZ], F32)
        halo_c = sb("halo_c", [128, L * NCH, HC], BF16)
        vecs = sb("vecs", [128, NVEC], F32)
        wtap = sb("wtap", [128, L * 8 * KC], F32)
        ident = sb("ident", [128, 128], F32)
        invc = sb("invc", [128, 64], F32)
        ones = sb("ones", [128, 128], BF16)
        ps = es.enter_context(nc.psum_tensor("ps", [128, 8, T], F32))

        hid32 = hid[:].bitcast(F32)
        Ybf = Y.bitcast(BF16)

        def cbf2(c, a, b):
            off = (c % 2) * 1024
            return Ybf[:, c // 2, off + a: off + b]

        eng_sems = {e: sem("m_" + e) for e in Sched.ENGS}
        slot_sem = [sem(f"slot{i}") for i in range(RING)]
        xin_sem = sem("xin")
        pin_sem = [sem("pin0"), sem("pin1")]
        out_sem = sem("outst")

        def R_hid(j0, j1):
            return [("hid", j) for j in range(j0, j1)]

        def co32(c):
            return hid32[:, c * T:(c + 1) * T]

        def R_co(c):
            return [("hid", 2 * c), ("hid", 2 * c + 1)]

        def ma(c):
            return Y[:, c // 2, (c % 2) * T:(c % 2 + 1) * T]

        def R_ma(c):
            return [("Y", c // 2)]

        bank_rr = [0]

        def next_bank():
            b = bank_rr[0]
            bank_rr[0] = (b + 1) % 6
            return b

        const_list = [(vecs[:], vecs_d, "vecs"), (wtap[:], wtap_d, "wtap"),
                      (ident[:], ident_d, "ident"), (invc[:], invc_d, "invc")]
        for dst, src, res in const_list:
            csem_ = sem("c_" + res)

            def fnc(e, dst=dst, src=src, csem_=csem_):
                return e.dma_start(out=dst, in_=src).then_inc(csem_, 16)
            S.add("sp", fnc, writes=[res], dma_sem=csem_)
        S.add("dve", lambda e: e.memset(ones[:], 1.0), writes=["ones"])
        S.add("dve", lambda e: e.memset(onesf[:], 1.0), writes=["onesf"])
        S.add("dve", lambda e: e.memset(mhalf[:], -0.5), writes=["mhalf"])
        S.add("dve", lambda e: e.memset(halo_z[:], 0.0), writes=[("halo_z", i) for i in range(L * NCH)])
        S.add("dve", lambda e: e.memset(halo_c[:], 0.0), writes=[("halo_c", i) for i in range(L * NCH)])
        CONSTS = ["ones", "vecs", "wtap", "ident", "invc"]

        cast_sems = [sem(f"cast{i_}") for i_ in range(8)]
        cast_list = [(l_, pos) for l_ in range(L) for pos in range(NU) if UNITS[pos][3] is not None]
        cst = {"next": 0}

        def emit_cast(n):
            for _ in range(n):
                k = cst["next"]
                if k >= len(cast_list):
                    return
                cst["next"] = k + 1
                l_, pos = cast_list[k]
                srci = UNITS[pos][3]
                csem = cast_sems[k % 8]
                src_ap = wsrc_d[l_ * NSRC + srci:l_ * NSRC + srci + 1]
                dst_ap = wbf_d[l_ * NU + pos:l_ * NU + pos + 1]

                def fn(e, src_ap=src_ap, dst_ap=dst_ap, csem=csem):
                    return e.dma_start(out=dst_ap, in_=src_ap, max_dma_last_dim=4096).then_inc(csem, 16)
                S.add("pool", fn, writes=[("wbf", l_, pos), ("castsem", k % 8)], dma_sem=csem)

        cast_idx = {lp: k_ for k_, lp in enumerate(cast_list)}
        emit_cast(8)

        zpf = zp[:].rearrange("p a b -> p (a b)").bitcast(BF16)
        Yf = Ybf[:].rearrange("p a b -> p (a b)")
        cbff = cbf[:].rearrange("p a b -> p (a b)")
        poolf = pooled[:].rearrange("p a b -> p (a b)")
        stg = [
            (zpf, 0, [("zp", m_) for m_ in range(4)]),
            (zpf, 4224, [("zp", m_) for m_ in range(4, 8)]),
            (Yf, 0, [("Y", 0), ("Y", 1)]),
            (Yf, 4096, [("Y", 2), ("Y", 3)]),
            (cbff, 0, [("cbf", m_) for m_ in range(8)]),
            (poolf, 0, [("pooled", m_) for m_ in range(8)]),
        ]
        dg_sem = [sem(f"dg{i_}") for i_ in range(6)]
        bcnt = {"dve": 0, "act": 0}
        nbuild = 0
        for l in range(L):
            for c in range(8):
                eng = "act" if nbuild % 8 in (1, 4, 6) else "dve"
                nbuild += 1
                sidx = (0 if eng == "dve" else 3) + bcnt[eng] % 3
                bcnt[eng] += 1
                buf, off, res = stg[sidx]
                u = 25 + c

                def fn(e, l=l, c=c, buf=buf, off=off, eng=eng):
                    inst = None
                    for k in range(KC):
                        col = (l * 8 + c) * KC + k
                        o = buf[:, off + k * 128: off + (k + 1) * 128]
                        if eng == "dve":
                            inst = e.tensor_scalar(out=o, in0=ident[:], scalar1=wtap[:, col:col + 1],
                                                   scalar2=None, op0=ALU.mult)
                        else:
                            inst = e.activation(out=o, in_=ident[:], func=AF.Copy,
                                                scale=wtap[:, col:col + 1])
                    return inst
                S.add(eng, fn, reads=["ident", "wtap"], writes=res)

                def fn2(e, l=l, u=u, buf=buf, off=off, sidx=sidx):
                    return e.dma_start(out=wbf_d[l * NU + u, :, 0:KC * 128],
                                       in_=buf[:, off:off + KC * 128]).then_inc(dg_sem[sidx], 16)
                S.add("act" if eng == "act" else "pool", fn2, reads=res, writes=[("wbf", l, u)],
                      dma_sem=dg_sem[sidx])

        stream = []
        for i in range(NT):
            for l in range(L):
                for u in range(NU):
                    stream.append((i, l, u))
        ws = {"next_load": 0, "pos": 0}

        def issue_load():
            s = ws["next_load"]
            if s >= len(stream):
                return
            ws["next_load"] = s + 1
            _, l, u = stream[s]
            slot = s % RING
            ncols = UNITS[u][2]

            def fn(e, l=l, u=u, slot=slot, ncols=ncols):
                return e.dma_start(out=ring[:, slot, 0:ncols],
                                   in_=wbf_d[l * NU + u, :, 0:ncols]).then_inc(slot_sem[slot], 16)
            rds = [("wbf", l, u)]
            if stream[s][0] == 0 and (l, u) in cast_idx:
                rds.append(("castsem", cast_idx[(l, u)] % 8))
            S.add("sp", fn, reads=rds, writes=[("slot", slot)], dma_sem=slot_sem[slot])

        class UnitCursor:
            def __init__(self):
                self.released = set()
                self.cur = {}

            def acquire(self, i, l, kind, arg):
                if i == 0:
                    emit_cast(1)
                s = ws["pos"]
                ti, tl, tu = stream[s]
                assert (ti, tl) == (i, l) and UNITS[tu][0] == kind and UNITS[tu][1] == arg, \
                    (stream[s], UNITS[tu], i, l, kind, arg)
                ws["pos"] = s + 1
                self.cur[(kind, arg)] = s
                return s % RING

            def release(self, kind, arg):
                s = self.cur.pop((kind, arg))
                self.released.add(s)
                while (ws["next_load"] - RING) in self.released:
                    self.released.discard(ws["next_load"] - RING)
                    if ws["next_load"] >= len(stream):
                        break
                    issue_load()

        UC = UnitCursor()
        for _ in range(RING):
            issue_load()

        def mm_group(bank, pairs, reads, extra_writes=()):
            n = len(pairs)

            def fn(e, pairs=pairs, bank=bank, n=n):
                inst = None
                for k, (lt, rh) in enumerate(pairs):
                    inst = e.matmul(ps[:, bank, :], lhsT=lt, rhs=rh, start=(k == 0), stop=(k == n - 1))
                return inst
            S.add("pe", fn, reads=reads, writes=[("bank", bank)] + list(extra_writes), npe=n)

        def stat_mms(k, col0, first, last):
            def fn(e, k=k, col0=col0, first=first, last=last):
                inst = None
                for s_ in range(4):
                    inst = e.matmul(ps[:, 6, col0 + s_:col0 + s_ + 1], lhsT=sq[:, k, s_ * 128:(s_ + 1) * 128],
                                    rhs=ones[:, 0:1], start=(first and s_ == 0), stop=(last and s_ == 3),
                                    skip_group_check=True)
                return inst
            S.add("pe", fn, reads=[("sq", k), "ones"], writes=[("bank", 6)], tag="stat", npe=4)

        def bcast(src4, col, dsel, bank):
            def fn(e, src4=src4, col=col, dsel=dsel):
                inst = None
                for s_ in range(4):
                    inst = e.tensor_scalar(out=dg[:, dsel, s_ * 128:(s_ + 1) * 128], in0=ident[:],
                                           scalar1=src4[:, col + s_:col + s_ + 1], scalar2=None, op0=ALU.mult)
                return inst
            S.add("dve", fn, reads=["ident", ("small", id(src4))], writes=[("dg", dsel)])
            S.add("pe", lambda e, dsel=dsel, bank=bank: e.matmul(ps[:, bank, :], lhsT=onesf[:], rhs=dg[:, dsel, :],
                                                                 start=True, stop=True),
                  reads=[("dg", dsel), "onesf"], writes=[("bank", bank)], tag="bcast")

        class NormAcc:
            def __init__(self):
                self.n = 0
                self.k = 0
                self.gbase = None
                self.want_xg = True

            def start(self, gbase, want_xg=True, buf="A"):
                self.n = 0
                self.gbase = gbase
                self.want_xg = want_xg
                self.buf = buf

            def chunk_act(self, c):
                k = self.k
                self.k = (k + 1) % 4
                S.add("act", lambda e, c=c, k=k: e.activation(out=sq[:, k, :], in_=h[:, c, :], func=AF.Square),
                      reads=[("h", c)], writes=[("sq", k)])
                if self.want_xg and EARLY:
                    gb = self.gbase
                    dst = lnout if self.buf == "A" else xn
                    rn = "lnout" if self.buf == "A" else "xn"
                    S.add("act", lambda e, c=c, gb=gb, dst=dst: e.activation(out=dst[:, c, :], in_=h[:, c, :],
                                                                             func=AF.Copy,
                                                                             scale=vecs[:, gb + c:gb + c + 1]),
                          reads=[("h", c), "vecs"], writes=[(rn, c)])
                return k

            def chunk_pe(self, k):
                stat_mms(k, 0, self.n == 0, self.n == NCH - 1)
                self.n += 1

            def finish_rstd(self):
                S.add("act", lambda e: e.activation(out=st4[:, 0:4], in_=ps[:, 6, 0:4], func=AF.Identity,
                                                    scale=1.0 / D, bias=epsr[:, 0:1]),
                      reads=[("bank", 6), "eps"], writes=[("small", id(st4))])
                S.add("pool", lambda e: e.tensor_tensor(out=r4[:, 0:4], in0=st4[:, 0:4], in1=mhalf[:, 0:4], op=ALU.pow),
                      reads=[("small", id(st4)), "mhalf"], writes=[("small", id(r4))])
                bcast(r4, 0, 0, 7)
                if EARLY:
                    S.add("act", lambda e: e.activation(out=rstd[:], in_=ps[:, 7, :], func=AF.Copy),
                          reads=[("bank", 7)], writes=["rstd"])

            def xn_ops(self, gbase, c0=0, c1=NCH):
                for c in range(c0, c1):
                    if EARLY:
                        S.add("dve", lambda e, c=c: e.scalar_tensor_tensor(
                            out=xn[:, c, :], in0=h[:, c, :], scalar=vecs[:, gbase + c:gbase + c + 1],
                            in1=rstd[:], op0=ALU.mult, op1=ALU.mult),
                            reads=[("h", c), "rstd", "vecs"], writes=[("xn", c)])
                    else:
                        S.add("dve", lambda e, c=c: e.scalar_tensor_tensor(
                            out=xn[:, c, :], in0=h[:, c, :], scalar=vecs[:, gbase + c:gbase + c + 1],
                            in1=ps[:, 7, :], op0=ALU.mult, op1=ALU.mult),
                            reads=[("h", c), ("bank", 7), "vecs"], writes=[("xn", c)])

        def run_boundary(n_items, mm_fn, evac_fn, xn_gbase=None):
            pend = {}
            pend[0] = mm_fn(0, True)
            pend[1] = mm_fn(1, True)
            NA.finish_rstd()
            pend[2] = mm_fn(2, True)
            evac_fn(0, pend[0], True)
            xq = 0
            for q in range(3, n_items):
                pend[q] = mm_fn(q, True)
                evac_fn(q - 2, pend[q - 2], True)
                if xn_gbase is not None and xq < NCH:
                    NA.xn_ops(xn_gbase, xq, min(NCH, xq + 2))
                    xq += 2
            evac_fn(n_items - 2, pend[n_items - 2], True)
            evac_fn(n_items - 1, pend[n_items - 1], True)
            if xn_gbase is not None and xq < NCH:
                NA.xn_ops(xn_gbase, xq, NCH)

        eps_t = sb("eps_t", [128, 2], F32)
        epsr = eps_t
        S.add("dve", lambda e: e.memset(eps_t[:, 0:1], RMS_EPS), writes=["eps"])
        S.add("dve", lambda e: e.memset(eps_t[:, 1:2], LN_EPS), reads=["eps"], writes=["eps"])

        NA = NormAcc()
        tA = [0]
        tB = [0]

        def nextA():
            k = tA[0]
            tA[0] = (k + 1) % 4
            return k

        def nextB():
            k = tB[0]
            tB[0] = (k + 1) % 4
            return k

        class Lag:
            def __init__(self, lag):
                self.lag = lag
                self.q = []

            def push(self, c):
                self.q.append(NA.chunk_act(c))
                if len(self.q) > self.lag:
                    NA.chunk_pe(self.q.pop(0))

            def flush(self):
                while self.q:
                    NA.chunk_pe(self.q.pop(0))

        def ffn(i, l, which, gbase, next_gbase):
            S.cur_tag = f"t{i}l{l}ffn{which}"
            gu = "gu1" if which == 1 else "gu2"
            dn = "dn1" if which == 1 else "dn2"
            st = {"slot": None}

            def mm_j(j, early):
                jp, jj = divmod(j, 2)
                if jj == 0:
                    st["slot"] = UC.acquire(i, l, gu, jp)
                slot = st["slot"]
                bg = next_bank()
                bu = next_bank()
                src = xn
                rn = "xn"
                for (bank, sel) in ((bg, 0), (bu, 1)):
                    base = (jj * 2 + sel) * 1024
                    pairs = [(ring[:, slot, base + k * 128: base + (k + 1) * 128], src[:, k, :])
                             for k in range(NCH)]
                    mm_group(bank, pairs, reads=[("slot", slot)] + [(rn, k) for k in range(NCH)])
                if jj == 1:
                    UC.release(gu, jp)
                return (bg, bu)

            def evac_j(j, banks, early):
                bg, bu = banks
                ka = nextA()
                if early:
                    k1 = nextB()
                    S.add("dve", lambda e, bg=bg, k1=k1: e.tensor_tensor(out=tmpB[:, k1, :], in0=ps[:, bg, :],
                                                                         in1=rstd[:], op=ALU.mult),
                          reads=[("bank", bg), "rstd"], writes=[("tmpB", k1)])
                    S.add("act", lambda e, k1=k1, ka=ka: e.activation(out=tmpA[:, ka, 0:T], in_=tmpB[:, k1, :],
                                                                     func=AF.Silu),
                          reads=[("tmpB", k1)], writes=[("tmpA", ka)])
                    k2 = nextB()
                    S.add("dve", lambda e, bu=bu, k2=k2: e.tensor_tensor(out=tmpB[:, k2, :], in0=ps[:, bu, :],
                                                                         in1=rstd[:], op=ALU.mult),
                          reads=[("bank", bu), "rstd"], writes=[("tmpB", k2)])
                    S.add("dve", lambda e, ka=ka, k2=k2, j=j: e.tensor_tensor(
                        out=hid[:, j * T:(j + 1) * T], in0=tmpA[:, ka, 0:T], in1=tmpB[:, k2, :], op=ALU.mult),
                        reads=[("tmpA", ka), ("tmpB", k2)], writes=[("hid", j)])
                else:
                    S.add("act", lambda e, bg=bg, ka=ka: e.activation(out=tmpA[:, ka, 0:T], in_=ps[:, bg, :],
                                                                     func=AF.Silu),
                          reads=[("bank", bg)], writes=[("tmpA", ka)])
                    S.add("dve", lambda e, bu=bu, ka=ka, j=j: e.tensor_tensor(
                        out=hid[:, j * T:(j + 1) * T], in0=tmpA[:, ka, 0:T], in1=ps[:, bu, :], op=ALU.mult),
                        reads=[("tmpA", ka), ("bank", bu)], writes=[("hid", j)])

            run_boundary(NJ, mm_j, evac_j)

            NA.start(next_gbase, True, "A")
            lag = Lag(1)
            for m in range(NCH):
                slot = UC.acquire(i, l, dn, m)
                b = next_bank()
                pairs = [(ring[:, slot, j * 128:(j + 1) * 128], hid[:, j * T:(j + 1) * T]) for j in range(NJ)]
                mm_group(b, pairs, reads=[("slot", slot)] + R_hid(0, NJ))
                UC.release(dn, m)
                S.add("dve", lambda e, b=b, m=m: e.scalar_tensor_tensor(
                    out=h[:, m, :], in0=ps[:, b, :], scalar=0.5, in1=h[:, m, :], op0=ALU.mult, op1=ALU.add),
                    reads=[("bank", b), ("h", m)], writes=[("h", m)])
                lag.push(m)
            lag.flush()

        def mixer(i, l):
            first_tile = (i == 0)
            S.cur_tag = f"t{i}l{l}mix"
            for m in range(NCH):
                S.add("pool", lambda e, m=m: e.tensor_copy(out=zp[:, m, 0:HZ], in_=halo_z[:, l * NCH + m, :]),
                      reads=[("halo_z", l * NCH + m)], writes=[("zp", m)])
                S.add("pool", lambda e, m=m: e.tensor_copy(out=cbf[:, m, 0:HC], in_=halo_c[:, l * NCH + m, :]),
                      reads=[("halo_c", l * NCH + m)], writes=[("cbf", m)])

            def zpool_chunk(m, slot):
                b = next_bank()
                base = (m % 4) * 1024
                pairs = [(ring[:, slot, base + k * 128: base + (k + 1) * 128], xn[:, k, :]) for k in range(NCH)]
                mm_group(b, pairs, reads=[("slot", slot)] + [("xn", k) for k in range(NCH)])
                S.add("act", lambda e, b=b, m=m: e.activation(out=zp[:, m, HZ:HZ + T], in_=ps[:, b, :], func=AF.Copy),
                      reads=[("bank", b)], writes=[("zp", m)])
                S.add("pool", lambda e, m=m: e.tensor_copy(out=halo_z[:, l * NCH + m, :], in_=zp[:, m, T:T + HZ]),
                      reads=[("zp", m)], writes=[("halo_z", l * NCH + m)])

            def pool_sums(m):
                g = m // 2
                w = 2 << g
                src = zp[:, m, :]
                src_res = ("zp", m)
                lo = 0
                sh = 1
                ksrc = None
                for lev in range(g + 1):
                    kd = nextA()
                    nlo = lo + sh
                    if ksrc is None:
                        a0 = zp[:, m, nlo:T + HZ]
                        a1 = zp[:, m, nlo - sh:T + HZ - sh]
                    else:
                        a0 = tmpA[:, ksrc, nlo:T + HZ]
                        a1 = tmpA[:, ksrc, nlo - sh:T + HZ - sh]
                    S.add("pool" if g == 3 else "dve", lambda e, kd=kd, nlo=nlo, a0=a0, a1=a1: e.tensor_tensor(
                        out=tmpA[:, kd, nlo:T + HZ], in0=a0, in1=a1, op=ALU.add),
                        reads=[src_res], writes=[("tmpA", kd)])
                    src_res = ("tmpA", kd)
                    ksrc = kd
                    lo = nlo
                    sh *= 2
                S.add("dve", lambda e, ksrc=ksrc, m=m, w=w: e.scalar_tensor_tensor(
                    out=pooled[:, m, :], in0=tmpA[:, ksrc, HZ:HZ + T], scalar=1.0 / w, in1=zp[:, m, HZ:HZ + T],
                    op0=ALU.mult, op1=ALU.subtract),
                    reads=[("tmpA", ksrc), ("zp", m)], writes=[("pooled", m)])
                if first_tile:
                    kb = nextB()
                    S.add("dve", lambda e, ksrc=ksrc, kb=kb, g=g: e.tensor_tensor(
                        out=tmpB[:, kb, 0:HZ], in0=tmpA[:, ksrc, HZ:2 * HZ], in1=invc[:, g * 16:(g + 1) * 16],
                        op=ALU.mult),
                        reads=[("tmpA", ksrc), "invc"], writes=[("tmpB", kb)])
                    S.add("dve", lambda e, kb=kb, m=m: e.tensor_tensor(
                        out=pooled[:, m, 0:HZ], in0=tmpB[:, kb, 0:HZ], in1=zp[:, m, HZ:2 * HZ], op=ALU.subtract),
                        reads=[("tmpB", kb), ("zp", m)], writes=[("pooled", m)])

            gst = {"slot": None}

            def glu_mm(m, early):
                if m % 2 == 0:
                    gst["slot"] = UC.acquire(i, l, "glu", m // 2)
                slot = gst["slot"]
                ba = next_bank()
                bgk = next_bank()
                src = lnout if early else xn
                rn = "lnout" if early else "xn"
                for (bank, sel) in ((ba, 0), (bgk, 1)):
                    base = ((m % 2) * 2 + sel) * 1024
                    pairs = [(ring[:, slot, base + k * 128: base + (k + 1) * 128], src[:, k, :]) for k in range(NCH)]
                    mm_group(bank, pairs, reads=[("slot", slot)] + [(rn, k) for k in range(NCH)])
                if m % 2 == 1:
                    UC.release("glu", m // 2)
                return (ba, bgk)

            def glu_evac(m, banks, early):
                ba, bgk = banks
                kb = nextB()
                if early:
                    k0 = nextB()
                    S.add("dve", lambda e, bgk=bgk, k0=k0: e.tensor_tensor(out=tmpB[:, k0, :], in0=ps[:, bgk, :],
                                                                           in1=rstd[:], op=ALU.mult),
                          reads=[("bank", bgk), "rstd"], writes=[("tmpB", k0)])
                    S.add("act", lambda e, k0=k0, kb=kb: e.activation(out=tmpB[:, kb, :], in_=tmpB[:, k0, :],
                                                                     func=AF.Tanh, scale=0.5),
                          reads=[("tmpB", k0)], writes=[("tmpB", kb)])
                    k3 = nextB()
                    S.add("dve", lambda e, ba=ba, kb=kb, k3=k3: e.scalar_tensor_tensor(
                        out=tmpB[:, k3, :], in0=tmpB[:, kb, :], scalar=1.0, in1=ps[:, ba, :],
                        op0=ALU.add, op1=ALU.mult),
                        reads=[("tmpB", kb), ("bank", ba)], writes=[("tmpB", k3)])
                    S.add("dve", lambda e, k3=k3, m=m: e.tensor_tensor(out=cbf[:, m, HC:HC + T], in0=tmpB[:, k3, :],
                                                                       in1=rstd[:], op=ALU.mult),
                          reads=[("tmpB", k3), "rstd"], writes=[("cbf", m)])
                else:
                    S.add("act", lambda e, bgk=bgk, kb=kb: e.activation(out=tmpB[:, kb, :], in_=ps[:, bgk, :],
                                                                        func=AF.Tanh, scale=0.5),
                          reads=[("bank", bgk)], writes=[("tmpB", kb)])
                    S.add("dve", lambda e, ba=ba, kb=kb, m=m: e.scalar_tensor_tensor(
                        out=cbf[:, m, HC:HC + T], in0=tmpB[:, kb, :], scalar=1.0, in1=ps[:, ba, :],
                        op0=ALU.add, op1=ALU.mult),
                        reads=[("tmpB", kb), ("bank", ba)], writes=[("cbf", m)])
                S.add("pool", lambda e, m=m: e.tensor_copy(out=halo_c[:, l * NCH + m, :], in_=cbf[:, m, T:T + HC]),
                      reads=[("cbf", m)], writes=[("halo_c", l * NCH + m)])
                S.add("pool", lambda e, m=m: e.tensor_copy(out=cbf2(m, 0, T + HC - 1), in_=cbf[:, m, 1:T + HC]),
                      reads=[("cbf", m)], writes=[("Y", m // 2)])

            run_boundary(NCH, glu_mm, glu_evac, xn_gbase=vcol(l, V_MIX, 0))

            for m in range(NCH):
                if m % 4 == 0:
                    zslot = UC.acquire(i, l, "zp", m // 4)
                zpool_chunk(m, zslot)
                if m % 4 == 3:
                    UC.release("zp", m // 4)
                pool_sums(m)

            lnb = (6, 7)
            cbias = vcol(l, V_CB, 0)
            pend = []

            def ln_stat_act(c):
                k1 = NA.k
                NA.k = (k1 + 1) % 4
                k2 = NA.k
                NA.k = (k2 + 1) % 4
                S.add("act", lambda e, c=c, k1=k1: e.activation(out=sq[:, k1, :], in_=co32(c), func=AF.Copy),
                      reads=R_co(c), writes=[("sq", k1)])
                S.add("act", lambda e, c=c, k2=k2: e.activation(out=sq[:, k2, :], in_=co32(c), func=AF.Square),
                      reads=R_co(c), writes=[("sq", k2)])
                return (c, k1, k2)

            def ln_stat_pe(ck):
                c, k1, k2 = ck
                stat_mms(k1, 0, c == 0, False)
                stat_mms(k2, 4, False, c == NCH - 1)

            for c in range(NCH):
                slot = UC.acquire(i, l, "cv", c)
                b = next_bank()
                pairs = [(ring[:, slot, k * 128:(k + 1) * 128],
                          cbf[:, c, k:k + T] if k % 2 == 0 else cbf2(c, k - 1, k - 1 + T)) for k in range(KC)]
                mm_group(b, pairs, reads=[("slot", slot), ("cbf", c), ("Y", c // 2)])
                UC.release("cv", c)
                if pend:
                    ln_stat_pe(pend.pop(0))
                S.add("act", lambda e, b=b, c=c: e.activation(
                    out=co32(c), in_=ps[:, b, :], func=AF.Identity, scale=0.5,
                    bias=vecs[:, cbias + c:cbias + c + 1]),
                    reads=[("bank", b), "vecs"], writes=R_co(c))
                pend.append(ln_stat_act(c))
            while pend:
                ln_stat_pe(pend.pop(0))

            psc = vcol(l, V_PSCALE, 0)
            gpst = {"slot": None}

            def gp_chunk(m):
                if m % 4 == 0:
                    gpst["slot"] = UC.acquire(i, l, "gp", m // 4)
                gslot = gpst["slot"]
                b = next_bank()
                base = (m % 4) * 1024
                pairs = [(ring[:, gslot, base + k * 128: base + (k + 1) * 128], xn[:, k, :]) for k in range(NCH)]
                mm_group(b, pairs, reads=[("slot", gslot)] + [("xn", k) for k in range(NCH)])
                if m % 4 == 3:
                    UC.release("gp", m // 4)
                S.add("act", lambda e, b=b, m=m: e.activation(out=ma(m), in_=ps[:, b, :], func=AF.Tanh, scale=0.5),
                      reads=[("bank", b)], writes=R_ma(m))

            for m in range(3):
                gp_chunk(m)

            S.add("dve", lambda e: e.tensor_scalar(out=mu4[:, 0:4], in0=ps[:, 6, 0:4], scalar1=1.0 / D, scalar2=None,
                                                   op0=ALU.mult),
                  reads=[("bank", 6)], writes=[("small", id(mu4))])
            S.add("dve", lambda e: e.tensor_tensor(out=mu4[:, 4:8], in0=mu4[:, 0:4], in1=mu4[:, 0:4], op=ALU.mult),
                  reads=[("small", id(mu4))], writes=[("small", id(mu4))])
            S.add("dve", lambda e: e.scalar_tensor_tensor(
                out=st4[:, 0:4], in0=ps[:, 6, 4:8], scalar=1.0 / D, in1=mu4[:, 4:8], op0=ALU.mult, op1=ALU.subtract),
                reads=[("bank", 6), ("small", id(mu4))], writes=[("small", id(st4))])
            S.add("pool", lambda e: e.tensor_scalar(out=st4[:, 4:8], in0=st4[:, 0:4], scalar1=1.0, scalar2=LN_EPS,
                                                    op0=ALU.mult, op1=ALU.add),
                  reads=[("small", id(st4))], writes=[("small", id(st4))])
            S.add("pool", lambda e: e.tensor_tensor(out=r4[:, 0:4], in0=st4[:, 4:8], in1=mhalf[:, 0:4], op=ALU.pow),
                  reads=[("small", id(st4)), "mhalf"], writes=[("small", id(r4))])
            S.add("dve", lambda e: e.scalar_tensor_tensor(
                out=r4[:, 4:8], in0=mu4[:, 0:4], scalar=-1.0, in1=r4[:, 0:4], op0=ALU.mult, op1=ALU.mult),
                reads=[("small", id(mu4)), ("small", id(r4))], writes=[("small", id(r4))])
            bcast(r4, 0, 0, 7)
            S.add("act", lambda e: e.activation(out=rstd[:], in_=ps[:, 7, :], func=AF.Copy),
                  reads=[("bank", 7)], writes=["rstd"])
            bx = next_bank()
            bcast(r4, 4, 1, bx)
            S.add("act", lambda e, bx=bx: e.activation(out=mu[:], in_=ps[:, bx, :], func=AF.Copy),
                  reads=[("bank", bx)], writes=["mu"])

            lg = vcol(l, V_LNG, 0)
            lb = vcol(l, V_LNB, 0)

            def ln_apply_chunk(c):
                k1 = nextB()
                S.add("dve", lambda e, c=c, k1=k1: e.tensor_tensor(out=tmpB[:, k1, :], in0=co32(c), in1=rstd[:],
                                                                   op=ALU.mult),
                      reads=R_co(c) + ["rstd"], writes=[("tmpB", k1)])
                k2 = nextB()
                S.add("dve", lambda e, k1=k1, k2=k2: e.tensor_tensor(out=tmpB[:, k2, :], in0=tmpB[:, k1, :],
                                                                     in1=mu[:], op=ALU.add),
                      reads=[("tmpB", k1), "mu"], writes=[("tmpB", k2)])
                S.add("act", lambda e, c=c, k2=k2: e.activation(
                    out=lnout[:, c, :], in_=tmpB[:, k2, :], func=AF.Silu,
                    scale=vecs[:, lg + c:lg + c + 1], bias=vecs[:, lb + c:lb + c + 1]),
                    reads=[("tmpB", k2), "vecs"], writes=[("lnout", c)])

            for m in range(3, NCH):
                gp_chunk(m)
                ln_apply_chunk(m - 3)
            for c in range(NCH - 3, NCH):
                ln_apply_chunk(c)
            pslot = UC.acquire(i, l, "pl", 0)
            for m in range(NCH):
                g = m // 2
                oc = m % 2
                b = next_bank()
                pairs = [(ring[:, pslot, ((g * 2 + oc) * 2 + kc) * 128:((g * 2 + oc) * 2 + kc + 1) * 128],
                          pooled[:, 2 * g + kc, :]) for kc in range(2)]
                mm_group(b, pairs, reads=[("slot", pslot), ("pooled", 2 * g), ("pooled", 2 * g + 1)])
                kb = nextB()
                S.add("act", lambda e, b=b, kb=kb, m=m: e.activation(
                    out=tmpB[:, kb, :], in_=ps[:, b, :], func=AF.Copy, scale=vecs[:, psc + m:psc + m + 1]),
                    reads=[("bank", b), "vecs"], writes=[("tmpB", kb)])
                S.add("dve", lambda e, kb=kb, m=m: e.scalar_tensor_tensor(
                    out=ma(m), in0=ma(m), scalar=1.0, in1=tmpB[:, kb, :], op0=ALU.add, op1=ALU.mult),
                    reads=R_ma(m) + [("tmpB", kb)], writes=R_ma(m))
            UC.release("pl", 0)

            for m in range(NCH):
                if m % 4 == 0:
                    gslot = UC.acquire(i, l, "gc", m // 4)
                    cslot = UC.acquire(i, l, "co", m // 4)
                bg = next_bank()
                base = (m % 4) * 1024
                pairs = [(ring[:, gslot, base + k * 128: base + (k + 1) * 128], xn[:, k, :]) for k in range(NCH)]
                mm_group(bg, pairs, reads=[("slot", gslot)] + [("xn", k) for k in range(NCH)])
                bc = next_bank()
                pairs = [(ring[:, cslot, base + k * 128: base + (k + 1) * 128], lnout[:, k, :]) for k in range(NCH)]
                mm_group(bc, pairs, reads=[("slot", cslot)] + [("lnout", k) for k in range(NCH)])
                if m % 4 == 3:
                    UC.release("gc", m // 4)
                    UC.release("co", m // 4)
                k1 = nextB()
                S.add("act", lambda e, bg=bg, k1=k1: e.activation(out=tmpB[:, k1, :], in_=ps[:, bg, :],
                                                                  func=AF.Tanh, scale=0.5),
                      reads=[("bank", bg)], writes=[("tmpB", k1)])
                k2 = nextB()
                S.add("dve", lambda e, bc=bc, k1=k1, k2=k2: e.scalar_tensor_tensor(
                    out=tmpB[:, k2, :], in0=tmpB[:, k1, :], scalar=1.0, in1=ps[:, bc, :], op0=ALU.add, op1=ALU.mult),
                    reads=[("tmpB", k1), ("bank", bc)], writes=[("tmpB", k2)])
                S.add("dve", lambda e, k2=k2, m=m: e.tensor_tensor(out=pooled[:, m, :], in0=tmpB[:, k2, :],
                                                                   in1=ma(m), op=ALU.add),
                      reads=[("tmpB", k2)] + R_ma(m), writes=[("pooled", m)])

            NA.start(vcol(l, V_FFN2, 0), True, "B")
            lag = Lag(2)
            for m in range(NCH):
                if m % 4 == 0:
                    wslot = UC.acquire(i, l, "wo", m // 4)
                b = next_bank()
                base = (m % 4) * 1024
                pairs = [(ring[:, wslot, base + k * 128: base + (k + 1) * 128], pooled[:, k, :]) for k in range(NCH)]
                mm_group(b, pairs, reads=[("slot", wslot)] + [("pooled", k) for k in range(NCH)])
                if m % 4 == 3:
                    UC.release("wo", m // 4)
                S.add("dve", lambda e, b=b, m=m: e.scalar_tensor_tensor(
                    out=h[:, m, :], in0=ps[:, b, :], scalar=0.5, in1=h[:, m, :], op0=ALU.mult, op1=ALU.add),
                    reads=[("bank", b), ("h", m)], writes=[("h", m)])
                lag.push(m)
            lag.flush()

        def load_p(i, l):
            k = (i * L + l) % 2

            def fn(e, i=i, l=l, k=k):
                return e.dma_start(out=pin[:, k, :, :],
                                   in_=p_d[l, i * T:(i + 1) * T, :].rearrange("(s p) f -> p s f", p=128)
                                   ).then_inc(pin_sem[k], 16)
            S.add("act", fn, writes=[("pin", k)], dma_sem=pin_sem[k])

        def ple_ptrans(i, l):
            k = (i * L + l) % 2
            for kc in range(2):
                b = next_bank()

                def fn(e, b=b, kc=kc, k=k):
                    inst = None
                    for s in range(4):
                        inst = e.transpose(ps[:, b, s * 128:(s + 1) * 128], pin[:, k, s, kc * 128:(kc + 1) * 128],
                                           ident[:])
                    return inst
                S.add("pe", fn, reads=[("pin", k), "ident"], writes=[("bank", b)], npe=4, tag="pT")
                S.add("act", lambda e, b=b, kc=kc: e.activation(out=pT[:, kc, :], in_=ps[:, b, :], func=AF.Copy),
                      reads=[("bank", b)], writes=[("pT", kc)])

        def ple(i, l, next_gbase, want_xg):
            S.cur_tag = f"t{i}l{l}ple"
            ple_ptrans(i, l)
            pst = {"pslot": None, "gslot": None}
            NA.start(next_gbase, want_xg, "B")
            lag = Lag(2)

            def ple_mm(m, early):
                if m == 0:
                    pst["pslot"] = UC.acquire(i, l, "pp", 0)
                if m % 4 == 0:
                    pst["gslot"] = UC.acquire(i, l, "pg", m // 4)
                pslot, gslot = pst["pslot"], pst["gslot"]
                src = lnout if early else xn
                rn = "lnout" if early else "xn"
                bg = next_bank()
                base = (m % 4) * 1024
                pairs = [(ring[:, gslot, base + kk * 128: base + (kk + 1) * 128], src[:, kk, :]) for kk in range(NCH)]
                mm_group(bg, pairs, reads=[("slot", gslot)] + [(rn, kk) for kk in range(NCH)])
                bp = next_bank()
                pairs = [(ring[:, pslot, (m * 2 + kc) * 128:(m * 2 + kc + 1) * 128], pT[:, kc, :]) for kc in range(2)]
                mm_group(bp, pairs, reads=[("slot", pslot), ("pT", 0), ("pT", 1)])
                if m % 4 == 3:
                    UC.release("pg", m // 4)
                if m == NCH - 1:
                    UC.release("pp", 0)
                return (bg, bp)

            def ple_evac(m, banks, early):
                bg, bp = banks
                k1 = nextB()
                if early:
                    k0 = nextB()
                    S.add("dve", lambda e, bg=bg, k0=k0: e.tensor_tensor(out=tmpB[:, k0, :], in0=ps[:, bg, :],
                                                                         in1=rstd[:], op=ALU.mult),
                          reads=[("bank", bg), "rstd"], writes=[("tmpB", k0)])
                    S.add("act", lambda e, k0=k0, k1=k1: e.activation(out=tmpB[:, k1, :], in_=tmpB[:, k0, :],
                                                                     func=AF.Tanh, scale=0.5),
                          reads=[("tmpB", k0)], writes=[("tmpB", k1)])
                else:
                    S.add("act", lambda e, bg=bg, k1=k1: e.activation(out=tmpB[:, k1, :], in_=ps[:, bg, :],
                                                                      func=AF.Tanh, scale=0.5),
                          reads=[("bank", bg)], writes=[("tmpB", k1)])
                k2 = nextB()
                S.add("dve", lambda e, bp=bp, k1=k1, k2=k2: e.scalar_tensor_tensor(
                    out=tmpB[:, k2, :], in0=tmpB[:, k1, :], scalar=1.0, in1=ps[:, bp, :], op0=ALU.add, op1=ALU.mult),
                    reads=[("tmpB", k1), ("bank", bp)], writes=[("tmpB", k2)])
                S.add("dve", lambda e, k2=k2, m=m: e.scalar_tensor_tensor(
                    out=h[:, m, :], in0=tmpB[:, k2, :], scalar=0.5, in1=h[:, m, :], op0=ALU.mult, op1=ALU.add),
                    reads=[("tmpB", k2), ("h", m)], writes=[("h", m)])
                lag.push(m)

            run_boundary(NCH, ple_mm, ple_evac)
            lag.flush()

        def load_x(i):
            def fn(e, i=i):
                return e.dma_start(out=xin[:], in_=x_d[i * T:(i + 1) * T, :].rearrange("(s p) f -> p s f", p=128)
                                   ).then_inc(xin_sem, 16)
            S.add("act", fn, writes=["xin"], dma_sem=xin_sem)

        def x_to_h(i):
            NA.start(vcol(0, V_FFN1, 0), True, "B")
            lag = Lag(1)
            for c in range(NCH):
                b = next_bank()

                def fn(e, b=b, c=c):
                    inst = None
                    for s in range(4):
                        inst = e.transpose(ps[:, b, s * 128:(s + 1) * 128], xin[:, s, c * 128:(c + 1) * 128], ident[:])
                    return inst
                S.add("pe", fn, reads=["xin", "ident"], writes=[("bank", b)], tag="xT", npe=4)
                eng = "act" if c % 2 == 0 else "dve"
                if eng == "act":
                    S.add("act", lambda e, b=b, c=c: e.activation(out=h[:, c, :], in_=ps[:, b, :], func=AF.Copy),
                          reads=[("bank", b)], writes=[("h", c)])
                else:
                    S.add("dve", lambda e, b=b, c=c: e.tensor_copy(out=h[:, c, :], in_=ps[:, b, :]),
                          reads=[("bank", b)], writes=[("h", c)])
                lag.push(c)
            lag.flush()

        def store_out(i, raw):
            if not raw:
                NA.finish_rstd()
            gf = L * 64
            for c in range(NCH):
                if raw:
                    S.add("dve", lambda e, c=c: e.tensor_copy(out=co32(c), in_=h[:, c, :]),
                          reads=[("h", c)], writes=R_co(c))
                else:
                    S.add("dve", lambda e, c=c: e.scalar_tensor_tensor(
                        out=co32(c), in0=h[:, c, :], scalar=vecs[:, gf + c:gf + c + 1],
                        in1=(rstd[:] if EARLY else ps[:, 7, :]), op0=ALU.mult, op1=ALU.mult),
                        reads=[("h", c), ("rstd" if EARLY else ("bank", 7)), "vecs"], writes=R_co(c))
            for s in range(4):
                for half in range(2):
                    b = next_bank()

                    def fn(e, b=b, s=s, half=half):
                        inst = None
                        for cc in range(4):
                            c = half * 4 + cc
                            inst = e.transpose(ps[:, b, cc * 128:(cc + 1) * 128],
                                               hid32[:, c * T + s * 128: c * T + (s + 1) * 128], ident[:])
                        return inst
                    S.add("pe", fn, reads=[r for c in range(half * 4, half * 4 + 4) for r in R_co(c)] + ["ident"],
                          writes=[("bank", b)], tag="outT", npe=4)
                    if half == 0:
                        S.add("act", lambda e, b=b, s=s: e.activation(out=Y[:, s, 0:512], in_=ps[:, b, :], func=AF.Copy),
                              reads=[("bank", b)], writes=[("Y", s)])
                    else:
                        S.add("dve", lambda e, b=b, s=s: e.tensor_copy(out=Y[:, s, 512:1024], in_=ps[:, b, :]),
                              reads=[("bank", b)], writes=[("Y", s)])

            def fn(e, i=i):
                return e.dma_start(out=out_d[i * T:(i + 1) * T, :].rearrange("(s p) f -> p s f", p=128),
                                   in_=Y[:]).then_inc(out_sem, 16)
            S.add("act", fn, reads=[("Y", s) for s in range(4)], dma_sem=out_sem)

        stages_all = ["ffn1", "mix", "ffn2", "ple"]
        load_x(0)
        load_p(0, 0)
        for i in range(NT):
            x_to_h(i)
            if i + 1 < NT:
                load_x(i + 1)
            stopped = False
            for l in range(L):
                ffn(i, l, 1, vcol(l, V_FFN1, 0), vcol(l, V_MIX, 0))
                if stop_after == (l, "ffn1"):
                    stopped = True
                    break
                mixer(i, l)
                if stop_after == (l, "mix"):
                    stopped = True
                    break
                ffn(i, l, 2, vcol(l, V_FFN2, 0), vcol(l, V_PLE, 0))
                if stop_after == (l, "ffn2"):
                    stopped = True
                    break
                if l + 1 < L:
                    ple(i, l, vcol(l + 1, V_FFN1, 0), True)
                else:
                    ple(i, l, L * 64, False)
                if l + 1 < L:
                    load_p(i, l + 1)
                elif i + 1 < NT:
                    load_p(i + 1, 0)
                if stop_after == (l, "ple"):
                    stopped = True
                    break
            if stopped:
                while ws["pos"] < (i + 1) * L * NU:
                    _, sl, su = stream[ws["pos"]]
                    UC.acquire(i, sl, UNITS[su][0], UNITS[su][1])
                    UC.release(UNITS[su][0], UNITS[su][1])
                store_out(i, raw=True)
            else:
                store_out(i, raw=False)
        S.add("act", lambda e: None, writes=[("Y", s) for s in range(4)])

        S.assign_tokens(eng_sems)
        nc._pe_log = S.pe_log
        with nc.Block() as block:
            @block.tensor
            def _(e):
                S.emit_engine("pe", e, eng_sems)

            @block.scalar
            def _(e):
                S.emit_engine("act", e, eng_sems)

            @block.vector
            def _(e):
                S.emit_engine("dve", e, eng_sems)

            @block.gpsimd
            def _(e):
                S.emit_engine("pool", e, eng_sems)

            @block.sync
            def _(e):
                S.emit_engine("sp", e, eng_sems)
    return nc


def _run(inputs, NT, stop_after=None, trace=False):
    inp = {k: np.asarray(v) for k, v in inputs.items()}
    S_LOC = NT * T
    wsrc = host_arrange_weights(inp)
    vecs, wtap = host_arrange_vecs(inp)
    ident, invc = host_consts()
    nc = build_program(NT, stop_after=stop_after)
    in_maps = []
    for c in range(NCORES):
        in_maps.append({
            "x": np.ascontiguousarray(inp["x"][c, :S_LOC, :]),
            "p": np.ascontiguousarray(inp["p"][:, c, :S_LOC, :]),
            "wsrc": wsrc, "vecs": vecs, "wtap": wtap, "ident": ident, "invc": invc,
        })
    res = run_bass_kernel_spmd(nc, in_maps, core_ids=list(range(NCORES)), **({"trace": True} if trace else {}))
    out = np.stack([res.results[c]["out"] for c in range(NCORES)], axis=0)
    return out.astype(np.float32, copy=False), res


def kernel(**inputs):
    out, _ = _run(inputs, SEQ // T)
    return out
```

```python
import numpy as np
import concourse.bass as bass
import concourse.mybir as mybir
from concourse.bass_utils import run_bass_kernel_spmd

F32 = mybir.dt.float32
BF16 = mybir.dt.bfloat16
AF = mybir.ActivationFunctionType
ALU = mybir.AluOpType

D = 1024
DFF = 2816
NCH = 8
NJ = 22
T = 512
L = 2
PLE = 256
KC = 31
HZ = 16
HC = 30
SEQ = 8192
NCORES = 8
NSRC = 56
NU = 64
UW = 4096
RING = 5
EARLY = True
RMS_EPS = 1e-6
LN_EPS = 1e-5
NVEC = L * 64 + 8

V_FFN1, V_MIX, V_PSCALE, V_CB, V_LNG, V_LNB, V_FFN2, V_PLE = range(8)


def vcol(l, v, c):
    return (l * 8 + v) * 8 + c


def unit_table():
    units = []
    src = 0
    for jp in range(11):
        units.append(("gu1", jp, 4096, src)); src += 1
    for m in range(8):
        units.append(("dn1", m, NJ * 128, src)); src += 1
    for q in range(4):
        units.append(("glu", q, 4096, src)); src += 1
    for q in range(2):
        units.append(("zp", q, 4096, src)); src += 1
    for c in range(8):
        units.append(("cv", c, KC * 128, None))
    for q in range(2):
        units.append(("gp", q, 4096, src)); src += 1
    units.append(("pl", 0, 2048, src)); src += 1
    units.append(("gc", 0, 4096, src)); src += 1
    units.append(("co", 0, 4096, src)); src += 1
    units.append(("gc", 1, 4096, src)); src += 1
    units.append(("co", 1, 4096, src)); src += 1
    for q in range(2):
        units.append(("wo", q, 4096, src)); src += 1
    for jp in range(11):
        units.append(("gu2", jp, 4096, src)); src += 1
    for m in range(8):
        units.append(("dn2", m, NJ * 128, src)); src += 1
    units.append(("pp", 0, 2048, src)); src += 1
    for q in range(2):
        units.append(("pg", q, 4096, src)); src += 1
    assert len(units) == NU and src == NSRC
    return units


UNITS = unit_table()


def _colblock(W, n):
    K = W.shape[0]
    blk = W[:, n * 128:(n + 1) * 128].reshape(K // 128, 128, 128)
    return np.ascontiguousarray(blk.transpose(1, 0, 2)).reshape(128, (K // 128) * 128)


def host_arrange_weights(inp):
    wsrc = np.zeros((L * NSRC, 128, UW), np.float32)
    for l in range(L):
        win = inp["w_in"][l]
        for (kind, arg, ncols, src) in UNITS:
            if src is None:
                continue
            dst = wsrc[l * NSRC + src]
            if kind in ("gu1", "gu2"):
                wg = inp["ffn1_w_gate" if kind == "gu1" else "ffn2_w_gate"][l]
                wu = inp["ffn1_w_up" if kind == "gu1" else "ffn2_w_up"][l]
                for jj in range(2):
                    j = 2 * arg + jj
                    dst[:, (jj * 2 + 0) * 1024:(jj * 2 + 1) * 1024] = _colblock(wg, j)
                    dst[:, (jj * 2 + 1) * 1024:(jj * 2 + 2) * 1024] = _colblock(wu, j)
            elif kind in ("dn1", "dn2"):
                wd = inp["ffn1_w_down" if kind == "dn1" else "ffn2_w_down"][l]
                dst[:, :NJ * 128] = _colblock(wd, arg)
            elif kind == "glu":
                chunks = []
                for mm in (2 * arg, 2 * arg + 1):
                    chunks += [8 + mm, 16 + mm]
                for i, ch in enumerate(chunks):
                    dst[:, i * 1024:(i + 1) * 1024] = _colblock(win, ch)
            elif kind in ("zp", "gp", "gc"):
                base = {"zp": 0, "gp": 24, "gc": 32}[kind]
                for i in range(4):
                    dst[:, i * 1024:(i + 1) * 1024] = _colblock(win, base + 4 * arg + i)
            elif kind == "pl":
                pw = inp["pool_w"][l]
                for g in range(4):
                    for oc in range(2):
                        for kc in range(2):
                            pos = ((g * 2 + oc) * 2 + kc) * 128
                            dst[:, pos:pos + 128] = pw[g, kc * 128:(kc + 1) * 128, oc * 128:(oc + 1) * 128]
            elif kind in ("co", "wo", "pg"):
                W = inp[{"co": "conv_w_out", "wo": "w_out", "pg": "ple_w_gate"}[kind]][l]
                for i in range(4):
                    dst[:, i * 1024:(i + 1) * 1024] = _colblock(W, 4 * arg + i)
            elif kind == "pp":
                W = inp["ple_w_proj"][l]
                for n in range(8):
                    dst[:, n * 256:(n + 1) * 256] = _colblock(W, n)
            else:
                raise AssertionError(kind)
    return wsrc


def host_arrange_vecs(inp):
    vecs = np.zeros((128, NVEC), np.float32)
    names = ["ffn1_norm", "mix_norm", "pool_scale", "conv_dw_b", "conv_ln_g", "conv_ln_b",
             "ffn2_norm", "ple_norm"]
    for l in range(L):
        for v, nm in enumerate(names):
            vecs[:, vcol(l, v, 0):vcol(l, v, 0) + 8] = inp[nm][l].reshape(8, 128).T
    vecs[:, L * 64:L * 64 + 8] = inp["final_norm"].reshape(8, 128).T
    wtap = np.zeros((128, L * 8 * KC), np.float32)
    for l in range(L):
        w = inp["conv_dw_w"][l].reshape(KC, 8, 128)
        wtap[:, l * 8 * KC:(l + 1) * 8 * KC] = w.transpose(2, 1, 0).reshape(128, 8 * KC)
    return vecs, wtap


def host_consts():
    ident = np.eye(128, dtype=np.float32)
    invc = np.zeros((128, 4 * 16), np.float32)
    for g, w in enumerate((2, 4, 8, 16)):
        for t in range(16):
            invc[:, g * 16 + t] = np.float32(1.0) / np.float32(min(t + 1, w))
    return ident, invc


class _Op:
    __slots__ = ("eng", "fn", "deps", "is_target", "token", "dma")

    def __init__(self, eng, fn, dma):
        self.eng = eng
        self.fn = fn
        self.deps = set()
        self.is_target = False
        self.token = None
        self.dma = dma


class Sched:
    ENGS = ("pe", "act", "dve", "pool", "sp")

    def __init__(self):
        self.ops = []
        self.eng_ops = {e: [] for e in self.ENGS}
        self.last_w = {}
        self.readers = {}
        self.dma_count = {}
        self.pe_log = []
        self.cur_tag = ""

    def add(self, eng, fn, reads=(), writes=(), dma_sem=None, tag=None, npe=1):
        idx = len(self.ops)
        if eng == "pe":
            self.pe_log.append((tag or self.cur_tag, npe))
        dma = None
        if dma_sem is not None:
            n = self.dma_count.get(id(dma_sem), 0) + 1
            self.dma_count[id(dma_sem)] = n
            dma = (dma_sem, 16 * n)
        op = _Op(eng, fn, dma)
        deps = op.deps
        for r in reads:
            w = self.last_w.get(r)
            if w is not None:
                deps.add(w)
            if isinstance(r, tuple) and r[0] == "bank":
                rd = self.readers.get(r)
                if rd:
                    deps.update(v for k_, v in rd.items() if k_ != eng)
        for r in writes:
            w = self.last_w.get(r)
            if w is not None:
                deps.add(w)
            rd = self.readers.get(r)
            if rd:
                deps.update(rd.values())
        keep = set()
        for d in deps:
            dop = self.ops[d]
            if dop.dma is None and dop.eng == eng and eng in ("pe", "sp"):
                continue
            keep.add(d)
            dop.is_target = True
        op.deps = keep
        key = eng if dma is None else ("dma", idx)
        for r in reads:
            self.readers.setdefault(r, {})[key] = idx
        for r in writes:
            self.last_w[r] = idx
            self.readers[r] = {}
        self.ops.append(op)
        self.eng_ops[eng].append(idx)
        return idx

    def assign_tokens(self, eng_sems):
        for e in self.ENGS:
            cnt = 0
            for i in self.eng_ops[e]:
                op = self.ops[i]
                if op.dma is not None:
                    op.token = op.dma
                elif op.is_target:
                    cnt += 1
                    op.token = (eng_sems[e], cnt)

    def emit_engine(self, eng, e, eng_sems):
        waited = {}
        for i in self.eng_ops[eng]:
            op = self.ops[i]
            need = {}
            for d in op.deps:
                sem, val = self.ops[d].token
                k = id(sem)
                if waited.get(k, 0) >= val:
                    continue
                if k not in need or need[k][1] < val:
                    need[k] = (sem, val)
            for k, (sem, val) in need.items():
                e.wait_ge(sem, val)
                waited[k] = val
            inst = op.fn(e)
            if op.dma is None and op.is_target:
                assert inst is not None
                inst.then_inc(eng_sems[eng], 1)


def build_program(NT, stop_after=None):
    from contextlib import ExitStack
    nc = bass.Bass("TRN2", target_bir_lowering=False)
    S_LOC = NT * T
    x_d = nc.dram_tensor("x", [S_LOC, D], F32, kind="ExternalInput").ap()
    p_d = nc.dram_tensor("p", [L, S_LOC, PLE], F32, kind="ExternalInput").ap()
    wsrc_d = nc.dram_tensor("wsrc", [L * NSRC, 128, UW], F32, kind="ExternalInput").ap()
    vecs_d = nc.dram_tensor("vecs", [128, NVEC], F32, kind="ExternalInput").ap()
    wtap_d = nc.dram_tensor("wtap", [128, L * 8 * KC], F32, kind="ExternalInput").ap()
    ident_d = nc.dram_tensor("ident", [128, 128], F32, kind="ExternalInput").ap()
    invc_d = nc.dram_tensor("invc", [128, 64], F32, kind="ExternalInput").ap()
    wbf_d = nc.dram_tensor("wbf", [L * NU, 128, UW], BF16, kind="Internal").ap()
    out_d = nc.dram_tensor("out", [S_LOC, D], F32, kind="ExternalOutput").ap()

    S = Sched()
    with ExitStack() as es:
        def sb(name, shape, dt):
            return es.enter_context(nc.sbuf_tensor(name + "_sb", shape, dt))

        def sem(name):
            return es.enter_context(nc.semaphore(name))

        h = sb("h", [128, NCH, T], F32)
        xn = sb("xn", [128, NCH, T], BF16)
        hid = sb("hid", [128, NJ * T], BF16)
        sq = sb("sq", [128, 4, T], BF16)
        st4 = sb("st4", [128, 8], F32)
        r4 = sb("r4", [128, 8], F32)
        mu4 = sb("mu4", [128, 8], F32)
        dg = sb("dg", [128, 2, T], F32)
        onesf = sb("onesf", [128, 128], F32)
        rstd = sb("rstd", [128, T], F32)
        mu = sb("mu", [128, T], F32)
        mhalf = sb("mhalf", [128, 8], F32)
        tmpA = sb("tmpA", [128, 4, T + HZ], F32)
        tmpB = sb("tmpB", [128, 4, T], F32)
        zp = sb("zp", [128, NCH, T + HZ], F32)
        pooled = sb("pooled", [128, NCH, T], BF16)
        cbf = sb("cbf", [128, NCH, T + HC], BF16)
        lnout = sb("lnout", [128, NCH, T], BF16)
        Y = sb("Y", [128, 4, D], F32)
        xin = sb("xin", [128, 4, D], F32)
        pin = sb("pin", [128, 2, 4, PLE], F32)
        pT = sb("pT", [128, 2, T], BF16)
        ring = sb("ring", [128, RING, UW], BF16)
        halo_z = sb("halo_z", [128, L * NCH, HZ], F32)
        halo_c = sb("halo_c", [128, L * NCH, HC], BF16)
        vecs = sb("vecs", [128, NVEC], F32)
        wtap = sb("wtap", [128, L * 8 * KC], F32)
        ident = sb("ident", [128, 128], F32)
        invc = sb("invc", [128, 64], F32)
        ones = sb("ones", [128, 128], BF16)
        ps = es.enter_context(nc.psum_tensor("ps", [128, 8, T], F32))

        hid32 = hid[:].bitcast(F32)
        Ybf = Y.bitcast(BF16)

        def cbf2(c, a, b):
            off = (c % 2) * 1024
            return Ybf[:, c // 2, off + a: off + b]

        eng_sems = {e: sem("m_" + e) for e in Sched.ENGS}
        slot_sem = [sem(f"slot{i}") for i in range(RING)]
        xin_sem = sem("xin")
        pin_sem = [sem("pin0"), sem("pin1")]
        out_sem = sem("outst")

        def R_hid(j0, j1):
            return [("hid", j) for j in range(j0, j1)]

        def co32(c):
            return hid32[:, c * T:(c + 1) * T]

        def R_co(c):
            return [("hid", 2 * c), ("hid", 2 * c + 1)]

        def ma(c):
            return Y[:, c // 2, (c % 2) * T:(c % 2 + 1) * T]

        def R_ma(c):
            return [("Y", c // 2)]

        bank_rr = [0]

        def next_bank():
            b = bank_rr[0]
            bank_rr[0] = (b + 1) % 6
            return b

        const_list = [(vecs[:], vecs_d, "vecs"), (wtap[:], wtap_d, "wtap"),
                      (ident[:], ident_d, "ident"), (invc[:], invc_d, "invc")]
        for dst, src, res in const_list:
            csem_ = sem("c_" + res)

            def fnc(e, dst=dst, src=src, csem_=csem_):
                return e.dma_start(out=dst, in_=src).then_inc(csem_, 16)
            S.add("sp", fnc, writes=[res], dma_sem=csem_)
        S.add("dve", lambda e: e.memset(ones[:], 1.0), writes=["ones"])
        S.add("dve", lambda e: e.memset(onesf[:], 1.0), writes=["onesf"])
        S.add("dve", lambda e: e.memset(mhalf[:], -0.5), writes=["mhalf"])
        S.add("dve", lambda e: e.memset(halo_z[:], 0.0), writes=[("halo_z", i) for i in range(L * NCH)])
        S.add("dve", lambda e: e.memset(halo_c[:], 0.0), writes=[("halo_c", i) for i in range(L * NCH)])
        CONSTS = ["ones", "vecs", "wtap", "ident", "invc"]

        cast_sems = [sem(f"cast{i_}") for i_ in range(8)]
        cast_list = [(l_, pos) for l_ in range(L) for pos in range(NU) if UNITS[pos][3] is not None]
        cst = {"next": 0}

        def emit_cast(n):
            for _ in range(n):
                k = cst["next"]
                if k >= len(cast_list):
                    return
                cst["next"] = k + 1
                l_, pos = cast_list[k]
                srci = UNITS[pos][3]
                csem = cast_sems[k % 8]
                src_ap = wsrc_d[l_ * NSRC + srci:l_ * NSRC + srci + 1]
                dst_ap = wbf_d[l_ * NU + pos:l_ * NU + pos + 1]

                def fn(e, src_ap=src_ap, dst_ap=dst_ap, csem=csem):
                    return e.dma_start(out=dst_ap, in_=src_ap, max_dma_last_dim=4096).then_inc(csem, 16)
                S.add("pool", fn, writes=[("wbf", l_, pos), ("castsem", k % 8)], dma_sem=csem)

        cast_idx = {lp: k_ for k_, lp in enumerate(cast_list)}
        emit_cast(8)

        zpf = zp[:].rearrange("p a b -> p (a b)").bitcast(BF16)
        Yf = Ybf[:].rearrange("p a b -> p (a b)")
        cbff = cbf[:].rearrange("p a b -> p (a b)")
        poolf = pooled[:].rearrange("p a b -> p (a b)")
        stg = [
            (zpf, 0, [("zp", m_) for m_ in range(4)]),
            (zpf, 4224, [("zp", m_) for m_ in range(4, 8)]),
            (Yf, 0, [("Y", 0), ("Y", 1)]),
            (Yf, 4096, [("Y", 2), ("Y", 3)]),
            (cbff, 0, [("cbf", m_) for m_ in range(8)]),
            (poolf, 0, [("pooled", m_) for m_ in range(8)]),
        ]
        dg_sem = [sem(f"dg{i_}") for i_ in range(6)]
        bcnt = {"dve": 0, "act": 0}
        nbuild = 0
        for l in range(L):
            for c in range(8):
                eng = "act" if nbuild % 8 in (1, 4, 6) else "dve"
                nbuild += 1
                sidx = (0 if eng == "dve" else 3) + bcnt[eng] % 3
                bcnt[eng] += 1
                buf, off, res = stg[sidx]
                u = 25 + c

                def fn(e, l=l, c=c, buf=buf, off=off, eng=eng):
                    inst = None
                    for k in range(KC):
                        col = (l * 8 + c) * KC + k
                        o = buf[:, off + k * 128: off + (k + 1) * 128]
                        if eng == "dve":
                            inst = e.tensor_scalar(out=o, in0=ident[:], scalar1=wtap[:, col:col + 1],
                                                   scalar2=None, op0=ALU.mult)
                        else:
                            inst = e.activation(out=o, in_=ident[:], func=AF.Copy,
                                                scale=wtap[:, col:col + 1])
                    return inst
                S.add(eng, fn, reads=["ident", "wtap"], writes=res)

                def fn2(e, l=l, u=u, buf=buf, off=off, sidx=sidx):
                    return e.dma_start(out=wbf_d[l * NU + u, :, 0:KC * 128],
                                       in_=buf[:, off:off + KC * 128]).then_inc(dg_sem[sidx], 16)
                S.add("act" if eng == "act" else "pool", fn2, reads=res, writes=[("wbf", l, u)],
                      dma_sem=dg_sem[sidx])

        stream = []
        for i in range(NT):
            for l in range(L):
                for u in range(NU):
                    stream.append((i, l, u))
        ws = {"next_load": 0, "pos": 0}

        def issue_load():
            s = ws["next_load"]
            if s >= len(stream):
                return
            ws["next_load"] = s + 1
            _, l, u = stream[s]
            slot = s % RING
            ncols = UNITS[u][2]

            def fn(e, l=l, u=u, slot=slot, ncols=ncols):
                return e.dma_start(out=ring[:, slot, 0:ncols],
                                   in_=wbf_d[l * NU + u, :, 0:ncols]).then_inc(slot_sem[slot], 16)
            rds = [("wbf", l, u)]
            if stream[s][0] == 0 and (l, u) in cast_idx:
                rds.append(("castsem", cast_idx[(l, u)] % 8))
            S.add("sp", fn, reads=rds, writes=[("slot", slot)], dma_sem=slot_sem[slot])

        class UnitCursor:
            def __init__(self):
                self.released = set()
                self.cur = {}

            def acquire(self, i, l, kind, arg):
                if i == 0:
                    emit_cast(1)
                s = ws["pos"]
                ti, tl, tu = stream[s]
                assert (ti, tl) == (i, l) and UNITS[tu][0] == kind and UNITS[tu][1] == arg, \
                    (stream[s], UNITS[tu], i, l, kind, arg)
                ws["pos"] = s + 1
                self.cur[(kind, arg)] = s
                return s % RING

            def release(self, kind, arg):
                s = self.cur.pop((kind, arg))
                self.released.add(s)
                while (ws["next_load"] - RING) in self.released:
                    self.released.discard(ws["next_load"] - RING)
                    if ws["next_load"] >= len(stream):
                        break
                    issue_load()

        UC = UnitCursor()
        for _ in range(RING):
            issue_load()

        def mm_group(bank, pairs, reads, extra_writes=()):
            n = len(pairs)

            def fn(e, pairs=pairs, bank=bank, n=n):
                inst = None
                for k, (lt, rh) in enumerate(pairs):
                    inst = e.matmul(ps[:, bank, :], lhsT=lt, rhs=rh, start=(k == 0), stop=(k == n - 1))
                return inst
            S.add("pe", fn, reads=reads, writes=[("bank", bank)] + list(extra_writes), npe=n)

        def stat_mms(k, col0, first, last):
            def fn(e, k=k, col0=col0, first=first, last=last):
                inst = None
                for s_ in range(4):
                    inst = e.matmul(ps[:, 6, col0 + s_:col0 + s_ + 1], lhsT=sq[:, k, s_ * 128:(s_ + 1) * 128],
                                    rhs=ones[:, 0:1], start=(first and s_ == 0), stop=(last and s_ == 3),
                                    skip_group_check=True)
                return inst
            S.add("pe", fn, reads=[("sq", k), "ones"], writes=[("bank", 6)], tag="stat", npe=4)

        def bcast(src4, col, dsel, bank):
            def fn(e, src4=src4, col=col, dsel=dsel):
                inst = None
                for s_ in range(4):
                    inst = e.tensor_scalar(out=dg[:, dsel, s_ * 128:(s_ + 1) * 128], in0=ident[:],
                                           scalar1=src4[:, col + s_:col + s_ + 1], scalar2=None, op0=ALU.mult)
                return inst
            S.add("dve", fn, reads=["ident", ("small", id(src4))], writes=[("dg", dsel)])
            S.add("pe", lambda e, dsel=dsel, bank=bank: e.matmul(ps[:, bank, :], lhsT=onesf[:], rhs=dg[:, dsel, :],
                                                                 start=True, stop=True),
                  reads=[("dg", dsel), "onesf"], writes=[("bank", bank)], tag="bcast")

        class NormAcc:
            def __init__(self):
                self.n = 0
                self.k = 0
                self.gbase = None
                self.want_xg = True

            def start(self, gbase, want_xg=True, buf="A"):
                self.n = 0
                self.gbase = gbase
                self.want_xg = want_xg
                self.buf = buf

            def chunk_act(self, c):
                k = self.k
                self.k = (k + 1) % 4
                S.add("act", lambda e, c=c, k=k: e.activation(out=sq[:, k, :], in_=h[:, c, :], func=AF.Square),
                      reads=[("h", c)], writes=[("sq", k)])
                if self.want_xg and EARLY:
                    gb = self.gbase
                    if self.buf == "F":
                        S.add("act", lambda e, c=c, gb=gb: e.activation(out=co32(c), in_=h[:, c, :], func=AF.Copy,
                                                                        scale=vecs[:, gb + c:gb + c + 1]),
                              reads=[("h", c), "vecs"], writes=R_co(c))
                    else:
                        dst = lnout if self.buf == "A" else xn
                        rn = "lnout" if self.buf == "A" else "xn"
                        S.add("act", lambda e, c=c, gb=gb, dst=dst: e.activation(out=dst[:, c, :], in_=h[:, c, :],
                                                                                 func=AF.Copy,
                                                                                 scale=vecs[:, gb + c:gb + c + 1]),
                              reads=[("h", c), "vecs"], writes=[(rn, c)])
                return k

            def chunk_pe(self, k):
                stat_mms(k, 0, self.n == 0, self.n == NCH - 1)
                self.n += 1

            def finish_rstd(self):
                S.add("act", lambda e: e.activation(out=st4[:, 0:4], in_=ps[:, 6, 0:4], func=AF.Identity,
                                                    scale=1.0 / D, bias=epsr[:, 0:1]),
                      reads=[("bank", 6), "eps"], writes=[("small", id(st4))])
                S.add("pool", lambda e: e.tensor_tensor(out=r4[:, 0:4], in0=st4[:, 0:4], in1=mhalf[:, 0:4], op=ALU.pow),
                      reads=[("small", id(st4)), "mhalf"], writes=[("small", id(r4))])
                bcast(r4, 0, 0, 7)
                if EARLY:
                    S.add("act", lambda e: e.activation(out=rstd[:], in_=ps[:, 7, :], func=AF.Copy),
                          reads=[("bank", 7)], writes=["rstd"])

            def xn_ops(self, gbase, c0=0, c1=NCH):
                for c in range(c0, c1):
                    if EARLY:
                        S.add("dve", lambda e, c=c: e.scalar_tensor_tensor(
                            out=xn[:, c, :], in0=h[:, c, :], scalar=vecs[:, gbase + c:gbase + c + 1],
                            in1=rstd[:], op0=ALU.mult, op1=ALU.mult),
                            reads=[("h", c), "rstd", "vecs"], writes=[("xn", c)])
                    else:
                        S.add("dve", lambda e, c=c: e.scalar_tensor_tensor(
                            out=xn[:, c, :], in0=h[:, c, :], scalar=vecs[:, gbase + c:gbase + c + 1],
                            in1=ps[:, 7, :], op0=ALU.mult, op1=ALU.mult),
                            reads=[("h", c), ("bank", 7), "vecs"], writes=[("xn", c)])

        def run_boundary(n_items, mm_fn, evac_fn, xn_gbase=None):
            pend = {}
            pend[0] = mm_fn(0, True)
            pend[1] = mm_fn(1, True)
            NA.finish_rstd()
            pend[2] = mm_fn(2, True)
            evac_fn(0, pend[0], True)
            xq = 0
            for q in range(3, n_items):
                pend[q] = mm_fn(q, True)
                evac_fn(q - 2, pend[q - 2], True)
                if xn_gbase is not None and xq < NCH:
                    NA.xn_ops(xn_gbase, xq, min(NCH, xq + 2))
                    xq += 2
            evac_fn(n_items - 2, pend[n_items - 2], True)
            evac_fn(n_items - 1, pend[n_items - 1], True)
            if xn_gbase is not None and xq < NCH:
                NA.xn_ops(xn_gbase, xq, NCH)

        eps_t = sb("eps_t", [128, 2], F32)
        epsr = eps_t
        S.add("dve", lambda e: e.memset(eps_t[:, 0:1], RMS_EPS), writes=["eps"])
        S.add("dve", lambda e: e.memset(eps_t[:, 1:2], LN_EPS), reads=["eps"], writes=["eps"])

        NA = NormAcc()
        tA = [0]
        tB = [0]

        def nextA():
            k = tA[0]
            tA[0] = (k + 1) % 4
            return k

        def nextB():
            k = tB[0]
            tB[0] = (k + 1) % 4
            return k

        class Lag:
            def __init__(self, lag):
                self.lag = lag
                self.q = []

            def push(self, c):
                self.q.append(NA.chunk_act(c))
                if len(self.q) > self.lag:
                    NA.chunk_pe(self.q.pop(0))

            def flush(self):
                while self.q:
                    NA.chunk_pe(self.q.pop(0))

        def ffn(i, l, which, gbase, next_gbase):
            S.cur_tag = f"t{i}l{l}ffn{which}"
            gu = "gu1" if which == 1 else "gu2"
            dn = "dn1" if which == 1 else "dn2"
            st = {"slot": None}

            def mm_j(j, early):
                jp, jj = divmod(j, 2)
                if jj == 0:
                    st["slot"] = UC.acquire(i, l, gu, jp)
                slot = st["slot"]
                bg = next_bank()
                bu = next_bank()
                src = xn
                rn = "xn"
                for (bank, sel) in ((bg, 0), (bu, 1)):
                    base = (jj * 2 + sel) * 1024
                    pairs = [(ring[:, slot, base + k * 128: base + (k + 1) * 128], src[:, k, :])
                             for k in range(NCH)]
                    mm_group(bank, pairs, reads=[("slot", slot)] + [(rn, k) for k in range(NCH)])
                if jj == 1:
                    UC.release(gu, jp)
                return (bg, bu)

            def evac_j(j, banks, early):
                bg, bu = banks
                ka = nextA()
                if early:
                    k1 = nextB()
                    S.add("dve", lambda e, bg=bg, k1=k1: e.tensor_tensor(out=tmpB[:, k1, :], in0=ps[:, bg, :],
                                                                         in1=rstd[:], op=ALU.mult),
                          reads=[("bank", bg), "rstd"], writes=[("tmpB", k1)])
                    S.add("act", lambda e, k1=k1, ka=ka: e.activation(out=tmpA[:, ka, 0:T], in_=tmpB[:, k1, :],
                                                                     func=AF.Silu),
                          reads=[("tmpB", k1)], writes=[("tmpA", ka)])
                    k2 = nextB()
                    S.add("dve", lambda e, bu=bu, k2=k2: e.tensor_tensor(out=tmpB[:, k2, :], in0=ps[:, bu, :],
                                                                         in1=rstd[:], op=ALU.mult),
                          reads=[("bank", bu), "rstd"], writes=[("tmpB", k2)])
                    S.add("dve", lambda e, ka=ka, k2=k2, j=j: e.tensor_tensor(
                        out=hid[:, j * T:(j + 1) * T], in0=tmpA[:, ka, 0:T], in1=tmpB[:, k2, :], op=ALU.mult),
                        reads=[("tmpA", ka), ("tmpB", k2)], writes=[("hid", j)])
                else:
                    S.add("act", lambda e, bg=bg, ka=ka: e.activation(out=tmpA[:, ka, 0:T], in_=ps[:, bg, :],
                                                                     func=AF.Silu),
                          reads=[("bank", bg)], writes=[("tmpA", ka)])
                    S.add("dve", lambda e, bu=bu, ka=ka, j=j: e.tensor_tensor(
                        out=hid[:, j * T:(j + 1) * T], in0=tmpA[:, ka, 0:T], in1=ps[:, bu, :], op=ALU.mult),
                        reads=[("tmpA", ka), ("bank", bu)], writes=[("hid", j)])

            run_boundary(NJ, mm_j, evac_j)

            NA.start(next_gbase, True, "A")
            lag = Lag(1)
            for m in range(NCH):
                slot = UC.acquire(i, l, dn, m)
                b = next_bank()
                pairs = [(ring[:, slot, j * 128:(j + 1) * 128], hid[:, j * T:(j + 1) * T]) for j in range(NJ)]
                mm_group(b, pairs, reads=[("slot", slot)] + R_hid(0, NJ))
                UC.release(dn, m)
                S.add("dve", lambda e, b=b, m=m: e.scalar_tensor_tensor(
                    out=h[:, m, :], in0=ps[:, b, :], scalar=0.5, in1=h[:, m, :], op0=ALU.mult, op1=ALU.add),
                    reads=[("bank", b), ("h", m)], writes=[("h", m)])
                lag.push(m)
            lag.flush()

        def mixer(i, l):
            first_tile = (i == 0)
            S.cur_tag = f"t{i}l{l}mix"
            for m in range(NCH):
                S.add("pool", lambda e, m=m: e.tensor_copy(out=zp[:, m, 0:HZ], in_=halo_z[:, l * NCH + m, :]),
                      reads=[("halo_z", l * NCH + m)], writes=[("zp", m)])
                S.add("pool", lambda e, m=m: e.tensor_copy(out=cbf[:, m, 0:HC], in_=halo_c[:, l * NCH + m, :]),
                      reads=[("halo_c", l * NCH + m)], writes=[("cbf", m)])

            def zpool_chunk(m, slot):
                b = next_bank()
                base = (m % 4) * 1024
                pairs = [(ring[:, slot, base + k * 128: base + (k + 1) * 128], xn[:, k, :]) for k in range(NCH)]
                mm_group(b, pairs, reads=[("slot", slot)] + [("xn", k) for k in range(NCH)])
                S.add("act", lambda e, b=b, m=m: e.activation(out=zp[:, m, HZ:HZ + T], in_=ps[:, b, :], func=AF.Copy),
                      reads=[("bank", b)], writes=[("zp", m)])
                S.add("pool", lambda e, m=m: e.tensor_copy(out=halo_z[:, l * NCH + m, :], in_=zp[:, m, T:T + HZ]),
                      reads=[("zp", m)], writes=[("halo_z", l * NCH + m)])

            def pool_sums(m):
                g = m // 2
                w = 2 << g
                src = zp[:, m, :]
                src_res = ("zp", m)
                lo = 0
                sh = 1
                ksrc = None
                for lev in range(g + 1):
                    kd = nextA()
                    nlo = lo + sh
                    if ksrc is None:
                        a0 = zp[:, m, nlo:T + HZ]
                        a1 = zp[:, m, nlo - sh:T + HZ - sh]
                    else:
                        a0 = tmpA[:, ksrc, nlo:T + HZ]
                        a1 = tmpA[:, ksrc, nlo - sh:T + HZ - sh]
                    S.add("pool" if g == 3 else "dve", lambda e, kd=kd, nlo=nlo, a0=a0, a1=a1: e.tensor_tensor(
                        out=tmpA[:, kd, nlo:T + HZ], in0=a0, in1=a1, op=ALU.add),
                        reads=[src_res], writes=[("tmpA", kd)])
                    src_res = ("tmpA", kd)
                    ksrc = kd
                    lo = nlo
                    sh *= 2
                S.add("dve", lambda e, ksrc=ksrc, m=m, w=w: e.scalar_tensor_tensor(
                    out=pooled[:, m, :], in0=tmpA[:, ksrc, HZ:HZ + T], scalar=1.0 / w, in1=zp[:, m, HZ:HZ + T],
                    op0=ALU.mult, op1=ALU.subtract),
                    reads=[("tmpA", ksrc), ("zp", m)], writes=[("pooled", m)])
                if first_tile:
                    kb = nextB()
                    S.add("dve", lambda e, ksrc=ksrc, kb=kb, g=g: e.tensor_tensor(
                        out=tmpB[:, kb, 0:HZ], in0=tmpA[:, ksrc, HZ:2 * HZ], in1=invc[:, g * 16:(g + 1) * 16],
                        op=ALU.mult),
                        reads=[("tmpA", ksrc), "invc"], writes=[("tmpB", kb)])
                    S.add("dve", lambda e, kb=kb, m=m: e.tensor_tensor(
                        out=pooled[:, m, 0:HZ], in0=tmpB[:, kb, 0:HZ], in1=zp[:, m, HZ:2 * HZ], op=ALU.subtract),
                        reads=[("tmpB", kb), ("zp", m)], writes=[("pooled", m)])

            gst = {"slot": None}

            def glu_mm(m, early):
                if m % 2 == 0:
                    gst["slot"] = UC.acquire(i, l, "glu", m // 2)
                slot = gst["slot"]
                ba = next_bank()
                bgk = next_bank()
                src = lnout if early else xn
                rn = "lnout" if early else "xn"
                for (bank, sel) in ((ba, 0), (bgk, 1)):
                    base = ((m % 2) * 2 + sel) * 1024
                    pairs = [(ring[:, slot, base + k * 128: base + (k + 1) * 128], src[:, k, :]) for k in range(NCH)]
                    mm_group(bank, pairs, reads=[("slot", slot)] + [(rn, k) for k in range(NCH)])
                if m % 2 == 1:
                    UC.release("glu", m // 2)
                return (ba, bgk)

            def glu_evac(m, banks, early):
                ba, bgk = banks
                kb = nextB()
                if early:
                    k0 = nextB()
                    S.add("dve", lambda e, bgk=bgk, k0=k0: e.tensor_tensor(out=tmpB[:, k0, :], in0=ps[:, bgk, :],
                                                                           in1=rstd[:], op=ALU.mult),
                          reads=[("bank", bgk), "rstd"], writes=[("tmpB", k0)])
                    S.add("act", lambda e, k0=k0, kb=kb: e.activation(out=tmpB[:, kb, :], in_=tmpB[:, k0, :],
                                                                     func=AF.Tanh, scale=0.5),
                          reads=[("tmpB", k0)], writes=[("tmpB", kb)])
                    k3 = nextB()
                    S.add("dve", lambda e, ba=ba, kb=kb, k3=k3: e.scalar_tensor_tensor(
                        out=tmpB[:, k3, :], in0=tmpB[:, kb, :], scalar=1.0, in1=ps[:, ba, :],
                        op0=ALU.add, op1=ALU.mult),
                        reads=[("tmpB", kb), ("bank", ba)], writes=[("tmpB", k3)])
                    S.add("dve", lambda e, k3=k3, m=m: e.tensor_tensor(out=cbf[:, m, HC:HC + T], in0=tmpB[:, k3, :],
                                                                       in1=rstd[:], op=ALU.mult),
                          reads=[("tmpB", k3), "rstd"], writes=[("cbf", m)])
                else:
                    S.add("act", lambda e, bgk=bgk, kb=kb: e.activation(out=tmpB[:, kb, :], in_=ps[:, bgk, :],
                                                                        func=AF.Tanh, scale=0.5),
                          reads=[("bank", bgk)], writes=[("tmpB", kb)])
                    S.add("dve", lambda e, ba=ba, kb=kb, m=m: e.scalar_tensor_tensor(
                        out=cbf[:, m, HC:HC + T], in0=tmpB[:, kb, :], scalar=1.0, in1=ps[:, ba, :],
                        op0=ALU.add, op1=ALU.mult),
                        reads=[("tmpB", kb), ("bank", ba)], writes=[("cbf", m)])
                S.add("pool", lambda e, m=m: e.tensor_copy(out=halo_c[:, l * NCH + m, :], in_=cbf[:, m, T:T + HC]),
                      reads=[("cbf", m)], writes=[("halo_c", l * NCH + m)])
                S.add("pool", lambda e, m=m: e.tensor_copy(out=cbf2(m, 0, T + HC - 1), in_=cbf[:, m, 1:T + HC]),
                      reads=[("cbf", m)], writes=[("Y", m // 2)])

            run_boundary(NCH, glu_mm, glu_evac, xn_gbase=vcol(l, V_MIX, 0))

            for m in range(NCH):
                if m % 4 == 0:
                    zslot = UC.acquire(i, l, "zp", m // 4)
                zpool_chunk(m, zslot)
                if m % 4 == 3:
                    UC.release("zp", m // 4)
                pool_sums(m)

            lnb = (6, 7)
            cbias = vcol(l, V_CB, 0)
            pend = []

            def ln_stat_act(c):
                k1 = NA.k
                NA.k = (k1 + 1) % 4
                k2 = NA.k
                NA.k = (k2 + 1) % 4
                S.add("act", lambda e, c=c, k1=k1: e.activation(out=sq[:, k1, :], in_=co32(c), func=AF.Copy),
                      reads=R_co(c), writes=[("sq", k1)])
                S.add("act", lambda e, c=c, k2=k2: e.activation(out=sq[:, k2, :], in_=co32(c), func=AF.Square),
                      reads=R_co(c), writes=[("sq", k2)])
                return (c, k1, k2)

            def ln_stat_pe(ck):
                c, k1, k2 = ck
                stat_mms(k1, 0, c == 0, False)
                stat_mms(k2, 4, False, c == NCH - 1)

            for c in range(NCH):
                slot = UC.acquire(i, l, "cv", c)
                b = next_bank()
                pairs = [(ring[:, slot, k * 128:(k + 1) * 128],
                          cbf[:, c, k:k + T] if k % 2 == 0 else cbf2(c, k - 1, k - 1 + T)) for k in range(KC)]
                mm_group(b, pairs, reads=[("slot", slot), ("cbf", c), ("Y", c // 2)])
                UC.release("cv", c)
                if pend:
                    ln_stat_pe(pend.pop(0))
                S.add("act", lambda e, b=b, c=c: e.activation(
                    out=co32(c), in_=ps[:, b, :], func=AF.Identity, scale=0.5,
                    bias=vecs[:, cbias + c:cbias + c + 1]),
                    reads=[("bank", b), "vecs"], writes=R_co(c))
                pend.append(ln_stat_act(c))
            while pend:
                ln_stat_pe(pend.pop(0))

            psc = vcol(l, V_PSCALE, 0)
            gpst = {"slot": None}

            def gp_chunk(m):
                if m % 4 == 0:
                    gpst["slot"] = UC.acquire(i, l, "gp", m // 4)
                gslot = gpst["slot"]
                b = next_bank()
                base = (m % 4) * 1024
                pairs = [(ring[:, gslot, base + k * 128: base + (k + 1) * 128], xn[:, k, :]) for k in range(NCH)]
                mm_group(b, pairs, reads=[("slot", gslot)] + [("xn", k) for k in range(NCH)])
                if m % 4 == 3:
                    UC.release("gp", m // 4)
                S.add("act", lambda e, b=b, m=m: e.activation(out=ma(m), in_=ps[:, b, :], func=AF.Tanh, scale=0.5),
                      reads=[("bank", b)], writes=R_ma(m))

            for m in range(3):
                gp_chunk(m)

            S.add("dve", lambda e: e.tensor_scalar(out=mu4[:, 0:4], in0=ps[:, 6, 0:4], scalar1=1.0 / D, scalar2=None,
                                                   op0=ALU.mult),
                  reads=[("bank", 6)], writes=[("small", id(mu4))])
            S.add("dve", lambda e: e.tensor_tensor(out=mu4[:, 4:8], in0=mu4[:, 0:4], in1=mu4[:, 0:4], op=ALU.mult),
                  reads=[("small", id(mu4))], writes=[("small", id(mu4))])
            S.add("dve", lambda e: e.scalar_tensor_tensor(
                out=st4[:, 0:4], in0=ps[:, 6, 4:8], scalar=1.0 / D, in1=mu4[:, 4:8], op0=ALU.mult, op1=ALU.subtract),
                reads=[("bank", 6), ("small", id(mu4))], writes=[("small", id(st4))])
            S.add("pool", lambda e: e.tensor_scalar(out=st4[:, 4:8], in0=st4[:, 0:4], scalar1=1.0, scalar2=LN_EPS,
                                                    op0=ALU.mult, op1=ALU.add),
                  reads=[("small", id(st4))], writes=[("small", id(st4))])
            S.add("pool", lambda e: e.tensor_tensor(out=r4[:, 0:4], in0=st4[:, 4:8], in1=mhalf[:, 0:4], op=ALU.pow),
                  reads=[("small", id(st4)), "mhalf"], writes=[("small", id(r4))])
            S.add("dve", lambda e: e.scalar_tensor_tensor(
                out=r4[:, 4:8], in0=mu4[:, 0:4], scalar=-1.0, in1=r4[:, 0:4], op0=ALU.mult, op1=ALU.mult),
                reads=[("small", id(mu4)), ("small", id(r4))], writes=[("small", id(r4))])
            bcast(r4, 0, 0, 7)
            S.add("act", lambda e: e.activation(out=rstd[:], in_=ps[:, 7, :], func=AF.Copy),
                  reads=[("bank", 7)], writes=["rstd"])
            bx = next_bank()
            bcast(r4, 4, 1, bx)
            S.add("act", lambda e, bx=bx: e.activation(out=mu[:], in_=ps[:, bx, :], func=AF.Copy),
                  reads=[("bank", bx)], writes=["mu"])

            lg = vcol(l, V_LNG, 0)
            lb = vcol(l, V_LNB, 0)

            def ln_apply_chunk(c):
                k1 = nextB()
                S.add("dve", lambda e, c=c, k1=k1: e.tensor_tensor(out=tmpB[:, k1, :], in0=co32(c), in1=rstd[:],
                                                                   op=ALU.mult),
                      reads=R_co(c) + ["rstd"], writes=[("tmpB", k1)])
                k2 = nextB()
                S.add("dve", lambda e, k1=k1, k2=k2: e.tensor_tensor(out=tmpB[:, k2, :], in0=tmpB[:, k1, :],
                                                                     in1=mu[:], op=ALU.add),
                      reads=[("tmpB", k1), "mu"], writes=[("tmpB", k2)])
                S.add("act", lambda e, c=c, k2=k2: e.activation(
                    out=lnout[:, c, :], in_=tmpB[:, k2, :], func=AF.Silu,
                    scale=vecs[:, lg + c:lg + c + 1], bias=vecs[:, lb + c:lb + c + 1]),
                    reads=[("tmpB", k2), "vecs"], writes=[("lnout", c)])

            for m in range(3, NCH):
                gp_chunk(m)
                ln_apply_chunk(m - 3)
            for c in range(NCH - 3, NCH):
                ln_apply_chunk(c)
            pslot = UC.acquire(i, l, "pl", 0)
            for m in range(NCH):
                g = m // 2
                oc = m % 2
                b = next_bank()
                pairs = [(ring[:, pslot, ((g * 2 + oc) * 2 + kc) * 128:((g * 2 + oc) * 2 + kc + 1) * 128],
                          pooled[:, 2 * g + kc, :]) for kc in range(2)]
                mm_group(b, pairs, reads=[("slot", pslot), ("pooled", 2 * g), ("pooled", 2 * g + 1)])
                kb = nextB()
                S.add("act", lambda e, b=b, kb=kb, m=m: e.activation(
                    out=tmpB[:, kb, :], in_=ps[:, b, :], func=AF.Copy, scale=vecs[:, psc + m:psc + m + 1]),
                    reads=[("bank", b), "vecs"], writes=[("tmpB", kb)])
                S.add("dve", lambda e, kb=kb, m=m: e.scalar_tensor_tensor(
                    out=ma(m), in0=ma(m), scalar=1.0, in1=tmpB[:, kb, :], op0=ALU.add, op1=ALU.mult),
                    reads=R_ma(m) + [("tmpB", kb)], writes=R_ma(m))
            UC.release("pl", 0)

            for m in range(NCH):
                if m % 4 == 0:
                    gslot = UC.acquire(i, l, "gc", m // 4)
                    cslot = UC.acquire(i, l, "co", m // 4)
                bg = next_bank()
                base = (m % 4) * 1024
                pairs = [(ring[:, gslot, base + k * 128: base + (k + 1) * 128], xn[:, k, :]) for k in range(NCH)]
                mm_group(bg, pairs, reads=[("slot", gslot)] + [("xn", k) for k in range(NCH)])
                bc = next_bank()
                pairs = [(ring[:, cslot, base + k * 128: base + (k + 1) * 128], lnout[:, k, :]) for k in range(NCH)]
                mm_group(bc, pairs, reads=[("slot", cslot)] + [("lnout", k) for k in range(NCH)])
                if m % 4 == 3:
                    UC.release("gc", m // 4)
                    UC.release("co", m // 4)
                k1 = nextB()
                S.add("act", lambda e, bg=bg, k1=k1: e.activation(out=tmpB[:, k1, :], in_=ps[:, bg, :],
                                                                  func=AF.Tanh, scale=0.5),
                      reads=[("bank", bg)], writes=[("tmpB", k1)])
                k2 = nextB()
                S.add("dve", lambda e, bc=bc, k1=k1, k2=k2: e.scalar_tensor_tensor(
                    out=tmpB[:, k2, :], in0=tmpB[:, k1, :], scalar=1.0, in1=ps[:, bc, :], op0=ALU.add, op1=ALU.mult),
                    reads=[("tmpB", k1), ("bank", bc)], writes=[("tmpB", k2)])
                S.add("dve", lambda e, k2=k2, m=m: e.tensor_tensor(out=pooled[:, m, :], in0=tmpB[:, k2, :],
                                                                   in1=ma(m), op=ALU.add),
                      reads=[("tmpB", k2)] + R_ma(m), writes=[("pooled", m)])

            NA.start(vcol(l, V_FFN2, 0), True, "B")
            lag = Lag(2)
            for m in range(NCH):
                if m % 4 == 0:
                    wslot = UC.acquire(i, l, "wo", m // 4)
                b = next_bank()
                base = (m % 4) * 1024
                pairs = [(ring[:, wslot, base + k * 128: base + (k + 1) * 128], pooled[:, k, :]) for k in range(NCH)]
                mm_group(b, pairs, reads=[("slot", wslot)] + [("pooled", k) for k in range(NCH)])
                if m % 4 == 3:
                    UC.release("wo", m // 4)
                S.add("dve", lambda e, b=b, m=m: e.scalar_tensor_tensor(
                    out=h[:, m, :], in0=ps[:, b, :], scalar=0.5, in1=h[:, m, :], op0=ALU.mult, op1=ALU.add),
                    reads=[("bank", b), ("h", m)], writes=[("h", m)])
                lag.push(m)
            lag.flush()

        def load_p(i, l):
            k = (i * L + l) % 2

            def fn(e, i=i, l=l, k=k):
                return e.dma_start(out=pin[:, k, :, :],
                                   in_=p_d[l, i * T:(i + 1) * T, :].rearrange("(s p) f -> p s f", p=128)
                                   ).then_inc(pin_sem[k], 16)
            S.add("act", fn, writes=[("pin", k)], dma_sem=pin_sem[k])

        def ple_ptrans(i, l):
            k = (i * L + l) % 2
            for kc in range(2):
                b = next_bank()

                def fn(e, b=b, kc=kc, k=k):
                    inst = None
                    for s in range(4):
                        inst = e.transpose(ps[:, b, s * 128:(s + 1) * 128], pin[:, k, s, kc * 128:(kc + 1) * 128],
                                           ident[:])
                    return inst
                S.add("pe", fn, reads=[("pin", k), "ident"], writes=[("bank", b)], npe=4, tag="pT")
                S.add("act", lambda e, b=b, kc=kc: e.activation(out=pT[:, kc, :], in_=ps[:, b, :], func=AF.Copy),
                      reads=[("bank", b)], writes=[("pT", kc)])

        def ple(i, l, next_gbase, want_xg, nbuf="B"):
            S.cur_tag = f"t{i}l{l}ple"
            ple_ptrans(i, l)
            pst = {"pslot": None, "gslot": None}
            NA.start(next_gbase, want_xg, nbuf)
            lag = Lag(2)

            def ple_mm(m, early):
                if m == 0:
                    pst["pslot"] = UC.acquire(i, l, "pp", 0)
                if m % 4 == 0:
                    pst["gslot"] = UC.acquire(i, l, "pg", m // 4)
                pslot, gslot = pst["pslot"], pst["gslot"]
                src = lnout if early else xn
                rn = "lnout" if early else "xn"
                bg = next_bank()
                base = (m % 4) * 1024
                pairs = [(ring[:, gslot, base + kk * 128: base + (kk + 1) * 128], src[:, kk, :]) for kk in range(NCH)]
                mm_group(bg, pairs, reads=[("slot", gslot)] + [(rn, kk) for kk in range(NCH)])
                bp = next_bank()
                pairs = [(ring[:, pslot, (m * 2 + kc) * 128:(m * 2 + kc + 1) * 128], pT[:, kc, :]) for kc in range(2)]
                mm_group(bp, pairs, reads=[("slot", pslot), ("pT", 0), ("pT", 1)])
                if m % 4 == 3:
                    UC.release("pg", m // 4)
                if m == NCH - 1:
                    UC.release("pp", 0)
                return (bg, bp)

            def ple_evac(m, banks, early):
                bg, bp = banks
                k1 = nextB()
                if early:
                    k0 = nextB()
                    S.add("dve", lambda e, bg=bg, k0=k0: e.tensor_tensor(out=tmpB[:, k0, :], in0=ps[:, bg, :],
                                                                         in1=rstd[:], op=ALU.mult),
                          reads=[("bank", bg), "rstd"], writes=[("tmpB", k0)])
                    S.add("act", lambda e, k0=k0, k1=k1: e.activation(out=tmpB[:, k1, :], in_=tmpB[:, k0, :],
                                                                     func=AF.Tanh, scale=0.5),
                          reads=[("tmpB", k0)], writes=[("tmpB", k1)])
                else:
                    S.add("act", lambda e, bg=bg, k1=k1: e.activation(out=tmpB[:, k1, :], in_=ps[:, bg, :],
                                                                      func=AF.Tanh, scale=0.5),
                          reads=[("bank", bg)], writes=[("tmpB", k1)])
                k2 = nextB()
                S.add("dve", lambda e, bp=bp, k1=k1, k2=k2: e.scalar_tensor_tensor(
                    out=tmpB[:, k2, :], in0=tmpB[:, k1, :], scalar=1.0, in1=ps[:, bp, :], op0=ALU.add, op1=ALU.mult),
                    reads=[("tmpB", k1), ("bank", bp)], writes=[("tmpB", k2)])
                S.add("dve", lambda e, k2=k2, m=m: e.scalar_tensor_tensor(
                    out=h[:, m, :], in0=tmpB[:, k2, :], scalar=0.5, in1=h[:, m, :], op0=ALU.mult, op1=ALU.add),
                    reads=[("tmpB", k2), ("h", m)], writes=[("h", m)])
                lag.push(m)

            run_boundary(NCH, ple_mm, ple_evac)
            lag.flush()

        def load_x(i):
            def fn(e, i=i):
                return e.dma_start(out=xin[:], in_=x_d[i * T:(i + 1) * T, :].rearrange("(s p) f -> p s f", p=128)
                                   ).then_inc(xin_sem, 16)
            S.add("act", fn, writes=["xin"], dma_sem=xin_sem)

        def x_to_h(i):
            NA.start(vcol(0, V_FFN1, 0), True, "B")
            lag = Lag(1)
            for c in range(NCH):
                b = next_bank()

                def fn(e, b=b, c=c):
                    inst = None
                    for s in range(4):
                        inst = e.transpose(ps[:, b, s * 128:(s + 1) * 128], xin[:, s, c * 128:(c + 1) * 128], ident[:])
                    return inst
                S.add("pe", fn, reads=["xin", "ident"], writes=[("bank", b)], tag="xT", npe=4)
                eng = "act" if c % 2 == 0 else "dve"
                if eng == "act":
                    S.add("act", lambda e, b=b, c=c: e.activation(out=h[:, c, :], in_=ps[:, b, :], func=AF.Copy),
                          reads=[("bank", b)], writes=[("h", c)])
                else:
                    S.add("dve", lambda e, b=b, c=c: e.tensor_copy(out=h[:, c, :], in_=ps[:, b, :]),
                          reads=[("bank", b)], writes=[("h", c)])
                lag.push(c)
            lag.flush()

        def store_out(i, raw):
            if not raw:
                S.add("act", lambda e: e.activation(out=st4[:, 0:4], in_=ps[:, 6, 0:4], func=AF.Identity,
                                                    scale=1.0 / D, bias=epsr[:, 0:1]),
                      reads=[("bank", 6), "eps"], writes=[("small", id(st4))])
                S.add("pool", lambda e: e.tensor_tensor(out=r4[:, 0:4], in0=st4[:, 0:4], in1=mhalf[:, 0:4], op=ALU.pow),
                      reads=[("small", id(st4)), "mhalf"], writes=[("small", id(r4))])
            else:
                for c in range(NCH):
                    S.add("dve", lambda e, c=c: e.tensor_copy(out=co32(c), in_=h[:, c, :]),
                          reads=[("h", c)], writes=R_co(c))
            for s in range(4):
                for half in range(2):
                    b = next_bank()

                    def fn(e, b=b, s=s, half=half):
                        inst = None
                        for cc in range(4):
                            c = half * 4 + cc
                            inst = e.transpose(ps[:, b, cc * 128:(cc + 1) * 128],
                                               hid32[:, c * T + s * 128: c * T + (s + 1) * 128], ident[:])
                        return inst
                    S.add("pe", fn, reads=[r for c in range(half * 4, half * 4 + 4) for r in R_co(c)] + ["ident"],
                          writes=[("bank", b)], tag="outT", npe=4)
                    if raw:
                        if half == 0:
                            S.add("act", lambda e, b=b, s=s: e.activation(out=Y[:, s, 0:512], in_=ps[:, b, :],
                                                                          func=AF.Copy),
                                  reads=[("bank", b)], writes=[("Y", s)])
                        else:
                            S.add("dve", lambda e, b=b, s=s: e.tensor_copy(out=Y[:, s, 512:1024], in_=ps[:, b, :]),
                                  reads=[("bank", b)], writes=[("Y", s)])
                    elif half == 0:
                        S.add("act", lambda e, b=b, s=s: e.activation(out=Y[:, s, 0:512], in_=ps[:, b, :],
                                                                      func=AF.Copy, scale=r4[:, s:s + 1]),
                              reads=[("bank", b), ("small", id(r4))], writes=[("Y", s)])
                    else:
                        S.add("dve", lambda e, b=b, s=s: e.tensor_scalar(out=Y[:, s, 512:1024], in0=ps[:, b, :],
                                                                         scalar1=r4[:, s:s + 1], scalar2=None,
                                                                         op0=ALU.mult),
                              reads=[("bank", b), ("small", id(r4))], writes=[("Y", s)])

            def fn(e, i=i):
                return e.dma_start(out=out_d[i * T:(i + 1) * T, :].rearrange("(s p) f -> p s f", p=128),
                                   in_=Y[:]).then_inc(out_sem, 16)
            S.add("act", fn, reads=[("Y", s) for s in range(4)], dma_sem=out_sem)

        stages_all = ["ffn1", "mix", "ffn2", "ple"]
        load_x(0)
        load_p(0, 0)
        for i in range(NT):
            x_to_h(i)
            if i + 1 < NT:
                load_x(i + 1)
            stopped = False
            for l in range(L):
                ffn(i, l, 1, vcol(l, V_FFN1, 0), vcol(l, V_MIX, 0))
                if stop_after == (l, "ffn1"):
                    stopped = True
                    break
                mixer(i, l)
                if stop_after == (l, "mix"):
                    stopped = True
                    break
                ffn(i, l, 2, vcol(l, V_FFN2, 0), vcol(l, V_PLE, 0))
                if stop_after == (l, "ffn2"):
                    stopped = True
                    break
                if l + 1 < L:
                    ple(i, l, vcol(l + 1, V_FFN1, 0), True)
                else:
                    ple(i, l, L * 64, True, "F")
                if l + 1 < L:
                    load_p(i, l + 1)
                elif i + 1 < NT:
                    load_p(i + 1, 0)
                if stop_after == (l, "ple"):
                    stopped = True
                    break
            if stopped:
                while ws["pos"] < (i + 1) * L * NU:
                    _, sl, su = stream[ws["pos"]]
                    UC.acquire(i, sl, UNITS[su][0], UNITS[su][1])
                    UC.release(UNITS[su][0], UNITS[su][1])
                store_out(i, raw=True)
            else:
                store_out(i, raw=False)
        S.add("act", lambda e: None, writes=[("Y", s) for s in range(4)])

        S.assign_tokens(eng_sems)
        nc._pe_log = S.pe_log
        with nc.Block() as block:
            @block.tensor
            def _(e):
                S.emit_engine("pe", e, eng_sems)

            @block.scalar
            def _(e):
                S.emit_engine("act", e, eng_sems)

            @block.vector
            def _(e):
                S.emit_engine("dve", e, eng_sems)

            @block.gpsimd
            def _(e):
                S.emit_engine("pool", e, eng_sems)

            @block.sync
            def _(e):
                S.emit_engine("sp", e, eng_sems)
    return nc


def _run(inputs, NT, stop_after=None, trace=False):
    inp = {k: np.asarray(v) for k, v in inputs.items()}
    S_LOC = NT * T
    wsrc = host_arrange_weights(inp)
    vecs, wtap = host_arrange_vecs(inp)
    ident, invc = host_consts()
    nc = build_program(NT, stop_after=stop_after)
    in_maps = []
    for c in range(NCORES):
        in_maps.append({
            "x": np.ascontiguousarray(inp["x"][c, :S_LOC, :]),
            "p": np.ascontiguousarray(inp["p"][:, c, :S_LOC, :]),
            "wsrc": wsrc, "vecs": vecs, "wtap": wtap, "ident": ident, "invc": invc,
        })
    res = run_bass_kernel_spmd(nc, in_maps, core_ids=list(range(NCORES)), **({"trace": True} if trace else {}))
    out = np.stack([res.results[c]["out"] for c in range(NCORES)], axis=0)
    return out.astype(np.float32, copy=False), res


def kernel(**inputs):
    out, _ = _run(inputs, SEQ // T)
    return out
```

```python
import numpy as np
import concourse.bass as bass
import concourse.mybir as mybir
from concourse.bass_utils import run_bass_kernel_spmd

F32 = mybir.dt.float32
BF16 = mybir.dt.bfloat16
AF = mybir.ActivationFunctionType
ALU = mybir.AluOpType

D = 1024
DFF = 2816
NCH = 8
NJ = 22
T = 512
L = 2
PLE = 256
KC = 31
HZ = 16
HC = 30
SEQ = 8192
NCORES = 8
NSRC = 56
NU = 64
UW = 4096
RING = 5
EARLY = True
RMS_EPS = 1e-6
LN_EPS = 1e-5
NVEC = L * 64 + 8

V_FFN1, V_MIX, V_PSCALE, V_CB, V_LNG, V_LNB, V_FFN2, V_PLE = range(8)


def vcol(l, v, c):
    return (l * 8 + v) * 8 + c


def unit_table():
    units = []
    src = 0
    for jp in range(11):
        units.append(("gu1", jp, 4096, src)); src += 1
    for m in range(8):
        units.append(("dn1", m, NJ * 128, src)); src += 1
    for q in range(4):
        units.append(("glu", q, 4096, src)); src += 1
    for q in range(2):
        units.append(("zp", q, 4096, src)); src += 1
    for c in range(8):
        units.append(("cv", c, KC * 128, None))
    for q in range(2):
        units.append(("gp", q, 4096, src)); src += 1
    units.append(("pl", 0, 2048, src)); src += 1
    units.append(("gc", 0, 4096, src)); src += 1
    units.append(("co", 0, 4096, src)); src += 1
    units.append(("gc", 1, 4096, src)); src += 1
    units.append(("co", 1, 4096, src)); src += 1
    for q in range(2):
        units.append(("wo", q, 4096, src)); src += 1
    for jp in range(11):
        units.append(("gu2", jp, 4096, src)); src += 1
    for m in range(8):
        units.append(("dn2", m, NJ * 128, src)); src += 1
    units.append(("pp", 0, 2048, src)); src += 1
    for q in range(2):
        units.append(("pg", q, 4096, src)); src += 1
    assert len(units) == NU and src == NSRC
    return units


UNITS = unit_table()


def _colblock(W, n):
    K = W.shape[0]
    blk = W[:, n * 128:(n + 1) * 128].reshape(K // 128, 128, 128)
    return np.ascontiguousarray(blk.transpose(1, 0, 2)).reshape(128, (K // 128) * 128)


def host_arrange_weights(inp):
    wsrc = np.zeros((L * NSRC, 128, UW), np.float32)
    for l in range(L):
        win = inp["w_in"][l]
        for (kind, arg, ncols, src) in UNITS:
            if src is None:
                continue
            dst = wsrc[l * NSRC + src]
            if kind in ("gu1", "gu2"):
                wg = inp["ffn1_w_gate" if kind == "gu1" else "ffn2_w_gate"][l]
                wu = inp["ffn1_w_up" if kind == "gu1" else "ffn2_w_up"][l]
                for jj in range(2):
                    j = 2 * arg + jj
                    dst[:, (jj * 2 + 0) * 1024:(jj * 2 + 1) * 1024] = _colblock(wg, j)
                    dst[:, (jj * 2 + 1) * 1024:(jj * 2 + 2) * 1024] = _colblock(wu, j)
            elif kind in ("dn1", "dn2"):
                wd = inp["ffn1_w_down" if kind == "dn1" else "ffn2_w_down"][l]
                dst[:, :NJ * 128] = _colblock(wd, arg)
            elif kind == "glu":
                chunks = []
                for mm in (2 * arg, 2 * arg + 1):
                    chunks += [8 + mm, 16 + mm]
                for i, ch in enumerate(chunks):
                    dst[:, i * 1024:(i + 1) * 1024] = _colblock(win, ch)
            elif kind in ("zp", "gp", "gc"):
                base = {"zp": 0, "gp": 24, "gc": 32}[kind]
                for i in range(4):
                    dst[:, i * 1024:(i + 1) * 1024] = _colblock(win, base + 4 * arg + i)
            elif kind == "pl":
                pw = inp["pool_w"][l]
                for g in range(4):
                    for oc in range(2):
                        for kc in range(2):
                            pos = ((g * 2 + oc) * 2 + kc) * 128
                            dst[:, pos:pos + 128] = pw[g, kc * 128:(kc + 1) * 128, oc * 128:(oc + 1) * 128]
            elif kind in ("co", "wo", "pg"):
                W = inp[{"co": "conv_w_out", "wo": "w_out", "pg": "ple_w_gate"}[kind]][l]
                for i in range(4):
                    dst[:, i * 1024:(i + 1) * 1024] = _colblock(W, 4 * arg + i)
            elif kind == "pp":
                W = inp["ple_w_proj"][l]
                for n in range(8):
                    dst[:, n * 256:(n + 1) * 256] = _colblock(W, n)
            else:
                raise AssertionError(kind)
    return wsrc


def host_arrange_vecs(inp):
    vecs = np.zeros((128, NVEC), np.float32)
    names = ["ffn1_norm", "mix_norm", "pool_scale", "conv_dw_b", "conv_ln_g", "conv_ln_b",
             "ffn2_norm", "ple_norm"]
    for l in range(L):
        for v, nm in enumerate(names):
            vecs[:, vcol(l, v, 0):vcol(l, v, 0) + 8] = inp[nm][l].reshape(8, 128).T
    vecs[:, L * 64:L * 64 + 8] = inp["final_norm"].reshape(8, 128).T
    wtap = np.zeros((128, L * 8 * KC), np.float32)
    for l in range(L):
        w = inp["conv_dw_w"][l].reshape(KC, 8, 128)
        wtap[:, l * 8 * KC:(l + 1) * 8 * KC] = w.transpose(2, 1, 0).reshape(128, 8 * KC)
    return vecs, wtap


def host_consts():
    ident = np.eye(128, dtype=np.float32)
    invc = np.zeros((128, 4 * 16), np.float32)
    for g, w in enumerate((2, 4, 8, 16)):
        for t in range(16):
            invc[:, g * 16 + t] = np.float32(1.0) / np.float32(min(t + 1, w))
    return ident, invc


class _Op:
    __slots__ = ("eng", "fn", "deps", "is_target", "token", "dma")

    def __init__(self, eng, fn, dma):
        self.eng = eng
        self.fn = fn
        self.deps = set()
        self.is_target = False
        self.token = None
        self.dma = dma


class Sched:
    ENGS = ("pe", "act", "dve", "pool", "sp")

    def __init__(self):
        self.ops = []
        self.eng_ops = {e: [] for e in self.ENGS}
        self.last_w = {}
        self.readers = {}
        self.dma_count = {}
        self.pe_log = []
        self.cur_tag = ""

    def add(self, eng, fn, reads=(), writes=(), dma_sem=None, tag=None, npe=1):
        idx = len(self.ops)
        if eng == "pe":
            self.pe_log.append((tag or self.cur_tag, npe))
        dma = None
        if dma_sem is not None:
            n = self.dma_count.get(id(dma_sem), 0) + 1
            self.dma_count[id(dma_sem)] = n
            dma = (dma_sem, 16 * n)
        op = _Op(eng, fn, dma)
        deps = op.deps
        for r in reads:
            w = self.last_w.get(r)
            if w is not None:
                deps.add(w)
            if isinstance(r, tuple) and r[0] == "bank":
                rd = self.readers.get(r)
                if rd:
                    deps.update(v for k_, v in rd.items() if k_ != eng)
        for r in writes:
            w = self.last_w.get(r)
            if w is not None:
                deps.add(w)
            rd = self.readers.get(r)
            if rd:
                deps.update(rd.values())
        keep = set()
        for d in deps:
            dop = self.ops[d]
            if dop.dma is None and dop.eng == eng and eng in ("pe", "sp"):
                continue
            keep.add(d)
            dop.is_target = True
        op.deps = keep
        key = eng if dma is None else ("dma", idx)
        for r in reads:
            self.readers.setdefault(r, {})[key] = idx
        for r in writes:
            self.last_w[r] = idx
            self.readers[r] = {}
        self.ops.append(op)
        self.eng_ops[eng].append(idx)
        return idx

    def assign_tokens(self, eng_sems):
        for e in self.ENGS:
            cnt = 0
            for i in self.eng_ops[e]:
                op = self.ops[i]
                if op.dma is not None:
                    op.token = op.dma
                elif op.is_target:
                    cnt += 1
                    op.token = (eng_sems[e], cnt)

    def emit_engine(self, eng, e, eng_sems):
        waited = {}
        for i in self.eng_ops[eng]:
            op = self.ops[i]
            need = {}
            for d in op.deps:
                sem, val = self.ops[d].token
                k = id(sem)
                if waited.get(k, 0) >= val:
                    continue
                if k not in need or need[k][1] < val:
                    need[k] = (sem, val)
            for k, (sem, val) in need.items():
                e.wait_ge(sem, val)
                waited[k] = val
            inst = op.fn(e)
            if op.dma is None and op.is_target:
                assert inst is not None
                inst.then_inc(eng_sems[eng], 1)


def build_program(NT, stop_after=None):
    from contextlib import ExitStack
    nc = bass.Bass("TRN2", target_bir_lowering=False)
    S_LOC = NT * T
    x_d = nc.dram_tensor("x", [S_LOC, D], F32, kind="ExternalInput").ap()
    p_d = nc.dram_tensor("p", [L, S_LOC, PLE], F32, kind="ExternalInput").ap()
    wsrc_d = nc.dram_tensor("wsrc", [L * NSRC, 128, UW], F32, kind="ExternalInput").ap()
    vecs_d = nc.dram_tensor("vecs", [128, NVEC], F32, kind="ExternalInput").ap()
    wtap_d = nc.dram_tensor("wtap", [128, L * 8 * KC], F32, kind="ExternalInput").ap()
    ident_d = nc.dram_tensor("ident", [128, 128], F32, kind="ExternalInput").ap()
    invc_d = nc.dram_tensor("invc", [128, 64], F32, kind="ExternalInput").ap()
    wbf_d = nc.dram_tensor("wbf", [L * NU, 128, UW], BF16, kind="Internal").ap()
    out_d = nc.dram_tensor("out", [S_LOC, D], F32, kind="ExternalOutput").ap()

    S = Sched()
    with ExitStack() as es:
        def sb(name, shape, dt):
            return es.enter_context(nc.sbuf_tensor(name + "_sb", shape, dt))

        def sem(name):
            return es.enter_context(nc.semaphore(name))

        h = sb("h", [128, NCH, T], F32)
        xn = sb("xn", [128, NCH, T], BF16)
        hid = sb("hid", [128, NJ * T], BF16)
        sq = sb("sq", [128, 4, T], BF16)
        st4 = sb("st4", [128, 8], F32)
        r4 = sb("r4", [128, 8], F32)
        mu4 = sb("mu4", [128, 8], F32)
        dg = sb("dg", [128, 2, T], F32)
        onesf = sb("onesf", [128, 128], F32)
        rstd = sb("rstd", [128, T], F32)
        mu = sb("mu", [128, T], F32)
        mhalf = sb("mhalf", [128, 8], F32)
        tmpA = sb("tmpA", [128, 4, T + HZ], F32)
        tmpB = sb("tmpB", [128, 4, T], F32)
        zp = sb("zp", [128, NCH, T + HZ], F32)
        pooled = sb("pooled", [128, NCH, T], BF16)
        cbf = sb("cbf", [128, NCH, T + HC], BF16)
        lnout = sb("lnout", [128, NCH, T], BF16)
        Y = sb("Y", [128, 4, D], F32)
        xin = sb("xin", [128, 4, D], F32)
        pin = sb("pin", [128, 2, 4, PLE], F32)
        pT = sb("pT", [128, 2, T], BF16)
        ring = sb("ring", [128, RING, UW], BF16)
        halo_z = sb("halo_z", [128, L * NCH, HZ], F32)
        halo_c = sb("halo_c", [128, L * NCH, HC], BF16)
        vecs = sb("vecs", [128, NVEC], F32)
        wtap = sb("wtap", [128, L * 8 * KC], F32)
        ident = sb("ident", [128, 128], F32)
        invc = sb("invc", [128, 64], F32)
        ones = sb("ones", [128, 128], BF16)
        ps = es.enter_context(nc.psum_tensor("ps", [128, 8, T], F32))

        hid32 = hid[:].bitcast(F32)
        Ybf = Y.bitcast(BF16)

        def cbf2(c, a, b):
            off = (c % 2) * 1024
            return Ybf[:, c // 2, off + a: off + b]

        eng_sems = {e: sem("m_" + e) for e in Sched.ENGS}
        slot_sem = [sem(f"slot{i}") for i in range(RING)]
        xin_sem = sem("xin")
        pin_sem = [sem("pin0"), sem("pin1")]
        out_sem = sem("outst")

        def R_hid(j0, j1):
            return [("hid", j) for j in range(j0, j1)]

        def co32(c):
            return hid32[:, c * T:(c + 1) * T]

        def R_co(c):
            return [("hid", 2 * c), ("hid", 2 * c + 1)]

        def ma(c):
            return Y[:, c // 2, (c % 2) * T:(c % 2 + 1) * T]

        def R_ma(c):
            return [("Y", c // 2)]

        bank_rr = [0]

        def next_bank():
            b = bank_rr[0]
            bank_rr[0] = (b + 1) % 6
            return b

        const_list = [(vecs[:], vecs_d, "vecs"), (wtap[:], wtap_d, "wtap"),
                      (ident[:], ident_d, "ident"), (invc[:], invc_d, "invc")]
        for dst, src, res in const_list:
            csem_ = sem("c_" + res)

            def fnc(e, dst=dst, src=src, csem_=csem_):
                return e.dma_start(out=dst, in_=src).then_inc(csem_, 16)
            S.add("sp", fnc, writes=[res], dma_sem=csem_)
        S.add("dve", lambda e: e.memset(ones[:], 1.0), writes=["ones"])
        S.add("dve", lambda e: e.memset(onesf[:], 1.0), writes=["onesf"])
        S.add("dve", lambda e: e.memset(mhalf[:], -0.5), writes=["mhalf"])
        S.add("dve", lambda e: e.memset(halo_z[:], 0.0), writes=[("halo_z", i) for i in range(L * NCH)])
        S.add("dve", lambda e: e.memset(halo_c[:], 0.0), writes=[("halo_c", i) for i in range(L * NCH)])
        CONSTS = ["ones", "vecs", "wtap", "ident", "invc"]

        cast_sems = [sem(f"cast{i_}") for i_ in range(8)]
        cast_list = [(l_, pos) for l_ in range(L) for pos in range(NU) if UNITS[pos][3] is not None]
        cst = {"next": 0}

        def emit_cast(n):
            for _ in range(n):
                k = cst["next"]
                if k >= len(cast_list):
                    return
                cst["next"] = k + 1
                l_, pos = cast_list[k]
                srci = UNITS[pos][3]
                csem = cast_sems[k % 8]
                src_ap = wsrc_d[l_ * NSRC + srci:l_ * NSRC + srci + 1]
                dst_ap = wbf_d[l_ * NU + pos:l_ * NU + pos + 1]

                def fn(e, src_ap=src_ap, dst_ap=dst_ap, csem=csem):
                    return e.dma_start(out=dst_ap, in_=src_ap, max_dma_last_dim=4096).then_inc(csem, 16)
                S.add("pool", fn, writes=[("wbf", l_, pos), ("castsem", k % 8)], dma_sem=csem)

        cast_idx = {lp: k_ for k_, lp in enumerate(cast_list)}
        emit_cast(8)

        zpf = zp[:].rearrange("p a b -> p (a b)").bitcast(BF16)
        Yf = Ybf[:].rearrange("p a b -> p (a b)")
        cbff = cbf[:].rearrange("p a b -> p (a b)")
        poolf = pooled[:].rearrange("p a b -> p (a b)")
        stg = [
            (zpf, 0, [("zp", m_) for m_ in range(4)]),
            (zpf, 4224, [("zp", m_) for m_ in range(4, 8)]),
            (Yf, 0, [("Y", 0), ("Y", 1)]),
            (Yf, 4096, [("Y", 2), ("Y", 3)]),
            (cbff, 0, [("cbf", m_) for m_ in range(8)]),
            (poolf, 0, [("pooled", m_) for m_ in range(8)]),
        ]
        dg_sem = [sem(f"dg{i_}") for i_ in range(6)]
        bcnt = {"dve": 0, "act": 0}
        nbuild = 0
        for l in range(L):
            for c in range(8):
                eng = "act" if nbuild % 8 in (1, 4, 6) else "dve"
                nbuild += 1
                sidx = (0 if eng == "dve" else 3) + bcnt[eng] % 3
                bcnt[eng] += 1
                buf, off, res = stg[sidx]
                u = 25 + c

                def fn(e, l=l, c=c, buf=buf, off=off, eng=eng):
                    inst = None
                    for k in range(KC):
                        col = (l * 8 + c) * KC + k
                        o = buf[:, off + k * 128: off + (k + 1) * 128]
                        if eng == "dve":
                            inst = e.tensor_scalar(out=o, in0=ident[:], scalar1=wtap[:, col:col + 1],
                                                   scalar2=None, op0=ALU.mult)
                        else:
                            inst = e.activation(out=o, in_=ident[:], func=AF.Copy,
                                                scale=wtap[:, col:col + 1])
                    return inst
                S.add(eng, fn, reads=["ident", "wtap"], writes=res)

                def fn2(e, l=l, u=u, buf=buf, off=off, sidx=sidx):
                    return e.dma_start(out=wbf_d[l * NU + u, :, 0:KC * 128],
                                       in_=buf[:, off:off + KC * 128]).then_inc(dg_sem[sidx], 16)
                S.add("act" if eng == "act" else "pool", fn2, reads=res, writes=[("wbf", l, u)],
                      dma_sem=dg_sem[sidx])

        stream = []
        for i in range(NT):
            for l in range(L):
                for u in range(NU):
                    stream.append((i, l, u))
        ws = {"next_load": 0, "pos": 0}

        def issue_load():
            s = ws["next_load"]
            if s >= len(stream):
                return
            ws["next_load"] = s + 1
            _, l, u = stream[s]
            slot = s % RING
            ncols = UNITS[u][2]

            def fn(e, l=l, u=u, slot=slot, ncols=ncols):
                return e.dma_start(out=ring[:, slot, 0:ncols],
                                   in_=wbf_d[l * NU + u, :, 0:ncols]).then_inc(slot_sem[slot], 16)
            rds = [("wbf", l, u)]
            if stream[s][0] == 0 and (l, u) in cast_idx:
                rds.append(("castsem", cast_idx[(l, u)] % 8))
            S.add("sp", fn, reads=rds, writes=[("slot", slot)], dma_sem=slot_sem[slot])

        class UnitCursor:
            def __init__(self):
                self.released = set()
                self.cur = {}

            def acquire(self, i, l, kind, arg):
                if i == 0:
                    emit_cast(1)
                s = ws["pos"]
                ti, tl, tu = stream[s]
                assert (ti, tl) == (i, l) and UNITS[tu][0] == kind and UNITS[tu][1] == arg, \
                    (stream[s], UNITS[tu], i, l, kind, arg)
                ws["pos"] = s + 1
                self.cur[(kind, arg)] = s
                return s % RING

            def release(self, kind, arg):
                s = self.cur.pop((kind, arg))
                self.released.add(s)
                while (ws["next_load"] - RING) in self.released:
                    self.released.discard(ws["next_load"] - RING)
                    if ws["next_load"] >= len(stream):
                        break
                    issue_load()

        UC = UnitCursor()
        for _ in range(RING):
            issue_load()

        def mm_group(bank, pairs, reads, extra_writes=()):
            n = len(pairs)

            def fn(e, pairs=pairs, bank=bank, n=n):
                inst = None
                for k, (lt, rh) in enumerate(pairs):
                    inst = e.matmul(ps[:, bank, :], lhsT=lt, rhs=rh, start=(k == 0), stop=(k == n - 1))
                return inst
            S.add("pe", fn, reads=reads, writes=[("bank", bank)] + list(extra_writes), npe=n)

        def stat_mms(k, col0, first, last):
            def fn(e, k=k, col0=col0, first=first, last=last):
                inst = None
                for s_ in range(4):
                    inst = e.matmul(ps[:, 6, col0 + s_:col0 + s_ + 1], lhsT=sq[:, k, s_ * 128:(s_ + 1) * 128],
                                    rhs=ones[:, 0:1], start=(first and s_ == 0), stop=(last and s_ == 3),
                                    skip_group_check=True)
                return inst
            S.add("pe", fn, reads=[("sq", k), "ones"], writes=[("bank", 6)], tag="stat", npe=4)

        def bcast(src4, col, dsel, bank):
            def fn(e, src4=src4, col=col, dsel=dsel):
                inst = None
                for s_ in range(4):
                    inst = e.tensor_scalar(out=dg[:, dsel, s_ * 128:(s_ + 1) * 128], in0=ident[:],
                                           scalar1=src4[:, col + s_:col + s_ + 1], scalar2=None, op0=ALU.mult)
                return inst
            S.add("dve", fn, reads=["ident", ("small", id(src4))], writes=[("dg", dsel)])
            S.add("pe", lambda e, dsel=dsel, bank=bank: e.matmul(ps[:, bank, :], lhsT=onesf[:], rhs=dg[:, dsel, :],
                                                                 start=True, stop=True),
                  reads=[("dg", dsel), "onesf"], writes=[("bank", bank)], tag="bcast")

        class NormAcc:
            def __init__(self):
                self.n = 0
                self.k = 0
                self.gbase = None
                self.want_xg = True

            def start(self, gbase, want_xg=True, buf="A"):
                self.n = 0
                self.gbase = gbase
                self.want_xg = want_xg
                self.buf = buf

            def chunk_act(self, c):
                k = self.k
                self.k = (k + 1) % 4
                S.add("act", lambda e, c=c, k=k: e.activation(out=sq[:, k, :], in_=h[:, c, :], func=AF.Square),
                      reads=[("h", c)], writes=[("sq", k)])
                if self.want_xg and EARLY:
                    gb = self.gbase
                    if self.buf == "F":
                        S.add("act", lambda e, c=c, gb=gb: e.activation(out=co32(c), in_=h[:, c, :], func=AF.Copy,
                                                                        scale=vecs[:, gb + c:gb + c + 1]),
                              reads=[("h", c), "vecs"], writes=R_co(c))
                    else:
                        dst = lnout if self.buf == "A" else xn
                        rn = "lnout" if self.buf == "A" else "xn"
                        S.add("act", lambda e, c=c, gb=gb, dst=dst: e.activation(out=dst[:, c, :], in_=h[:, c, :],
                                                                                 func=AF.Copy,
                                                                                 scale=vecs[:, gb + c:gb + c + 1]),
                              reads=[("h", c), "vecs"], writes=[(rn, c)])
                return k

            def chunk_pe(self, k):
                stat_mms(k, 0, self.n == 0, self.n == NCH - 1)
                self.n += 1

            def finish_rstd(self):
                S.add("act", lambda e: e.activation(out=st4[:, 0:4], in_=ps[:, 6, 0:4], func=AF.Identity,
                                                    scale=1.0 / D, bias=epsr[:, 0:1]),
                      reads=[("bank", 6), "eps"], writes=[("small", id(st4))])
                S.add("pool", lambda e: e.tensor_tensor(out=r4[:, 0:4], in0=st4[:, 0:4], in1=mhalf[:, 0:4], op=ALU.pow),
                      reads=[("small", id(st4)), "mhalf"], writes=[("small", id(r4))])
                bcast(r4, 0, 0, 7)
                if EARLY:
                    S.add("act", lambda e: e.activation(out=rstd[:], in_=ps[:, 7, :], func=AF.Copy),
                          reads=[("bank", 7)], writes=["rstd"])

            def xn_ops(self, gbase, c0=0, c1=NCH):
                for c in range(c0, c1):
                    if EARLY:
                        S.add("dve", lambda e, c=c: e.scalar_tensor_tensor(
                            out=xn[:, c, :], in0=h[:, c, :], scalar=vecs[:, gbase + c:gbase + c + 1],
                            in1=rstd[:], op0=ALU.mult, op1=ALU.mult),
                            reads=[("h", c), "rstd", "vecs"], writes=[("xn", c)])
                    else:
                        S.add("dve", lambda e, c=c: e.scalar_tensor_tensor(
                            out=xn[:, c, :], in0=h[:, c, :], scalar=vecs[:, gbase + c:gbase + c + 1],
                            in1=ps[:, 7, :], op0=ALU.mult, op1=ALU.mult),
                            reads=[("h", c), ("bank", 7), "vecs"], writes=[("xn", c)])

        def run_boundary(n_items, mm_fn, evac_fn, xn_gbase=None, evac_lag=2):
            pend = {}
            pend[0] = mm_fn(0, True)
            pend[1] = mm_fn(1, True)
            NA.finish_rstd()
            pend[2] = mm_fn(2, True)
            nev = 0
            for _ in range(3 - evac_lag):
                evac_fn(nev, pend[nev], True)
                nev += 1
            xq = 0
            for q in range(3, n_items):
                pend[q] = mm_fn(q, True)
                evac_fn(nev, pend[nev], True)
                nev += 1
                if xn_gbase is not None and xq < NCH:
                    NA.xn_ops(xn_gbase, xq, min(NCH, xq + 2))
                    xq += 2
            while nev < n_items:
                evac_fn(nev, pend[nev], True)
                nev += 1
            if xn_gbase is not None and xq < NCH:
                NA.xn_ops(xn_gbase, xq, NCH)

        eps_t = sb("eps_t", [128, 2], F32)
        epsr = eps_t
        S.add("dve", lambda e: e.memset(eps_t[:, 0:1], RMS_EPS), writes=["eps"])
        S.add("dve", lambda e: e.memset(eps_t[:, 1:2], LN_EPS), reads=["eps"], writes=["eps"])

        NA = NormAcc()
        tA = [0]
        tB = [0]

        def nextA():
            k = tA[0]
            tA[0] = (k + 1) % 4
            return k

        def nextB():
            k = tB[0]
            tB[0] = (k + 1) % 4
            return k

        class Lag:
            def __init__(self, lag):
                self.lag = lag
                self.q = []

            def push(self, c):
                self.q.append(NA.chunk_act(c))
                if len(self.q) > self.lag:
                    NA.chunk_pe(self.q.pop(0))

            def flush(self):
                while self.q:
                    NA.chunk_pe(self.q.pop(0))

        def ffn(i, l, which, gbase, next_gbase):
            S.cur_tag = f"t{i}l{l}ffn{which}"
            gu = "gu1" if which == 1 else "gu2"
            dn = "dn1" if which == 1 else "dn2"
            st = {"slot": None}

            def mm_j(j, early):
                jp, jj = divmod(j, 2)
                if jj == 0:
                    st["slot"] = UC.acquire(i, l, gu, jp)
                slot = st["slot"]
                bg = next_bank()
                bu = next_bank()
                src = xn
                rn = "xn"
                for (bank, sel) in ((bg, 0), (bu, 1)):
                    base = (jj * 2 + sel) * 1024
                    pairs = [(ring[:, slot, base + k * 128: base + (k + 1) * 128], src[:, k, :])
                             for k in range(NCH)]
                    mm_group(bank, pairs, reads=[("slot", slot)] + [(rn, k) for k in range(NCH)])
                if jj == 1:
                    UC.release(gu, jp)
                return (bg, bu)

            def evac_j(j, banks, early):
                bg, bu = banks
                ka = nextA()
                if early:
                    k1 = nextB()
                    S.add("dve", lambda e, bg=bg, k1=k1: e.tensor_tensor(out=tmpB[:, k1, :], in0=ps[:, bg, :],
                                                                         in1=rstd[:], op=ALU.mult),
                          reads=[("bank", bg), "rstd"], writes=[("tmpB", k1)])
                    S.add("act", lambda e, k1=k1, ka=ka: e.activation(out=tmpA[:, ka, 0:T], in_=tmpB[:, k1, :],
                                                                     func=AF.Silu),
                          reads=[("tmpB", k1)], writes=[("tmpA", ka)])
                    k2 = nextB()
                    S.add("dve", lambda e, bu=bu, k2=k2: e.tensor_tensor(out=tmpB[:, k2, :], in0=ps[:, bu, :],
                                                                         in1=rstd[:], op=ALU.mult),
                          reads=[("bank", bu), "rstd"], writes=[("tmpB", k2)])
                    S.add("dve", lambda e, ka=ka, k2=k2, j=j: e.tensor_tensor(
                        out=hid[:, j * T:(j + 1) * T], in0=tmpA[:, ka, 0:T], in1=tmpB[:, k2, :], op=ALU.mult),
                        reads=[("tmpA", ka), ("tmpB", k2)], writes=[("hid", j)])
                else:
                    S.add("act", lambda e, bg=bg, ka=ka: e.activation(out=tmpA[:, ka, 0:T], in_=ps[:, bg, :],
                                                                     func=AF.Silu),
                          reads=[("bank", bg)], writes=[("tmpA", ka)])
                    S.add("dve", lambda e, bu=bu, ka=ka, j=j: e.tensor_tensor(
                        out=hid[:, j * T:(j + 1) * T], in0=tmpA[:, ka, 0:T], in1=ps[:, bu, :], op=ALU.mult),
                        reads=[("tmpA", ka), ("bank", bu)], writes=[("hid", j)])

            run_boundary(NJ, mm_j, evac_j)

            NA.start(next_gbase, True, "A")
            lag = Lag(1)
            for m in range(NCH):
                slot = UC.acquire(i, l, dn, m)
                b = next_bank()
                pairs = [(ring[:, slot, j * 128:(j + 1) * 128], hid[:, j * T:(j + 1) * T]) for j in range(NJ)]
                mm_group(b, pairs, reads=[("slot", slot)] + R_hid(0, NJ))
                UC.release(dn, m)
                S.add("dve", lambda e, b=b, m=m: e.scalar_tensor_tensor(
                    out=h[:, m, :], in0=ps[:, b, :], scalar=0.5, in1=h[:, m, :], op0=ALU.mult, op1=ALU.add),
                    reads=[("bank", b), ("h", m)], writes=[("h", m)])
                lag.push(m)
            lag.flush()

        def mixer(i, l):
            first_tile = (i == 0)
            S.cur_tag = f"t{i}l{l}mix"
            for m in range(NCH):
                S.add("pool", lambda e, m=m: e.tensor_copy(out=zp[:, m, 0:HZ], in_=halo_z[:, l * NCH + m, :]),
                      reads=[("halo_z", l * NCH + m)], writes=[("zp", m)])
                S.add("pool", lambda e, m=m: e.tensor_copy(out=cbf[:, m, 0:HC], in_=halo_c[:, l * NCH + m, :]),
                      reads=[("halo_c", l * NCH + m)], writes=[("cbf", m)])

            def zpool_chunk(m, slot):
                b = next_bank()
                base = (m % 4) * 1024
                pairs = [(ring[:, slot, base + k * 128: base + (k + 1) * 128], xn[:, k, :]) for k in range(NCH)]
                mm_group(b, pairs, reads=[("slot", slot)] + [("xn", k) for k in range(NCH)])
                S.add("act", lambda e, b=b, m=m: e.activation(out=zp[:, m, HZ:HZ + T], in_=ps[:, b, :], func=AF.Copy),
                      reads=[("bank", b)], writes=[("zp", m)])
                S.add("pool", lambda e, m=m: e.tensor_copy(out=halo_z[:, l * NCH + m, :], in_=zp[:, m, T:T + HZ]),
                      reads=[("zp", m)], writes=[("halo_z", l * NCH + m)])

            def pool_sums(m):
                g = m // 2
                w = 2 << g
                src = zp[:, m, :]
                src_res = ("zp", m)
                lo = 0
                sh = 1
                ksrc = None
                for lev in range(g + 1):
                    kd = nextA()
                    nlo = lo + sh
                    if ksrc is None:
                        a0 = zp[:, m, nlo:T + HZ]
                        a1 = zp[:, m, nlo - sh:T + HZ - sh]
                    else:
                        a0 = tmpA[:, ksrc, nlo:T + HZ]
                        a1 = tmpA[:, ksrc, nlo - sh:T + HZ - sh]
                    S.add("pool" if g == 3 else "dve", lambda e, kd=kd, nlo=nlo, a0=a0, a1=a1: e.tensor_tensor(
                        out=tmpA[:, kd, nlo:T + HZ], in0=a0, in1=a1, op=ALU.add),
                        reads=[src_res], writes=[("tmpA", kd)])
                    src_res = ("tmpA", kd)
                    ksrc = kd
                    lo = nlo
                    sh *= 2
                S.add("dve", lambda e, ksrc=ksrc, m=m, w=w: e.scalar_tensor_tensor(
                    out=pooled[:, m, :], in0=tmpA[:, ksrc, HZ:HZ + T], scalar=1.0 / w, in1=zp[:, m, HZ:HZ + T],
                    op0=ALU.mult, op1=ALU.subtract),
                    reads=[("tmpA", ksrc), ("zp", m)], writes=[("pooled", m)])
                if first_tile:
                    kb = nextB()
                    S.add("dve", lambda e, ksrc=ksrc, kb=kb, g=g: e.tensor_tensor(
                        out=tmpB[:, kb, 0:HZ], in0=tmpA[:, ksrc, HZ:2 * HZ], in1=invc[:, g * 16:(g + 1) * 16],
                        op=ALU.mult),
                        reads=[("tmpA", ksrc), "invc"], writes=[("tmpB", kb)])
                    S.add("dve", lambda e, kb=kb, m=m: e.tensor_tensor(
                        out=pooled[:, m, 0:HZ], in0=tmpB[:, kb, 0:HZ], in1=zp[:, m, HZ:2 * HZ], op=ALU.subtract),
                        reads=[("tmpB", kb), ("zp", m)], writes=[("pooled", m)])

            gst = {"slot": None}

            def glu_mm(m, early):
                if m % 2 == 0:
                    gst["slot"] = UC.acquire(i, l, "glu", m // 2)
                slot = gst["slot"]
                ba = next_bank()
                bgk = next_bank()
                src = lnout if early else xn
                rn = "lnout" if early else "xn"
                for (bank, sel) in ((ba, 0), (bgk, 1)):
                    base = ((m % 2) * 2 + sel) * 1024
                    pairs = [(ring[:, slot, base + k * 128: base + (k + 1) * 128], src[:, k, :]) for k in range(NCH)]
                    mm_group(bank, pairs, reads=[("slot", slot)] + [(rn, k) for k in range(NCH)])
                if m % 2 == 1:
                    UC.release("glu", m // 2)
                return (ba, bgk)

            def glu_evac(m, banks, early):
                ba, bgk = banks
                kb = nextB()
                if early:
                    k0 = nextB()
                    S.add("dve", lambda e, bgk=bgk, k0=k0: e.tensor_tensor(out=tmpB[:, k0, :], in0=ps[:, bgk, :],
                                                                           in1=rstd[:], op=ALU.mult),
                          reads=[("bank", bgk), "rstd"], writes=[("tmpB", k0)])
                    S.add("act", lambda e, k0=k0, kb=kb: e.activation(out=tmpB[:, kb, :], in_=tmpB[:, k0, :],
                                                                     func=AF.Tanh, scale=0.5),
                          reads=[("tmpB", k0)], writes=[("tmpB", kb)])
                    k3 = nextB()
                    S.add("dve", lambda e, ba=ba, kb=kb, k3=k3: e.scalar_tensor_tensor(
                        out=tmpB[:, k3, :], in0=tmpB[:, kb, :], scalar=1.0, in1=ps[:, ba, :],
                        op0=ALU.add, op1=ALU.mult),
                        reads=[("tmpB", kb), ("bank", ba)], writes=[("tmpB", k3)])
                    S.add("dve", lambda e, k3=k3, m=m: e.tensor_tensor(out=cbf[:, m, HC:HC + T], in0=tmpB[:, k3, :],
                                                                       in1=rstd[:], op=ALU.mult),
                          reads=[("tmpB", k3), "rstd"], writes=[("cbf", m)])
                else:
                    S.add("act", lambda e, bgk=bgk, kb=kb: e.activation(out=tmpB[:, kb, :], in_=ps[:, bgk, :],
                                                                        func=AF.Tanh, scale=0.5),
                          reads=[("bank", bgk)], writes=[("tmpB", kb)])
                    S.add("dve", lambda e, ba=ba, kb=kb, m=m: e.scalar_tensor_tensor(
                        out=cbf[:, m, HC:HC + T], in0=tmpB[:, kb, :], scalar=1.0, in1=ps[:, ba, :],
                        op0=ALU.add, op1=ALU.mult),
                        reads=[("tmpB", kb), ("bank", ba)], writes=[("cbf", m)])
                S.add("pool", lambda e, m=m: e.tensor_copy(out=halo_c[:, l * NCH + m, :], in_=cbf[:, m, T:T + HC]),
                      reads=[("cbf", m)], writes=[("halo_c", l * NCH + m)])
                S.add("pool", lambda e, m=m: e.tensor_copy(out=cbf2(m, 0, T + HC - 1), in_=cbf[:, m, 1:T + HC]),
                      reads=[("cbf", m)], writes=[("Y", m // 2)])

            run_boundary(NCH, glu_mm, glu_evac, xn_gbase=vcol(l, V_MIX, 0))

            for m in range(NCH):
                if m % 4 == 0:
                    zslot = UC.acquire(i, l, "zp", m // 4)
                zpool_chunk(m, zslot)
                if m % 4 == 3:
                    UC.release("zp", m // 4)
                pool_sums(m)

            lnb = (6, 7)
            cbias = vcol(l, V_CB, 0)
            pend = []

            def ln_stat_act(c):
                k1 = NA.k
                NA.k = (k1 + 1) % 4
                k2 = NA.k
                NA.k = (k2 + 1) % 4
                S.add("act", lambda e, c=c, k1=k1: e.activation(out=sq[:, k1, :], in_=co32(c), func=AF.Copy),
                      reads=R_co(c), writes=[("sq", k1)])
                S.add("act", lambda e, c=c, k2=k2: e.activation(out=sq[:, k2, :], in_=co32(c), func=AF.Square),
                      reads=R_co(c), writes=[("sq", k2)])
                return (c, k1, k2)

            def ln_stat_pe(ck):
                c, k1, k2 = ck
                stat_mms(k1, 0, c == 0, False)
                stat_mms(k2, 4, False, c == NCH - 1)

            for c in range(NCH):
                slot = UC.acquire(i, l, "cv", c)
                b = next_bank()
                pairs = [(ring[:, slot, k * 128:(k + 1) * 128],
                          cbf[:, c, k:k + T] if k % 2 == 0 else cbf2(c, k - 1, k - 1 + T)) for k in range(KC)]
                mm_group(b, pairs, reads=[("slot", slot), ("cbf", c), ("Y", c // 2)])
                UC.release("cv", c)
                if pend:
                    ln_stat_pe(pend.pop(0))
                S.add("act", lambda e, b=b, c=c: e.activation(
                    out=co32(c), in_=ps[:, b, :], func=AF.Identity, scale=0.5,
                    bias=vecs[:, cbias + c:cbias + c + 1]),
                    reads=[("bank", b), "vecs"], writes=R_co(c))
                pend.append(ln_stat_act(c))
            while pend:
                ln_stat_pe(pend.pop(0))

            psc = vcol(l, V_PSCALE, 0)
            gpst = {"slot": None}

            def gp_chunk(m):
                if m % 4 == 0:
                    gpst["slot"] = UC.acquire(i, l, "gp", m // 4)
                gslot = gpst["slot"]
                b = next_bank()
                base = (m % 4) * 1024
                pairs = [(ring[:, gslot, base + k * 128: base + (k + 1) * 128], xn[:, k, :]) for k in range(NCH)]
                mm_group(b, pairs, reads=[("slot", gslot)] + [("xn", k) for k in range(NCH)])
                if m % 4 == 3:
                    UC.release("gp", m // 4)
                S.add("act", lambda e, b=b, m=m: e.activation(out=ma(m), in_=ps[:, b, :], func=AF.Tanh, scale=0.5),
                      reads=[("bank", b)], writes=R_ma(m))

            for m in range(3):
                gp_chunk(m)

            S.add("dve", lambda e: e.tensor_scalar(out=mu4[:, 0:4], in0=ps[:, 6, 0:4], scalar1=1.0 / D, scalar2=None,
                                                   op0=ALU.mult),
                  reads=[("bank", 6)], writes=[("small", id(mu4))])
            S.add("dve", lambda e: e.tensor_tensor(out=mu4[:, 4:8], in0=mu4[:, 0:4], in1=mu4[:, 0:4], op=ALU.mult),
                  reads=[("small", id(mu4))], writes=[("small", id(mu4))])
            S.add("dve", lambda e: e.scalar_tensor_tensor(
                out=st4[:, 0:4], in0=ps[:, 6, 4:8], scalar=1.0 / D, in1=mu4[:, 4:8], op0=ALU.mult, op1=ALU.subtract),
                reads=[("bank", 6), ("small", id(mu4))], writes=[("small", id(st4))])
            S.add("pool", lambda e: e.tensor_scalar(out=st4[:, 4:8], in0=st4[:, 0:4], scalar1=1.0, scalar2=LN_EPS,
                                                    op0=ALU.mult, op1=ALU.add),
                  reads=[("small", id(st4))], writes=[("small", id(st4))])
            S.add("pool", lambda e: e.tensor_tensor(out=r4[:, 0:4], in0=st4[:, 4:8], in1=mhalf[:, 0:4], op=ALU.pow),
                  reads=[("small", id(st4)), "mhalf"], writes=[("small", id(r4))])
            S.add("dve", lambda e: e.scalar_tensor_tensor(
                out=r4[:, 4:8], in0=mu4[:, 0:4], scalar=-1.0, in1=r4[:, 0:4], op0=ALU.mult, op1=ALU.mult),
                reads=[("small", id(mu4)), ("small", id(r4))], writes=[("small", id(r4))])
            bcast(r4, 0, 0, 7)
            S.add("act", lambda e: e.activation(out=rstd[:], in_=ps[:, 7, :], func=AF.Copy),
                  reads=[("bank", 7)], writes=["rstd"])
            bx = next_bank()
            bcast(r4, 4, 1, bx)
            S.add("act", lambda e, bx=bx: e.activation(out=mu[:], in_=ps[:, bx, :], func=AF.Copy),
                  reads=[("bank", bx)], writes=["mu"])

            lg = vcol(l, V_LNG, 0)
            lb = vcol(l, V_LNB, 0)

            def ln_apply_chunk(c):
                k1 = nextB()
                S.add("dve", lambda e, c=c, k1=k1: e.tensor_tensor(out=tmpB[:, k1, :], in0=co32(c), in1=rstd[:],
                                                                   op=ALU.mult),
                      reads=R_co(c) + ["rstd"], writes=[("tmpB", k1)])
                k2 = nextB()
                S.add("dve", lambda e, k1=k1, k2=k2: e.tensor_tensor(out=tmpB[:, k2, :], in0=tmpB[:, k1, :],
                                                                     in1=mu[:], op=ALU.add),
                      reads=[("tmpB", k1), "mu"], writes=[("tmpB", k2)])
                S.add("act", lambda e, c=c, k2=k2: e.activation(
                    out=lnout[:, c, :], in_=tmpB[:, k2, :], func=AF.Silu,
                    scale=vecs[:, lg + c:lg + c + 1], bias=vecs[:, lb + c:lb + c + 1]),
                    reads=[("tmpB", k2), "vecs"], writes=[("lnout", c)])

            for m in range(3, NCH):
                gp_chunk(m)
                ln_apply_chunk(m - 3)
            for c in range(NCH - 3, NCH):
                ln_apply_chunk(c)
            pslot = UC.acquire(i, l, "pl", 0)
            for m in range(NCH):
                g = m // 2
                oc = m % 2
                b = next_bank()
                pairs = [(ring[:, pslot, ((g * 2 + oc) * 2 + kc) * 128:((g * 2 + oc) * 2 + kc + 1) * 128],
                          pooled[:, 2 * g + kc, :]) for kc in range(2)]
                mm_group(b, pairs, reads=[("slot", pslot), ("pooled", 2 * g), ("pooled", 2 * g + 1)])
                kb = nextB()
                S.add("act", lambda e, b=b, kb=kb, m=m: e.activation(
                    out=tmpB[:, kb, :], in_=ps[:, b, :], func=AF.Copy, scale=vecs[:, psc + m:psc + m + 1]),
                    reads=[("bank", b), "vecs"], writes=[("tmpB", kb)])
                S.add("dve", lambda e, kb=kb, m=m: e.scalar_tensor_tensor(
                    out=ma(m), in0=ma(m), scalar=1.0, in1=tmpB[:, kb, :], op0=ALU.add, op1=ALU.mult),
                    reads=R_ma(m) + [("tmpB", kb)], writes=R_ma(m))
            UC.release("pl", 0)

            for m in range(NCH):
                if m % 4 == 0:
                    gslot = UC.acquire(i, l, "gc", m // 4)
                    cslot = UC.acquire(i, l, "co", m // 4)
                bg = next_bank()
                base = (m % 4) * 1024
                pairs = [(ring[:, gslot, base + k * 128: base + (k + 1) * 128], xn[:, k, :]) for k in range(NCH)]
                mm_group(bg, pairs, reads=[("slot", gslot)] + [("xn", k) for k in range(NCH)])
                bc = next_bank()
                pairs = [(ring[:, cslot, base + k * 128: base + (k + 1) * 128], lnout[:, k, :]) for k in range(NCH)]
                mm_group(bc, pairs, reads=[("slot", cslot)] + [("lnout", k) for k in range(NCH)])
                if m % 4 == 3:
                    UC.release("gc", m // 4)
                    UC.release("co", m // 4)
                k1 = nextB()
                S.add("act", lambda e, bg=bg, k1=k1: e.activation(out=tmpB[:, k1, :], in_=ps[:, bg, :],
                                                                  func=AF.Tanh, scale=0.5),
                      reads=[("bank", bg)], writes=[("tmpB", k1)])
                k2 = nextB()
                S.add("dve", lambda e, bc=bc, k1=k1, k2=k2: e.scalar_tensor_tensor(
                    out=tmpB[:, k2, :], in0=tmpB[:, k1, :], scalar=1.0, in1=ps[:, bc, :], op0=ALU.add, op1=ALU.mult),
                    reads=[("tmpB", k1), ("bank", bc)], writes=[("tmpB", k2)])
                S.add("dve", lambda e, k2=k2, m=m: e.tensor_tensor(out=pooled[:, m, :], in0=tmpB[:, k2, :],
                                                                   in1=ma(m), op=ALU.add),
                      reads=[("tmpB", k2)] + R_ma(m), writes=[("pooled", m)])

            NA.start(vcol(l, V_FFN2, 0), True, "B")
            lag = Lag(2)
            for m in range(NCH):
                if m % 4 == 0:
                    wslot = UC.acquire(i, l, "wo", m // 4)
                b = next_bank()
                base = (m % 4) * 1024
                pairs = [(ring[:, wslot, base + k * 128: base + (k + 1) * 128], pooled[:, k, :]) for k in range(NCH)]
                mm_group(b, pairs, reads=[("slot", wslot)] + [("pooled", k) for k in range(NCH)])
                if m % 4 == 3:
                    UC.release("wo", m // 4)
                S.add("dve", lambda e, b=b, m=m: e.scalar_tensor_tensor(
                    out=h[:, m, :], in0=ps[:, b, :], scalar=0.5, in1=h[:, m, :], op0=ALU.mult, op1=ALU.add),
                    reads=[("bank", b), ("h", m)], writes=[("h", m)])
                lag.push(m)
            lag.flush()

        def load_p(i, l):
            k = (i * L + l) % 2

            def fn(e, i=i, l=l, k=k):
                return e.dma_start(out=pin[:, k, :, :],
                                   in_=p_d[l, i * T:(i + 1) * T, :].rearrange("(s p) f -> p s f", p=128)
                                   ).then_inc(pin_sem[k], 16)
            S.add("act", fn, writes=[("pin", k)], dma_sem=pin_sem[k])

        def ple_ptrans(i, l):
            k = (i * L + l) % 2
            for kc in range(2):
                b = next_bank()

                def fn(e, b=b, kc=kc, k=k):
                    inst = None
                    for s in range(4):
                        inst = e.transpose(ps[:, b, s * 128:(s + 1) * 128], pin[:, k, s, kc * 128:(kc + 1) * 128],
                                           ident[:])
                    return inst
                S.add("pe", fn, reads=[("pin", k), "ident"], writes=[("bank", b)], npe=4, tag="pT")
                S.add("act", lambda e, b=b, kc=kc: e.activation(out=pT[:, kc, :], in_=ps[:, b, :], func=AF.Copy),
                      reads=[("bank", b)], writes=[("pT", kc)])

        def ple(i, l, next_gbase, want_xg, nbuf="B"):
            S.cur_tag = f"t{i}l{l}ple"
            ple_ptrans(i, l)
            pst = {"pslot": None, "gslot": None}
            NA.start(next_gbase, want_xg, nbuf)
            lag = Lag(2)

            def ple_mm(m, early):
                if m == 0:
                    pst["pslot"] = UC.acquire(i, l, "pp", 0)
                if m % 4 == 0:
                    pst["gslot"] = UC.acquire(i, l, "pg", m // 4)
                pslot, gslot = pst["pslot"], pst["gslot"]
                src = lnout if early else xn
                rn = "lnout" if early else "xn"
                bg = next_bank()
                base = (m % 4) * 1024
                pairs = [(ring[:, gslot, base + kk * 128: base + (kk + 1) * 128], src[:, kk, :]) for kk in range(NCH)]
                mm_group(bg, pairs, reads=[("slot", gslot)] + [(rn, kk) for kk in range(NCH)])
                bp = next_bank()
                pairs = [(ring[:, pslot, (m * 2 + kc) * 128:(m * 2 + kc + 1) * 128], pT[:, kc, :]) for kc in range(2)]
                mm_group(bp, pairs, reads=[("slot", pslot), ("pT", 0), ("pT", 1)])
                if m % 4 == 3:
                    UC.release("pg", m // 4)
                if m == NCH - 1:
                    UC.release("pp", 0)
                return (bg, bp)

            pq = []

            def ple_B(m, k1, bp):
                ka = nextA()
                S.add("dve", lambda e, bp=bp, k1=k1, ka=ka: e.scalar_tensor_tensor(
                    out=tmpA[:, ka, 0:T], in0=tmpB[:, k1, :], scalar=1.0, in1=ps[:, bp, :], op0=ALU.add, op1=ALU.mult),
                    reads=[("tmpB", k1), ("bank", bp)], writes=[("tmpA", ka)])
                S.add("dve", lambda e, ka=ka, m=m: e.scalar_tensor_tensor(
                    out=h[:, m, :], in0=tmpA[:, ka, 0:T], scalar=0.5, in1=h[:, m, :], op0=ALU.mult, op1=ALU.add),
                    reads=[("tmpA", ka), ("h", m)], writes=[("h", m)])
                lag.push(m)

            def ple_evac(m, banks, early):
                bg, bp = banks
                k0 = nextB()
                k1 = nextB()
                S.add("dve", lambda e, bg=bg, k0=k0: e.tensor_tensor(out=tmpB[:, k0, :], in0=ps[:, bg, :],
                                                                     in1=rstd[:], op=ALU.mult),
                      reads=[("bank", bg), "rstd"], writes=[("tmpB", k0)])
                S.add("act", lambda e, k0=k0, k1=k1: e.activation(out=tmpB[:, k1, :], in_=tmpB[:, k0, :],
                                                                 func=AF.Tanh, scale=0.5),
                      reads=[("tmpB", k0)], writes=[("tmpB", k1)])
                if pq:
                    ple_B(*pq.pop(0))
                pq.append((m, k1, bp))

            run_boundary(NCH, ple_mm, ple_evac, evac_lag=1)
            while pq:
                ple_B(*pq.pop(0))
            lag.flush()

        def load_x(i):
            def fn(e, i=i):
                return e.dma_start(out=xin[:], in_=x_d[i * T:(i + 1) * T, :].rearrange("(s p) f -> p s f", p=128)
                                   ).then_inc(xin_sem, 16)
            S.add("act", fn, writes=["xin"], dma_sem=xin_sem)

        def x_to_h(i):
            NA.start(vcol(0, V_FFN1, 0), True, "B")
            lag = Lag(1)
            for c in range(NCH):
                b = next_bank()

                def fn(e, b=b, c=c):
                    inst = None
                    for s in range(4):
                        inst = e.transpose(ps[:, b, s * 128:(s + 1) * 128], xin[:, s, c * 128:(c + 1) * 128], ident[:])
                    return inst
                S.add("pe", fn, reads=["xin", "ident"], writes=[("bank", b)], tag="xT", npe=4)
                eng = "act" if c % 2 == 0 else "dve"
                if eng == "act":
                    S.add("act", lambda e, b=b, c=c: e.activation(out=h[:, c, :], in_=ps[:, b, :], func=AF.Copy),
                          reads=[("bank", b)], writes=[("h", c)])
                else:
                    S.add("dve", lambda e, b=b, c=c: e.tensor_copy(out=h[:, c, :], in_=ps[:, b, :]),
                          reads=[("bank", b)], writes=[("h", c)])
                lag.push(c)
            lag.flush()

        def store_out(i, raw):
            if not raw:
                S.add("act", lambda e: e.activation(out=st4[:, 0:4], in_=ps[:, 6, 0:4], func=AF.Identity,
                                                    scale=1.0 / D, bias=epsr[:, 0:1]),
                      reads=[("bank", 6), "eps"], writes=[("small", id(st4))])
                S.add("pool", lambda e: e.tensor_tensor(out=r4[:, 0:4], in0=st4[:, 0:4], in1=mhalf[:, 0:4], op=ALU.pow),
                      reads=[("small", id(st4)), "mhalf"], writes=[("small", id(r4))])
            else:
                for c in range(NCH):
                    S.add("dve", lambda e, c=c: e.tensor_copy(out=co32(c), in_=h[:, c, :]),
                          reads=[("h", c)], writes=R_co(c))
            for s in range(4):
                for half in range(2):
                    b = next_bank()

                    def fn(e, b=b, s=s, half=half):
                        inst = None
                        for cc in range(4):
                            c = half * 4 + cc
                            inst = e.transpose(ps[:, b, cc * 128:(cc + 1) * 128],
                                               hid32[:, c * T + s * 128: c * T + (s + 1) * 128], ident[:])
                        return inst
                    S.add("pe", fn, reads=[r for c in range(half * 4, half * 4 + 4) for r in R_co(c)] + ["ident"],
                          writes=[("bank", b)], tag="outT", npe=4)
                    if raw:
                        if half == 0:
                            S.add("act", lambda e, b=b, s=s: e.activation(out=Y[:, s, 0:512], in_=ps[:, b, :],
                                                                          func=AF.Copy),
                                  reads=[("bank", b)], writes=[("Y", s)])
                        else:
                            S.add("dve", lambda e, b=b, s=s: e.tensor_copy(out=Y[:, s, 512:1024], in_=ps[:, b, :]),
                                  reads=[("bank", b)], writes=[("Y", s)])
                    elif half == 0:
                        S.add("act", lambda e, b=b, s=s: e.activation(out=Y[:, s, 0:512], in_=ps[:, b, :],
                                                                      func=AF.Copy, scale=r4[:, s:s + 1]),
                              reads=[("bank", b), ("small", id(r4))], writes=[("Y", s)])
                    else:
                        S.add("dve", lambda e, b=b, s=s: e.tensor_scalar(out=Y[:, s, 512:1024], in0=ps[:, b, :],
                                                                         scalar1=r4[:, s:s + 1], scalar2=None,
                                                                         op0=ALU.mult),
                              reads=[("bank", b), ("small", id(r4))], writes=[("Y", s)])

            def fn(e, i=i):
                return e.dma_start(out=out_d[i * T:(i + 1) * T, :].rearrange("(s p) f -> p s f", p=128),
                                   in_=Y[:]).then_inc(out_sem, 16)
            S.add("act", fn, reads=[("Y", s) for s in range(4)], dma_sem=out_sem)

        stages_all = ["ffn1", "mix", "ffn2", "ple"]
        load_x(0)
        load_p(0, 0)
        for i in range(NT):
            x_to_h(i)
            if i + 1 < NT:
                load_x(i + 1)
            stopped = False
            for l in range(L):
                ffn(i, l, 1, vcol(l, V_FFN1, 0), vcol(l, V_MIX, 0))
                if stop_after == (l, "ffn1"):
                    stopped = True
                    break
                mixer(i, l)
                if stop_after == (l, "mix"):
                    stopped = True
                    break
                ffn(i, l, 2, vcol(l, V_FFN2, 0), vcol(l, V_PLE, 0))
                if stop_after == (l, "ffn2"):
                    stopped = True
                    break
                if l + 1 < L:
                    ple(i, l, vcol(l + 1, V_FFN1, 0), True)
                else:
                    ple(i, l, L * 64, True, "F")
                if l + 1 < L:
                    load_p(i, l + 1)
                elif i + 1 < NT:
                    load_p(i + 1, 0)
                if stop_after == (l, "ple"):
                    stopped = True
                    break
            if stopped:
                while ws["pos"] < (i + 1) * L * NU:
                    _, sl, su = stream[ws["pos"]]
                    UC.acquire(i, sl, UNITS[su][0], UNITS[su][1])
                    UC.release(UNITS[su][0], UNITS[su][1])
                store_out(i, raw=True)
            else:
                store_out(i, raw=False)
        S.add("act", lambda e: None, writes=[("Y", s) for s in range(4)])

        S.assign_tokens(eng_sems)
        nc._pe_log = S.pe_log
        with nc.Block() as block:
            @block.tensor
            def _(e):
                S.emit_engine("pe", e, eng_sems)

            @block.scalar
            def _(e):
                S.emit_engine("act", e, eng_sems)

            @block.vector
            def _(e):
                S.emit_engine("dve", e, eng_sems)

            @block.gpsimd
            def _(e):
                S.emit_engine("pool", e, eng_sems)

            @block.sync
            def _(e):
                S.emit_engine("sp", e, eng_sems)
    return nc


def _run(inputs, NT, stop_after=None, trace=False):
    inp = {k: np.asarray(v) for k, v in inputs.items()}
    S_LOC = NT * T
    wsrc = host_arrange_weights(inp)
    vecs, wtap = host_arrange_vecs(inp)
    ident, invc = host_consts()
    nc = build_program(NT, stop_after=stop_after)
    in_maps = []
    for c in range(NCORES):
        in_maps.append({
            "x": np.ascontiguousarray(inp["x"][c, :S_LOC, :]),
            "p": np.ascontiguousarray(inp["p"][:, c, :S_LOC, :]),
            "wsrc": wsrc, "vecs": vecs, "wtap": wtap, "ident": ident, "invc": invc,
        })
    res = run_bass_kernel_spmd(nc, in_maps, core_ids=list(range(NCORES)), **({"trace": True} if trace else {}))
    out = np.stack([res.results[c]["out"] for c in range(NCORES)], axis=0)
    return out.astype(np.float32, copy=False), res


def kernel(**inputs):
    out, _ = _run(inputs, SEQ // T)
    return out
```
